# Optimizing a Trainium2 kernel written in Bass

```python
import math
import jax
import jax.numpy as jnp
from jax import lax
import numpy as np

D_MODEL = 1024
BATCH = 16
SEQ = 2048
DEPTH = 1
DEC_BATCH = 128
DEC_SEQ = 4
PAST_LEN = 8192
PAGE_SIZE = 128

HEAD_DIM = 64
N_HEADS_NSA = 8
N_KV_NSA = 2
GROUP = N_HEADS_NSA // N_KV_NSA
N_HEADS_RWKV = 8
NSA_WIDTH = N_HEADS_NSA * HEAD_DIM
RWKV_WIDTH = N_HEADS_RWKV * HEAD_DIM
MIX_WIDTH = NSA_WIDTH + RWKV_WIDTH
KV_WIDTH = N_KV_NSA * HEAD_DIM
CMP_BLOCK = 64
CMP_HIDDEN = 64
SLC_BLOCK = 64
TOP_K_BLOCKS = 16
WINDOW = 512
BAND_BLOCK = 128
SLC_QUERY_CHUNK = 32
N_BUCKETS = 32
MAX_DISTANCE = 128
DECAY_LORA = 64
AAA_LORA = 64
GATE_LORA = 128
D_FF = 2816
NORM_EPS = 1e-6
GN_EPS = 64e-5
ATTN_SCALE = HEAD_DIM ** -0.5
FORCE_SCORE = 1e4
NEG_INF = -1e30
NSA_SPLITS = (NSA_WIDTH,) + (KV_WIDTH,) * 6 + (3 * N_HEADS_NSA,)
RWKV_SPLITS = (RWKV_WIDTH,) * 3 + (DECAY_LORA, AAA_LORA, GATE_LORA)
NSA_COLS = sum(NSA_SPLITS)
RWKV_COLS = sum(RWKV_SPLITS)
IN_COLS = NSA_COLS + RWKV_COLS

kernel_name = 'hymba_nsa_rwkv7_macaron_step'


def split_cols(x, sizes):
    cuts = [int(c) for c in np.cumsum(sizes)[:-1]]
    return jnp.split(x, cuts, axis=-1)


def rms_norm(x, g):
    xf = x.astype(jnp.float32)
    y = xf * lax.rsqrt(jnp.mean(xf * xf, axis=-1, keepdims=True) + NORM_EPS)
    return (y * g.astype(jnp.float32)).astype(x.dtype)


def ffn_half(x, g, wg, wu, wd):
    h = rms_norm(x, g)
    return x + 0.5 * ((jax.nn.silu(h @ wg) * (h @ wu)) @ wd)


def t5_bucket(dist):
    n = jnp.maximum(dist, 0)
    max_exact = N_BUCKETS // 2
    nf = jnp.maximum(n, 1).astype(jnp.float32)
    large = max_exact + (jnp.log(nf / max_exact) / math.log(MAX_DISTANCE / max_exact)
                         * (N_BUCKETS - max_exact)).astype(jnp.int32)
    return jnp.where(n < max_exact, n, jnp.minimum(large, N_BUCKETS - 1))


def rel_bias(dist, table):
    b = table.astype(jnp.float32)[t5_bucket(dist)]
    return jnp.moveaxis(b.reshape(dist.shape + (N_KV_NSA, GROUP)), (-2, -1), (0, 1))


def masked_softmax(logits, mask, axes):
    logits = jnp.where(mask, logits, NEG_INF)
    m = jnp.max(logits, axis=axes, keepdims=True)
    e = jnp.where(mask, jnp.exp(logits - m), 0.0)
    return e / jnp.maximum(jnp.sum(e, axis=axes, keepdims=True), 1e-30)


def attend(q, k, v, dist, mask, table):
    logits = jnp.einsum('ntgrd,nlgd->ngrtl', q, k).astype(jnp.float32) * ATTN_SCALE + rel_bias(dist, table)
    p = masked_softmax(logits, mask, (-1,))
    return jnp.einsum('ngrtl,nlgd->ntgrd', p.astype(v.dtype), v), p


def compress(rows, pe, w1, w2):
    n, length = rows.shape[:2]
    blk = rows.reshape(n, length // CMP_BLOCK, CMP_BLOCK, N_KV_NSA, HEAD_DIM) + pe[:, None, :]
    hid = jnp.einsum('nbcgd,cde->nbge', blk, w1)
    return jax.nn.gelu(hid) @ w2


def cmp_branch(q, q_pos, kcc, vcc, table):
    blk_end = (jnp.arange(kcc.shape[1], dtype=jnp.int32) + 1) * CMP_BLOCK - 1
    dist = q_pos[:, None] - blk_end[None, :]
    return attend(q, kcc, vcc, dist, dist >= 0, table)


def select_blocks(importance, q_pos, n_blocks):
    n_done = importance.shape[-1]
    imp = jnp.pad(importance, ((0, 0), (0, 0), (0, 0), (0, n_blocks - n_done)))
    blk = jnp.arange(n_blocks, dtype=jnp.int32)
    cur = (q_pos // SLC_BLOCK)[:, None]
    forced = (blk == 0) | (blk == cur) | (blk == cur - 1)
    score = jnp.where(blk <= cur, jnp.where(forced, FORCE_SCORE, imp), -FORCE_SCORE)
    _, idx = lax.top_k(score, min(TOP_K_BLOCKS, n_blocks))
    return idx, idx <= cur


def slc_attend(q, q_pos, k_sel, v_sel, idx, valid, table):
    key_pos = idx[..., None] * SLC_BLOCK + jnp.arange(SLC_BLOCK, dtype=jnp.int32)
    dist = q_pos[:, None, None] - key_pos
    mask = valid[..., None] & (dist >= 0)
    tb = table.astype(jnp.float32).reshape(N_BUCKETS, N_KV_NSA, GROUP)
    g_ix = jnp.arange(N_KV_NSA)[None, :, None, None, None]
    bias = jnp.moveaxis(tb[t5_bucket(dist), g_ix], -1, 2)
    logits = jnp.einsum('ntgrd,ngtkld->ngrtkl', q, k_sel).astype(jnp.float32) * ATTN_SCALE + bias
    p = masked_softmax(logits, mask[:, :, None], (-2, -1))
    return jnp.einsum('ngrtkl,ngtkld->ntgrd', p.astype(v_sel.dtype), v_sel)


def select_prompt(q, ks, vs, idx, valid, pos, table):
    n, s = q.shape[:2]
    nsb = s // SLC_BLOCK
    nc = s // SLC_QUERY_CHUNK
    kb = ks.reshape(n, nsb, SLC_BLOCK, N_KV_NSA, HEAD_DIM)
    vb = vs.reshape(n, nsb, SLC_BLOCK, N_KV_NSA, HEAD_DIM)
    n_ix = jnp.arange(n)[:, None, None, None]
    g_ix = jnp.arange(N_KV_NSA)[None, :, None, None]

    def chunk(args):
        qc, ic, vc_, pc = args
        k_sel = kb[n_ix, ic, :, g_ix]
        v_sel = vb[n_ix, ic, :, g_ix]
        return slc_attend(qc, pc, k_sel, v_sel, ic, vc_, table)

    kk = idx.shape[-1]
    xs = (jnp.moveaxis(q.reshape(n, nc, SLC_QUERY_CHUNK, N_KV_NSA, GROUP, HEAD_DIM), 1, 0),
          jnp.moveaxis(idx.reshape(n, N_KV_NSA, nc, SLC_QUERY_CHUNK, kk), 2, 0),
          jnp.moveaxis(valid.reshape(n, N_KV_NSA, nc, SLC_QUERY_CHUNK, kk), 2, 0),
          pos.reshape(nc, SLC_QUERY_CHUNK))
    o = lax.map(chunk, xs)
    return jnp.moveaxis(o, 0, 1).reshape(n, s, N_KV_NSA, GROUP, HEAD_DIM)


def window_prompt(q, kw, vw, table):
    n, s = q.shape[:2]
    nq = s // BAND_BLOCK
    nprev = WINDOW // BAND_BLOCK
    kb_len = (nprev + 1) * BAND_BLOCK

    def band(u):
        up = jnp.pad(u, ((0, 0), (WINDOW, 0), (0, 0), (0, 0))).reshape(n, nprev + nq, BAND_BLOCK, N_KV_NSA, HEAD_DIM)
        ub = jnp.stack([up[:, j:j + nq] for j in range(nprev + 1)], axis=2)
        return jnp.moveaxis(ub.reshape(n, nq, kb_len, N_KV_NSA, HEAD_DIM), 1, 0)

    qb = jnp.moveaxis(q.reshape(n, nq, BAND_BLOCK, N_KV_NSA, GROUP, HEAD_DIM), 1, 0)
    q_pos = jnp.arange(s, dtype=jnp.int32).reshape(nq, BAND_BLOCK)
    k_pos = (jnp.arange(nq, dtype=jnp.int32) * BAND_BLOCK - WINDOW)[:, None] + jnp.arange(kb_len, dtype=jnp.int32)

    def block(args):
        qc, kc, vc, qp, kp = args
        dist = qp[:, None] - kp[None, :]
        mask = (dist >= 0) & (dist <= WINDOW) & (kp >= 0)[None, :]
        return attend(qc, kc, vc, dist, mask, table)[0]

    o = lax.map(block, (qb, band(kw), band(vw), q_pos, k_pos))
    return jnp.moveaxis(o, 0, 1).reshape(n, s, N_KV_NSA, GROUP, HEAD_DIM)


def gather_sample_blocks(pool, new_rows, idx, page_table):
    n, t = new_rows.shape[:2]
    bpp = PAGE_SIZE // SLC_BLOCK
    n_past = PAST_LEN // SLC_BLOCK
    n_new = -(-t // SLC_BLOCK)
    s_ix = jnp.arange(n)[:, None, None, None]
    g_ix = jnp.arange(N_KV_NSA)[None, :, None, None]
    pb = jnp.minimum(idx, n_past - 1)
    phys = page_table[s_ix, pb // bpp]
    past = pool.reshape(pool.shape[0], bpp, SLC_BLOCK, N_KV_NSA, HEAD_DIM)[phys, pb % bpp, :, g_ix]
    new = jnp.pad(new_rows, ((0, 0), (0, n_new * SLC_BLOCK - t), (0, 0), (0, 0)))
    new = new.reshape(n, n_new, SLC_BLOCK, N_KV_NSA, HEAD_DIM)[s_ix, jnp.clip(idx - n_past, 0, n_new - 1), :, g_ix]
    return jnp.where((idx < n_past)[..., None, None], past, new)


def nsa_heads(z):
    n, t = z.shape[:2]
    q, kc, vc, ks, vs, kw, vw, gate = split_cols(z, NSA_SPLITS)
    q = q.reshape(n, t, N_KV_NSA, GROUP, HEAD_DIM)
    kv = tuple(u.reshape(n, t, N_KV_NSA, HEAD_DIM) for u in (kc, vc, ks, vs, kw, vw))
    gate = jax.nn.sigmoid(gate.reshape(n, t, 3, N_KV_NSA, GROUP))[..., None]
    return q, kv, gate


def nsa_combine(gate, o_cmp, o_slc, o_win):
    o = gate[:, :, 0] * o_cmp + gate[:, :, 1] * o_slc + gate[:, :, 2] * o_win
    return o.reshape(o.shape[0], o.shape[1], NSA_WIDTH)


def nsa_prompt(q, kv, gate, cmp_w, table):
    kc, vc, ks, vs, kw, vw = kv
    pe_k, w1_k, w2_k, pe_v, w1_v, w2_v = cmp_w
    s = q.shape[1]
    pos = jnp.arange(s, dtype=jnp.int32)
    kcc = compress(kc, pe_k, w1_k, w2_k)
    vcc = compress(vc, pe_v, w1_v, w2_v)
    o_cmp, p_cmp = cmp_branch(q, pos, kcc, vcc, table)
    idx, valid = select_blocks(p_cmp.sum(axis=2), pos, s // SLC_BLOCK)
    o_slc = select_prompt(q, ks, vs, idx, valid, pos, table)
    o_win = window_prompt(q, kw, vw, table)
    wb = min(WINDOW, s)
    return nsa_combine(gate, o_cmp, o_slc, o_win), kw[:, s - wb:], vw[:, s - wb:]


def nsa_sample(q, kv, gate, cache_cmp_k, cache_cmp_v, cache_slc_k, cache_slc_v,
               cache_win_k, cache_win_v, page_table, cmp_w, table):
    kc, vc, ks, vs, kw, vw = kv
    pe_k, w1_k, w2_k, pe_v, w1_v, w2_v = cmp_w
    n, t = q.shape[:2]
    pos = PAST_LEN + jnp.arange(t, dtype=jnp.int32)
    n_new_done = (t // CMP_BLOCK) * CMP_BLOCK

    def cmp_rows(pool, new, pe, w1, w2):
        past = pool[page_table].reshape(n, PAST_LEN, N_KV_NSA, HEAD_DIM)
        return jnp.concatenate([compress(past, pe, w1, w2), compress(new[:, :n_new_done], pe, w1, w2)], axis=1)

    kcc = cmp_rows(cache_cmp_k, kc, pe_k, w1_k, w2_k)
    vcc = cmp_rows(cache_cmp_v, vc, pe_v, w1_v, w2_v)
    o_cmp, p_cmp = cmp_branch(q, pos, kcc, vcc, table)
    idx, valid = select_blocks(p_cmp.sum(axis=2), pos, -(-(PAST_LEN + t) // SLC_BLOCK))
    k_sel = gather_sample_blocks(cache_slc_k, ks, idx, page_table)
    v_sel = gather_sample_blocks(cache_slc_v, vs, idx, page_table)
    o_slc = slc_attend(q, pos, k_sel, v_sel, idx, valid, table)
    wb = cache_win_k.shape[1]
    kw_all = jnp.concatenate([cache_win_k, kw], axis=1)
    vw_all = jnp.concatenate([cache_win_v, vw], axis=1)
    k_pos = PAST_LEN - wb + jnp.arange(wb + t, dtype=jnp.int32)
    dist = pos[:, None] - k_pos[None, :]
    o_win, _ = attend(q, kw_all, vw_all, dist, (dist >= 0) & (dist <= WINDOW), table)
    return nsa_combine(gate, o_cmp, o_slc, o_win), kw_all[:, -wb:], vw_all[:, -wb:]


def rwkv_mix(p, p_prev, wkv0, mu, w0, w2, a0, a2, g2, k_k, k_a, r_k, ln_w, ln_b):
    n, t = p.shape[:2]
    prev = jnp.concatenate([p_prev[:, None, :].astype(p.dtype), p[:, :-1]], axis=1)
    xs = p + (prev - p) * mu
    r, k, v, xw, xa, xg = split_cols(xs, RWKV_SPLITS)
    w = -jax.nn.softplus(-(w0 + jnp.tanh(xw) @ w2)) - 0.5
    a = jax.nn.sigmoid(a0 + xa @ a2)
    g = jax.nn.sigmoid(xg) @ g2
    heads = lambda u: u.astype(jnp.float32).reshape(n, t, N_HEADS_RWKV, HEAD_DIM)
    kk = heads(k * k_k)
    kk = kk / jnp.maximum(jnp.linalg.norm(kk, axis=-1, keepdims=True), 1e-12)
    k = k * (1 + (a - 1) * k_a)
    r, k, v, a = heads(r), heads(k), heads(v), heads(a)
    decay = jnp.exp(-jnp.exp(heads(w)))

    def step(S, inp):
        r_t, k_t, v_t, kk_t, a_t, d_t = inp
        s_kk = jnp.einsum('nhvk,nhk->nhv', S, -kk_t)
        S = S * d_t[:, :, None, :] + s_kk[..., None] * (kk_t * a_t)[:, :, None, :] + v_t[..., None] * k_t[:, :, None, :]
        return S, jnp.einsum('nhvk,nhk->nhv', S, r_t)

    seq = tuple(jnp.moveaxis(u, 1, 0) for u in (r, k, v, kk, a, decay))
    S_last, y = lax.scan(step, wkv0.astype(jnp.float32), seq)
    y = jnp.moveaxis(y, 0, 1)
    mean = jnp.mean(y, axis=-1, keepdims=True)
    var = jnp.mean(jnp.square(y - mean), axis=-1, keepdims=True)
    y = ((y - mean) * lax.rsqrt(var + GN_EPS)).reshape(n, t, RWKV_WIDTH) * ln_w + ln_b
    bonus = (jnp.sum(r * k * r_k, axis=-1, keepdims=True) * v).reshape(n, t, RWKV_WIDTH)
    out = ((y + bonus) * g.astype(jnp.float32)).astype(p.dtype)
    return out, S_last, p[:, -1]


def setup_inputs(seed: int = 0) -> dict:
    key = jax.random.key(seed)
    keys = iter(jax.random.split(key, 48))

    def nrm(shape, scale=1.0):
        return jax.random.normal(next(keys), shape, jnp.float32) * scale

    def unif(shape, lo, hi):
        return jax.random.uniform(next(keys), shape, jnp.float32, lo, hi)

    L = DEPTH
    n_pages = PAST_LEN // PAGE_SIZE
    n_phys = (DEC_BATCH * n_pages * 5 + 3) // 4
    win_buf = min(WINDOW, PAST_LEN)
    pool = (L, n_phys, PAGE_SIZE, N_KV_NSA, HEAD_DIM)
    x_prompt = nrm((BATCH, SEQ, D_MODEL))
    x_sample = nrm((DEC_BATCH, DEC_SEQ, D_MODEL))
    cache_cmp_k = nrm(pool)
    cache_cmp_v = nrm(pool)
    cache_slc_k = nrm(pool)
    cache_slc_v = nrm(pool)
    cache_win_k = nrm((L, DEC_BATCH, win_buf, N_KV_NSA, HEAD_DIM))
    cache_win_v = nrm((L, DEC_BATCH, win_buf, N_KV_NSA, HEAD_DIM))
    state_wkv = nrm((L, DEC_BATCH, N_HEADS_RWKV, HEAD_DIM, HEAD_DIM), 0.5)
    state_shift = nrm((L, DEC_BATCH, RWKV_COLS))
    page_table = jax.random.permutation(next(keys), n_phys)[: DEC_BATCH * n_pages].reshape(DEC_BATCH, n_pages).astype(jnp.int32)
    return {
        'x_prompt': x_prompt,
        'x_sample': x_sample,
        'cache_cmp_k': cache_cmp_k,
        'cache_cmp_v': cache_cmp_v,
        'cache_slc_k': cache_slc_k,
        'cache_slc_v': cache_slc_v,
        'cache_win_k': cache_win_k,
        'cache_win_v': cache_win_v,
        'state_wkv': state_wkv,
        'state_shift': state_shift,
        'page_table': page_table,
        'rel_bias_table': nrm((N_BUCKETS, N_HEADS_NSA), 0.5),
        'ffn1_norm': 1.0 + nrm((L, D_MODEL), 0.02),
        'ffn1_wg': nrm((L, D_MODEL, D_FF), D_MODEL ** -0.5),
        'ffn1_wu': nrm((L, D_MODEL, D_FF), D_MODEL ** -0.5),
        'ffn1_wd': nrm((L, D_FF, D_MODEL), D_FF ** -0.5),
        'mix_norm': 1.0 + nrm((L, D_MODEL), 0.02),
        'w_in': nrm((L, D_MODEL, IN_COLS), D_MODEL ** -0.5),
        'cmp_pe_k': nrm((L, CMP_BLOCK, HEAD_DIM), 0.1),
        'cmp_w1_k': nrm((L, CMP_BLOCK, HEAD_DIM, CMP_HIDDEN), (CMP_BLOCK * HEAD_DIM) ** -0.5),
        'cmp_w2_k': nrm((L, CMP_HIDDEN, HEAD_DIM), CMP_HIDDEN ** -0.5),
        'cmp_pe_v': nrm((L, CMP_BLOCK, HEAD_DIM), 0.1),
        'cmp_w1_v': nrm((L, CMP_BLOCK, HEAD_DIM, CMP_HIDDEN), (CMP_BLOCK * HEAD_DIM) ** -0.5),
        'cmp_w2_v': nrm((L, CMP_HIDDEN, HEAD_DIM), CMP_HIDDEN ** -0.5),
        'shift_mu': unif((L, RWKV_COLS), 0.0, 1.0),
        'decay_w0': unif((L, RWKV_WIDTH), -6.0, 0.0),
        'decay_w2': nrm((L, DECAY_LORA, RWKV_WIDTH), DECAY_LORA ** -0.5),
        'aaa_a0': nrm((L, RWKV_WIDTH), 0.1),
        'aaa_a2': nrm((L, AAA_LORA, RWKV_WIDTH), AAA_LORA ** -0.5),
        'gate_g2': nrm((L, GATE_LORA, RWKV_WIDTH), GATE_LORA ** -0.5),
        'k_k': 0.85 + nrm((L, RWKV_WIDTH), 0.05),
        'k_a': 1.0 + nrm((L, RWKV_WIDTH), 0.05),
        'r_k': nrm((L, N_HEADS_RWKV, HEAD_DIM), 0.1),
        'ln_x_w': 1.0 + nrm((L, RWKV_WIDTH), 0.02),
        'ln_x_b': nrm((L, RWKV_WIDTH), 0.02),
        'w_out': nrm((L, MIX_WIDTH, D_MODEL), MIX_WIDTH ** -0.5),
        'ffn2_norm': 1.0 + nrm((L, D_MODEL), 0.02),
        'ffn2_wg': nrm((L, D_MODEL, D_FF), D_MODEL ** -0.5),
        'ffn2_wu': nrm((L, D_MODEL, D_FF), D_MODEL ** -0.5),
        'ffn2_wd': nrm((L, D_FF, D_MODEL), D_FF ** -0.5),
        'final_norm': 1.0 + nrm((D_MODEL,), 0.02),
    }


def reference(x_prompt, x_sample, cache_cmp_k, cache_cmp_v, cache_slc_k, cache_slc_v,
              cache_win_k, cache_win_v, state_wkv, state_shift, page_table, rel_bias_table,
              ffn1_norm, ffn1_wg, ffn1_wu, ffn1_wd, mix_norm, w_in,
              cmp_pe_k, cmp_w1_k, cmp_w2_k, cmp_pe_v, cmp_w1_v, cmp_w2_v,
              shift_mu, decay_w0, decay_w2, aaa_a0, aaa_a2, gate_g2, k_k, k_a, r_k, ln_x_w, ln_x_b,
              w_out, ffn2_norm, ffn2_wg, ffn2_wu, ffn2_wd, final_norm):
    xp, xs = x_prompt, x_sample
    bp = xp.shape[0]
    layer_states = []
    for l in range(DEPTH):
        cmp_w = (cmp_pe_k[l], cmp_w1_k[l], cmp_w2_k[l], cmp_pe_v[l], cmp_w1_v[l], cmp_w2_v[l])
        rw = (shift_mu[l], decay_w0[l], decay_w2[l], aaa_a0[l], aaa_a2[l], gate_g2[l],
              k_k[l], k_a[l], r_k[l], ln_x_w[l], ln_x_b[l])
        xp = ffn_half(xp, ffn1_norm[l], ffn1_wg[l], ffn1_wu[l], ffn1_wd[l])
        xs = ffn_half(xs, ffn1_norm[l], ffn1_wg[l], ffn1_wu[l], ffn1_wd[l])
        zp = rms_norm(xp, mix_norm[l]) @ w_in[l]
        zs = rms_norm(xs, mix_norm[l]) @ w_in[l]
        qp, kvp, gp = nsa_heads(zp[..., :NSA_COLS])
        o_nsa_p, wk_p, wv_p = nsa_prompt(qp, kvp, gp, cmp_w, rel_bias_table)
        o_rw_p, wkv_p, sh_p = rwkv_mix(zp[..., NSA_COLS:], jnp.zeros((bp, RWKV_COLS), zp.dtype),
                                       jnp.zeros((bp, N_HEADS_RWKV, HEAD_DIM, HEAD_DIM), jnp.float32), *rw)
        qs, kvs, gs = nsa_heads(zs[..., :NSA_COLS])
        o_nsa_s, wk_s, wv_s = nsa_sample(qs, kvs, gs, cache_cmp_k[l], cache_cmp_v[l], cache_slc_k[l], cache_slc_v[l],
                                         cache_win_k[l], cache_win_v[l], page_table, cmp_w, rel_bias_table)
        o_rw_s, wkv_s, sh_s = rwkv_mix(zs[..., NSA_COLS:], state_shift[l], state_wkv[l], *rw)
        xp = xp + jnp.concatenate([o_nsa_p, o_rw_p], axis=-1) @ w_out[l]
        xs = xs + jnp.concatenate([o_nsa_s, o_rw_s], axis=-1) @ w_out[l]
        xp = ffn_half(xp, ffn2_norm[l], ffn2_wg[l], ffn2_wu[l], ffn2_wd[l])
        xs = ffn_half(xs, ffn2_norm[l], ffn2_wg[l], ffn2_wu[l], ffn2_wd[l])
        layer_states.append((kvp[0], kvp[1], kvp[2], kvp[3], wk_p, wv_p, wkv_p, sh_p,
                             kvs[0], kvs[1], kvs[2], kvs[3], wk_s, wv_s, wkv_s, sh_s))
    y_prompt = rms_norm(xp, final_norm)
    y_sample = rms_norm(xs, final_norm)
    (p_cmp_k, p_cmp_v, p_slc_k, p_slc_v, p_win_k, p_win_v, p_wkv, p_shift,
     s_cmp_k, s_cmp_v, s_slc_k, s_slc_v, s_win_k, s_win_v, s_wkv, s_shift) = [jnp.stack(z) for z in zip(*layer_states)]
    return (y_prompt, y_sample, p_cmp_k, p_cmp_v, p_slc_k, p_slc_v, p_win_k, p_win_v, p_wkv, p_shift,
            s_cmp_k, s_cmp_v, s_slc_k, s_slc_v, s_win_k, s_win_v, s_wkv, s_shift)
```

```python
import contextlib
import numpy as np
import concourse.bass as bass
import concourse.mybir as mybir
from concourse.bass_utils import run_bass_kernel_spmd

F32 = mybir.dt.float32
BF16 = mybir.dt.bfloat16
I32 = mybir.dt.int32
AF = mybir.ActivationFunctionType
ALU = mybir.AluOpType
AX = mybir.AxisListType

NCORES = 8
D = 1024
DFF = 2816
NFC = DFF // 128
INC = 3096
NSA_COLS = 1304
RWC = 1792
PB = 2
S = 2048
SBT = 16
DS = 4
NS_TOK = SBT * DS
WIN = 512
EPS = 1e-6
SAME_ENG_SYNC = True
DEBUG = False
NSA_PARTS = 4
NSA_SAMPLE = 1
S4_QT = 4
S4_H = 8
S4_BR = 3
S4_NORM = 1
S4_STEP = 9
S4_SUB = 3
S4_NOBIAS = 1
S_BARRIERS = 0
S_SKIP23 = 0
PV_PLAIN = 0
VAUG_ENG = 0
MAX_WAITS = 2
TB_SHIFT = 1


class Buf:
    __slots__ = ("name", "w", "r")

    def __init__(self, name):
        self.name = name
        self.w = None
        self.r = {}


class EngW:
    def __init__(self, e, sem, is_pe=False):
        self.e = e
        self.sem = sem
        self.cnt = 0
        self.seen = {}
        self.is_pe = is_pe
        self.nwait = 0

    def wait(self, ev):
        if ev is None:
            return
        sem, val = ev
        if sem is self.sem and (self.is_pe or not SAME_ENG_SYNC):
            return
        k = id(sem)
        if self.seen.get(k, 0) >= val:
            return
        if MAX_WAITS and self.nwait >= MAX_WAITS:
            self.e.nop(nofuse=True)
            self.nwait = 0
        self.e.wait_ge(sem, val)
        self.nwait += 1
        self.seen[k] = val


class K:
    def __init__(self, nc, es, nring=40):
        self.nc = nc
        self.es = es
        mk = lambda n: es.enter_context(nc.semaphore(n))
        self.eng = {
            "pe": EngW(nc.tensor, mk("s_pe"), True),
            "act": EngW(nc.scalar, mk("s_act")),
            "dve": EngW(nc.vector, mk("s_dve")),
            "pool": EngW(nc.gpsimd, mk("s_pool")),
            "sp": EngW(nc.sync, mk("s_sp")),
        }
        self.ring = [[mk(f"s_dma{i}"), 0] for i in range(nring)]
        self.ring_i = 0
        self.sw_ring = [[mk(f"s_swdma{i}"), 0] for i in range(8)]
        self.sw_ring_i = 0
        self.nbuf = 0

    def buf(self, name=None):
        self.nbuf += 1
        return Buf(name or f"b{self.nbuf}")

    def uniq(self, name):
        self.nbuf += 1
        return f"{name}_u{self.nbuf}"

    def sb(self, name, shape, dt):
        return self.es.enter_context(self.nc.sbuf_tensor(self.uniq(name), shape, dt))

    def _pre(self, E, reads, writes):
        for b in reads:
            E.wait(b.w)
        for b in writes:
            E.wait(b.w)
            for ev in b.r.values():
                E.wait(ev)

    def _post(self, ev, reads, writes):
        for b in reads:
            b.r[id(ev[0])] = ev
        for b in writes:
            b.w = ev
            b.r = {}

    def op(self, eng, fn, reads=(), writes=()):
        E = self.eng[eng]
        self._pre(E, reads, writes)
        ins = fn(E.e)
        E.nwait = 0
        E.cnt += 1
        ins.then_inc(E.sem, 1)
        ev = (E.sem, E.cnt)
        self._post(ev, reads, writes)
        return ev

    def dma(self, q, out, in_, reads=(), writes=(), indirect=None, **kw):
        E = self.eng[q]
        self._pre(E, reads, writes)
        if q == "pool":
            slot = self.sw_ring[self.sw_ring_i]
            self.sw_ring_i = (self.sw_ring_i + 1) % len(self.sw_ring)
        else:
            slot = self.ring[self.ring_i]
            self.ring_i = (self.ring_i + 1) % len(self.ring)
        E.wait((slot[0], slot[1]) if slot[1] else None)
        if indirect is not None:
            ins = E.e.indirect_dma_start(out=out, out_offset=None, in_=in_, in_offset=indirect, **kw)
        else:
            ins = E.e.dma_start(out=out, in_=in_, **kw)
        slot[1] += 16
        ins.then_inc(slot[0], 16)
        ev = (slot[0], slot[1])
        self._post(ev, reads, writes)
        return ev

    def barrier(self):
        evs = [(E.sem, E.cnt) for E in self.eng.values() if E.cnt] + [(s[0], s[1]) for s in self.ring + self.sw_ring if s[1]]
        for E in self.eng.values():
            for ev in evs:
                if ev[0] is not E.sem:
                    E.wait(ev)

    def finish(self):
        E = self.eng["sp"]
        for s in self.ring + self.sw_ring:
            if s[1]:
                E.wait((s[0], s[1]))
        for E2 in self.eng.values():
            if E2.cnt and E2 is not E:
                E.wait((E2.sem, E2.cnt))


def load_cast(k, dst, src, wbuf, maxel=2048):
    n = src.shape[-1]
    for c0 in range(0, n, maxel):
        c1 = min(n, c0 + maxel)
        k.dma("pool", dst[:, c0:c1], src[:, c0:c1], writes=[wbuf])


def rms_to_hT(k, es_bufs, x_ap, xbuf, np_, gT, hT, hT_buf, tok0, consts):
    sq, ss, rstd, xs, xs_b, ss_b, pT, pT_b, ident = es_bufs
    k.op("act", lambda e: e.activation(out=sq[:np_, :], in_=x_ap, func=AF.Square, accum_out=ss[:np_, :]),
         reads=[xbuf], writes=[ss_b])
    k.op("act", lambda e: e.activation(out=rstd[:np_, :], in_=ss[:np_, :], func=AF.Sqrt, scale=1.0 / D, bias=consts["eps"][:np_, :]),
         reads=[ss_b], writes=[ss_b])
    k.op("dve", lambda e: e.reciprocal(out=rstd[:np_, :], in_=rstd[:np_, :]), reads=[ss_b], writes=[ss_b])
    k.op("act", lambda e: e.activation(out=xs[:np_, :], in_=x_ap, func=AF.Copy, scale=rstd[:np_, :]),
         reads=[xbuf, ss_b], writes=[xs_b])
    for kc in range(8):
        k.op("pe", lambda e: e.transpose(out=pT[:, kc, :np_], in_=xs[:np_, kc * 128:(kc + 1) * 128], identity=ident[:np_, :np_]),
             reads=[xs_b], writes=[pT_b])
    k.op("dve", lambda e: e.tensor_tensor(out=hT[:, :, tok0:tok0 + np_], in0=pT[:, :, :np_],
                                          in1=gT[:, :].unsqueeze(2).to_broadcast([128, 8, np_]), op=ALU.mult),
         reads=[pT_b], writes=[hT_buf])


class PsumPool:
    def __init__(self, k, nc, es, n=8):
        self.k = k
        self.banks = []
        for i in range(n):
            t = es.enter_context(nc.psum_tensor(k.uniq("bank"), [128, 512], F32))
            self.banks.append((t, k.buf(f"bank{i}")))
        self.i = 0

    def get(self):
        b = self.banks[self.i]
        self.i = (self.i + 1) % len(self.banks)
        return b


def rw_host_consts():
    i = np.arange(128)[:, None]
    t = np.arange(128)[None, :]
    f = np.float32
    U1 = (i <= t).astype(f) - (i <= 63).astype(f)
    U2 = (i < t).astype(f) - (i <= 63).astype(f)
    sel2 = np.zeros((128, 128), f)
    sel2[:64, 0] = 1
    sel2[64:, 1] = 1
    st = (i < t).astype(f)
    inc = (i <= t).astype(f)
    maskB = (t < i).astype(f)
    ident = np.eye(128, dtype=f)
    ones = np.ones((128, 128), f)
    return np.ascontiguousarray(np.stack([U1, U2, sel2, st, st, inc, inc, maskB, ident, ones], axis=1))


C_DEC = float(np.exp(-0.5))
GN_EPS = 64e-5


def rwkv_phase(k, nc, W, Z, O, p_wkv, rwc_in, state_shift, state_wkv, s_wkv, RS):
    with contextlib.ExitStack() as es3:
        sb = lambda n, s, d=F32: es3.enter_context(nc.sbuf_tensor(k.uniq(n), s, d))
        pp = PsumPool(k, nc, es3)
        cb = k.buf("rwconst")
        RC = sb("RC", [128, 10, 128])
        k.dma("sp", RC[:], rwc_in, writes=[cb])
        U1, U2, SEL2 = RC[:, 0, :], RC[:, 1, :], RC[:, 2, 0:2]
        MASKA = RC[:, 3:7, :]
        MASKB = RC[:, 7, :]
        IDF = RC[:, 8, :]
        ONES1 = RC[0:1, 9, :]
        bc = {}
        for n, w in (("shift_mu", RWC), ("k_k", 512), ("k_a", 512), ("r_k", 512), ("ln_x_w", 512), ("ln_x_b", 512)):
            bc[n] = sb("bc_" + n, [128, w])
            k.dma("sp", bc[n][:], W[n].partition_broadcast(128), writes=[cb])
        w2a2 = sb("w2a2", [128, 512])
        k.dma("sp", w2a2[0:64, :], W["decay_w2"], writes=[cb])
        k.dma("sp", w2a2[64:128, :], W["aaa_a2"], writes=[cb])
        g2 = sb("g2", [128, 512])
        k.dma("sp", g2[:], W["gate_g2"], writes=[cb])
        w0a0 = sb("w0a0", [1, 2, 512])
        k.dma("sp", w0a0[0:1, 0, :], W["decay_w0"].unsqueeze(0), writes=[cb])
        k.dma("sp", w0a0[0:1, 1, :], W["aaa_a0"].unsqueeze(0), writes=[cb])

        es4 = contextlib.ExitStack()

        def T(n, shape=(128, 512), tmp_stack=False):
            st = es4 if tmp_stack else es3
            return st.enter_context(nc.sbuf_tensor(k.uniq(n), list(shape), F32)), k.buf(n)

        def TP(n, shape=(128, 512)):
            return T(n, shape, True)

        pt, pt_b = T("pt", (128, RWC))
        pv, pv_b = T("pv", (128, RWC))
        xs, xs_b = T("xs", (128, RWC))
        loraT, lora_b = T("loraT", (128, 2, 128))
        sg, sg_b = T("sg")
        aa, aa_b = T("aa")
        gg, gg_b = T("gg")
        kk, kk_b = T("kk")
        tmp, tmp_b = T("tmp")
        sm, sm_b = T("sm", (128, 8, 4))
        kap, kap_b = T("kap")
        kmod, kmod_b = T("kmod")
        EN, EN_b = T("EN")
        Bn, Bn_b = T("Bn")
        GM_b = [k.buf() for _ in range(8)]
        NT_hb = [[k.buf() for _ in range(2)] for _ in range(2)]
        AA_hb = [[k.buf() for _ in range(2)] for _ in range(2)]
        Tm_b = [k.buf() for _ in range(2)]
        yt, yt_b = T("yt")
        ysq, ysq_b = T("ysq")
        ot, ot_b = T("ot")
        EP, EP_b = TP("EP")
        EPX, EPX_b = TP("EPX")
        PV, PV_b = TP("PV", (128, 4, 2))
        Kc, Kc_b = TP("Kc")
        Kt, Kt_b = TP("Kt")
        Rt, Rt_b = TP("Rt")
        FT = [TP(f"FT{i}", (128, 4, 128)) for i in range(4)]
        GM, _ = TP("GM", (128, 8, 4, 128))
        NT = [TP(f"NT{i}", (128, 8, 128)) for i in range(2)]
        AA = [TP(f"AA{i}", (128, 8, 128)) for i in range(2)]
        Tm, _ = TP("Tm", (128, 8, 128))
        ST = [TP(f"ST{i}", (128, 4, 64)) for i in range(2)]
        S0s, S0s_b = TP("S0s", (128, 4, 64))
        XTs, XTs_b = TP("XTs", (128, 8, 64))
        ETs, ETs_b = TP("ETs", (128, 8, 64))

        def act(fn, r, w):
            return k.op("act", fn, reads=r, writes=w)

        def dve(fn, r, w):
            return k.op("dve", fn, reads=r, writes=w)

        def pool(fn, r, w):
            return k.op("pool", fn, reads=r, writes=w)

        def pe(fn, r, w):
            return k.op("pe", fn, reads=r, writes=w)

        r_ap, k_ap, v_ap = xs[:, 0:512], xs[:, 512:1024], xs[:, 1024:1536]

        def prep():
            pool(lambda e: e.tensor_tensor(out=pv[:], in0=pv[:], in1=pt[:], op=ALU.subtract), [pt_b, pv_b], [pv_b])
            pool(lambda e: e.tensor_tensor(out=pv[:], in0=pv[:], in1=bc["shift_mu"][:], op=ALU.mult), [pv_b, cb], [pv_b])
            dve(lambda e: e.tensor_tensor(out=xs[:], in0=pv[:], in1=pt[:], op=ALU.add), [pv_b, pt_b], [xs_b])
            bT, bT_b = pp.get()
            for c in range(2):
                pe(lambda e: e.transpose(out=bT[:, c * 128:(c + 1) * 128], in_=xs[:, 1536 + c * 128:1664 + c * 128], identity=IDF),
                   [xs_b, cb], [bT_b])
            act(lambda e: e.activation(out=loraT[0:64, 0, :], in_=bT[0:64, 0:128], func=AF.Tanh), [bT_b], [lora_b])
            act(lambda e: e.activation(out=loraT[64:128, 0, :], in_=bT[64:128, 0:128], func=AF.Copy), [bT_b], [lora_b])
            act(lambda e: e.activation(out=loraT[:, 1, :], in_=bT[:, 128:256], func=AF.Sigmoid), [bT_b], [lora_b])
            pw, pw_b = pp.get()
            pe(lambda e: e.matmul(pw[:], lhsT=loraT[0:64, 0, :], rhs=w2a2[0:64, :], start=True, stop=False), [lora_b, cb], [pw_b])
            pe(lambda e: e.matmul(pw[:], lhsT=ONES1, rhs=w0a0[0:1, 0, :], start=False, stop=True), [cb], [pw_b])
            pa, pa_b = pp.get()
            pe(lambda e: e.matmul(pa[:], lhsT=loraT[64:128, 0, :], rhs=w2a2[64:128, :], start=True, stop=False), [lora_b, cb], [pa_b])
            pe(lambda e: e.matmul(pa[:], lhsT=ONES1, rhs=w0a0[0:1, 1, :], start=False, stop=True), [cb], [pa_b])
            pg, pg_b = pp.get()
            pe(lambda e: e.matmul(pg[:], lhsT=loraT[:, 1, :], rhs=g2[:], start=True, stop=True), [lora_b, cb], [pg_b])
            act(lambda e: e.activation(out=sg[:], in_=pw[:], func=AF.Sigmoid), [pw_b], [sg_b])
            act(lambda e: e.activation(out=aa[:], in_=pa[:], func=AF.Sigmoid), [pa_b], [aa_b])
            act(lambda e: e.copy(out=gg[:], in_=pg[:]), [pg_b], [gg_b])
            dve(lambda e: e.tensor_tensor(out=kk[:], in0=k_ap, in1=bc["k_k"][:], op=ALU.mult), [xs_b, cb], [kk_b])
            pool(lambda e: e.tensor_tensor(out=tmp[:], in0=kk[:], in1=kk[:], op=ALU.mult), [kk_b], [tmp_b])
            dve(lambda e: e.tensor_reduce(out=sm[:, :, 0], in_=tmp[:].rearrange("p (h d) -> p h d", d=64), axis=AX.X, op=ALU.add), [tmp_b], [sm_b])
            act(lambda e: e.activation(out=sm[:, :, 1], in_=sm[:, :, 0], func=AF.Sqrt), [sm_b], [sm_b])
            dve(lambda e: e.tensor_scalar(out=sm[:, :, 1], in0=sm[:, :, 1], scalar1=1e-12, scalar2=None, op0=ALU.max), [sm_b], [sm_b])
            dve(lambda e: e.reciprocal(out=sm[:, :, 1], in_=sm[:, :, 1]), [sm_b], [sm_b])
            dve(lambda e: e.tensor_tensor(out=kap[:].rearrange("p (h d) -> p h d", d=64), in0=kk[:].rearrange("p (h d) -> p h d", d=64),
                                          in1=sm[:, :, 1:2].to_broadcast([128, 8, 64]), op=ALU.mult), [kk_b, sm_b], [kap_b])
            dve(lambda e: e.scalar_tensor_tensor(out=tmp[:], in0=aa[:], scalar=-1.0, in1=bc["k_a"][:], op0=ALU.add, op1=ALU.mult), [aa_b, cb], [tmp_b])
            dve(lambda e: e.scalar_tensor_tensor(out=kmod[:], in0=tmp[:], scalar=1.0, in1=k_ap, op0=ALU.add, op1=ALU.mult), [tmp_b, xs_b], [kmod_b])

        def post(dst_ap, np_=128):
            y3 = yt[:].rearrange("p (h d) -> p h d", d=64)
            dve(lambda e: e.tensor_reduce(out=sm[:, :, 2], in_=y3, axis=AX.X, op=ALU.add), [yt_b], [sm_b])
            pool(lambda e: e.tensor_tensor(out=ysq[:], in0=yt[:], in1=yt[:], op=ALU.mult), [yt_b], [ysq_b])
            dve(lambda e: e.tensor_reduce(out=sm[:, :, 3], in_=ysq[:].rearrange("p (h d) -> p h d", d=64), axis=AX.X, op=ALU.add), [ysq_b], [sm_b])
            dve(lambda e: e.tensor_scalar(out=sm[:, :, 2], in0=sm[:, :, 2], scalar1=1.0 / 64, scalar2=None, op0=ALU.mult), [sm_b], [sm_b])
            dve(lambda e: e.tensor_tensor(out=sm[:, :, 0], in0=sm[:, :, 2], in1=sm[:, :, 2], op=ALU.mult), [sm_b], [sm_b])
            dve(lambda e: e.scalar_tensor_tensor(out=sm[:, :, 3], in0=sm[:, :, 3], scalar=1.0 / 64, in1=sm[:, :, 0], op0=ALU.mult, op1=ALU.subtract),
                [sm_b], [sm_b])
            dve(lambda e: e.tensor_scalar(out=sm[:, :, 3], in0=sm[:, :, 3], scalar1=GN_EPS, scalar2=None, op0=ALU.add), [sm_b], [sm_b])
            act(lambda e: e.activation(out=sm[:, :, 3], in_=sm[:, :, 3], func=AF.Sqrt), [sm_b], [sm_b])
            dve(lambda e: e.reciprocal(out=sm[:, :, 3], in_=sm[:, :, 3]), [sm_b], [sm_b])
            dve(lambda e: e.tensor_tensor(out=y3, in0=y3, in1=sm[:, :, 2:3].to_broadcast([128, 8, 64]), op=ALU.subtract), [yt_b, sm_b], [yt_b])
            dve(lambda e: e.tensor_tensor(out=y3, in0=y3, in1=sm[:, :, 3:4].to_broadcast([128, 8, 64]), op=ALU.mult), [yt_b, sm_b], [yt_b])
            pool(lambda e: e.tensor_tensor(out=yt[:], in0=yt[:], in1=bc["ln_x_w"][:], op=ALU.mult), [yt_b, cb], [yt_b])
            pool(lambda e: e.tensor_tensor(out=yt[:], in0=yt[:], in1=bc["ln_x_b"][:], op=ALU.add), [yt_b, cb], [yt_b])
            pool(lambda e: e.tensor_tensor(out=tmp[:], in0=r_ap, in1=kmod[:], op=ALU.mult), [xs_b, kmod_b], [tmp_b])
            pool(lambda e: e.tensor_tensor(out=tmp[:], in0=tmp[:], in1=bc["r_k"][:], op=ALU.mult), [tmp_b, cb], [tmp_b])
            dve(lambda e: e.tensor_reduce(out=sm[:, :, 0], in_=tmp[:].rearrange("p (h d) -> p h d", d=64), axis=AX.X, op=ALU.add), [tmp_b], [sm_b])
            dve(lambda e: e.tensor_tensor(out=ysq[:].rearrange("p (h d) -> p h d", d=64), in0=v_ap.rearrange("p (h d) -> p h d", d=64),
                                          in1=sm[:, :, 0:1].to_broadcast([128, 8, 64]), op=ALU.mult), [xs_b, sm_b], [ysq_b])
            dve(lambda e: e.tensor_tensor(out=ot[:], in0=yt[:], in1=ysq[:], op=ALU.add), [yt_b, ysq_b], [ot_b])
            dve(lambda e: e.tensor_tensor(out=ot[:], in0=ot[:], in1=gg[:], op=ALU.mult), [ot_b, gg_b], [ot_b])
            k.dma("sp", dst_ap, ot[:np_, :], reads=[ot_b])

        for seq in range(PB):
            stc = 0
            dve(lambda e: e.memset(ST[0][0][:], 0.0), [], [ST[0][1]])
            for ti in range(S // 128):
                r0 = seq * S + ti * 128
                k.dma("sp", pt[:], Z[r0:r0 + 128, NSA_COLS:INC], writes=[pt_b])
                if ti == 0:
                    dve(lambda e: e.memset(pv[0:1, :], 0.0), [], [pv_b])
                    k.dma("sp", pv[1:128, :], Z[r0:r0 + 127, NSA_COLS:INC], writes=[pv_b])
                else:
                    k.dma("sp", pv[:], Z[r0 - 1:r0 + 127, NSA_COLS:INC], writes=[pv_b])
                prep()
                lpi, lpi_b = pp.get()
                pe(lambda e: e.matmul(lpi[:], lhsT=U1, rhs=sg[:], start=True, stop=True), [sg_b, cb], [lpi_b])
                lpe, lpe_b = pp.get()
                pe(lambda e: e.matmul(lpe[:], lhsT=U2, rhs=sg[:], start=True, stop=True), [sg_b, cb], [lpe_b])
                ppv, ppv_b = pp.get()
                for gq in range(4):
                    pe(lambda e: e.matmul(ppv[:, gq * 2:gq * 2 + 2], lhsT=sg[:, gq * 128:(gq + 1) * 128], rhs=SEL2, start=True, stop=True),
                       [sg_b, cb], [ppv_b])
                act(lambda e: e.activation(out=EN[:], in_=lpi[:], func=AF.Exp, scale=C_DEC), [lpi_b], [EN_b])
                act(lambda e: e.activation(out=EP[:], in_=lpi[:], func=AF.Exp, scale=-C_DEC), [lpi_b], [EP_b])
                act(lambda e: e.activation(out=EPX[:], in_=lpe[:], func=AF.Exp, scale=-C_DEC), [lpe_b], [EPX_b])
                act(lambda e: e.activation(out=PV[:].rearrange("p a b -> p (a b)"), in_=ppv[:, 0:8], func=AF.Exp, scale=-C_DEC), [ppv_b], [PV_b])
                dve(lambda e: e.tensor_tensor(out=Kc[:], in0=kap[:], in1=EPX[:], op=ALU.mult), [kap_b, EPX_b], [Kc_b])
                pool(lambda e: e.tensor_tensor(out=tmp[:], in0=kap[:], in1=aa[:], op=ALU.mult), [kap_b, aa_b], [tmp_b])
                dve(lambda e: e.scalar_tensor_tensor(out=Bn[:], in0=tmp[:], scalar=-1.0, in1=EN[:], op0=ALU.mult, op1=ALU.mult), [tmp_b, EN_b], [Bn_b])
                pool(lambda e: e.tensor_tensor(out=Kt[:], in0=kmod[:], in1=EN[:], op=ALU.mult), [kmod_b, EN_b], [Kt_b])
                pool(lambda e: e.tensor_tensor(out=Rt[:], in0=r_ap, in1=EP[:], op=ALU.mult), [xs_b, EP_b], [Rt_b])
                for xi, (src, src_b) in enumerate(((Kc, Kc_b), (Bn, Bn_b), (Kt, Kt_b), (Rt, Rt_b))):
                    bk, bk_b = pp.get()
                    for gq in range(4):
                        pe(lambda e: e.transpose(out=bk[:, gq * 128:(gq + 1) * 128], in_=src[:, gq * 128:(gq + 1) * 128], identity=IDF),
                           [src_b, cb], [bk_b])
                    if xi % 2 == 0:
                        act(lambda e: e.copy(out=FT[xi][0][:].rearrange("p a b -> p (a b)"), in_=bk[:]), [bk_b], [FT[xi][1]])
                    else:
                        dve(lambda e: e.tensor_copy(out=FT[xi][0][:].rearrange("p a b -> p (a b)"), in_=bk[:]), [bk_b], [FT[xi][1]])
                KcT, BnT, KtT, RtT = (FT[i][0] for i in range(4))
                ftb = [FT[i][1] for i in range(4)]
                for hb in range(2):
                    b2, b2_b = pp.get()
                    for hl in range(4):
                        h = hb * 4 + hl
                        gq, base = h // 2, (h % 2) * 64
                        sl = slice(base, base + 64)
                        b1, b1_b = pp.get()
                        for gi, (l, r) in enumerate(((BnT, KcT), (KtT, KcT), (BnT, RtT), (KtT, RtT))):
                            pe(lambda e: e.matmul(b1[:, gi * 128:(gi + 1) * 128], lhsT=l[sl, gq, :], rhs=r[sl, gq, :], start=True, stop=True),
                               ftb, [b1_b])
                        dve(lambda e: e.tensor_tensor(out=GM[:, h, :, :], in0=b1[:].rearrange("p (a b) -> p a b", b=128), in1=MASKA, op=ALU.mult),
                            [b1_b, cb], [GM_b[h]])
                        pe(lambda e: e.matmul(b2[:, hl * 128:(hl + 1) * 128], lhsT=KcT[sl, gq, :], rhs=BnT[sl, gq, :], start=True, stop=True),
                           ftb, [b2_b])
                    dve(lambda e: e.tensor_tensor(out=NT[0][0][:, hb * 4:hb * 4 + 4, :], in0=b2[:].rearrange("p (a b) -> p a b", b=128),
                                                  in1=MASKB.unsqueeze(1).to_broadcast([128, 4, 128]), op=ALU.mult), [b2_b, cb], [NT_hb[0][hb]])
                for hb in range(2):
                    hs = slice(hb * 4, hb * 4 + 4)
                    gmb = GM_b[hb * 4:hb * 4 + 4]
                    dve(lambda e: e.tensor_tensor(out=Tm[:, hs, :], in0=GM[:, hs, 0, :], in1=IDF.unsqueeze(1).to_broadcast([128, 4, 128]), op=ALU.add),
                        gmb + [cb], [Tm_b[hb]])
                    cur = None
                    for rnd in range(6):
                        last = rnd == 5
                        new = rnd % 2
                        if rnd == 0:
                            Aold = lambda h: GM[:, h, 0, :]
                            Aold_b = gmb
                            ATold = lambda h: NT[0][0][:, h, :]
                            ATold_b = [NT_hb[0][hb]]
                            nA, nA_b = AA[0][0], AA_hb[0][hb]
                            nAT, nAT_b = NT[1][0], NT_hb[1][hb]
                        else:
                            pa_i, pt_i = (rnd - 1) % 2, rnd % 2
                            Aold = (lambda ii: (lambda h: AA[ii][0][:, h, :]))(pa_i)
                            Aold_b = [AA_hb[pa_i][hb]]
                            ATold = (lambda ii: (lambda h: NT[ii][0][:, h, :]))(pt_i)
                            ATold_b = [NT_hb[pt_i][hb]]
                            nA, nA_b = AA[1 - pa_i][0], AA_hb[1 - pa_i][hb]
                            nAT, nAT_b = NT[1 - pt_i][0], NT_hb[1 - pt_i][hb]
                        if not last:
                            bA, bA_b = pp.get()
                            for hl in range(4):
                                h = hb * 4 + hl
                                pe(lambda e: e.matmul(bA[:, hl * 128:(hl + 1) * 128], lhsT=ATold(h), rhs=Aold(h), start=True, stop=True),
                                   Aold_b + ATold_b, [bA_b])
                            act(lambda e: e.copy(out=nA[:, hs, :], in_=bA[:].rearrange("p (a b) -> p a b", b=128)), [bA_b], [nA_b])
                        bAT, bAT_b = pp.get()
                        for hl in range(4):
                            h = hb * 4 + hl
                            pe(lambda e: e.matmul(bAT[:, hl * 128:(hl + 1) * 128], lhsT=Aold(h), rhs=ATold(h), start=True, stop=True),
                               Aold_b + ATold_b, [bAT_b])
                        act(lambda e: e.copy(out=nAT[:, hs, :], in_=bAT[:].rearrange("p (a b) -> p a b", b=128)), [bAT_b], [nAT_b])
                        bTT, bTT_b = pp.get()
                        for hl in range(4):
                            h = hb * 4 + hl
                            pe(lambda e: e.matmul(bTT[:, hl * 128:(hl + 1) * 128], lhsT=nAT[:, h, :], rhs=Tm[:, h, :], start=True, stop=True),
                               [nAT_b, Tm_b[hb]], [bTT_b])
                        dve(lambda e: e.tensor_tensor(out=Tm[:, hs, :], in0=Tm[:, hs, :], in1=bTT[:].rearrange("p (a b) -> p a b", b=128), op=ALU.add),
                            [bTT_b, Tm_b[hb]], [Tm_b[hb]])
                Sold, Sold_b = ST[stc % 2]
                Snew, Snew_b = ST[(stc + 1) % 2]
                stc += 1
                dve(lambda e: e.tensor_tensor(out=S0s[:], in0=Sold[:], in1=PV[:, :, 0:1].to_broadcast([128, 4, 64]), op=ALU.mult),
                    [Sold_b, PV_b], [S0s_b])
                bX, bX_b = pp.get()
                for h in range(8):
                    gq, base = h // 2, (h % 2) * 64
                    sl = slice(base, base + 64)
                    pe(lambda e: e.matmul(bX[:, h * 64:(h + 1) * 64], lhsT=KcT[sl, gq, :], rhs=S0s[sl, gq, :], start=True, stop=False),
                       ftb + [S0s_b], [bX_b])
                    pe(lambda e: e.matmul(bX[:, h * 64:(h + 1) * 64], lhsT=GM[:, h, 1, :], rhs=v_ap[:, h * 64:(h + 1) * 64], start=False, stop=True),
                       [GM_b[h], xs_b], [bX_b])
                act(lambda e: e.copy(out=XTs[:].rearrange("p a b -> p (a b)"), in_=bX[:]), [bX_b], [XTs_b])
                bE, bE_b = pp.get()
                for h in range(8):
                    pe(lambda e: e.matmul(bE[:, h * 64:(h + 1) * 64], lhsT=Tm[:, h, :], rhs=XTs[:, h, :], start=True, stop=True),
                       [Tm_b[h // 4], XTs_b], [bE_b])
                dve(lambda e: e.tensor_copy(out=ETs[:].rearrange("p a b -> p (a b)"), in_=bE[:]), [bE_b], [ETs_b])
                bY, bY_b = pp.get()
                bS, bS_b = pp.get()
                for h in range(8):
                    gq, base = h // 2, (h % 2) * 64
                    sl = slice(base, base + 64)
                    hc = slice(h * 64, (h + 1) * 64)
                    pe(lambda e: e.matmul(bY[:, hc], lhsT=RtT[sl, gq, :], rhs=S0s[sl, gq, :], start=True, stop=False), ftb + [S0s_b], [bY_b])
                    pe(lambda e: e.matmul(bY[:, hc], lhsT=GM[:, h, 2, :], rhs=ETs[:, h, :], start=False, stop=False), [GM_b[h], ETs_b], [bY_b])
                    pe(lambda e: e.matmul(bY[:, hc], lhsT=GM[:, h, 3, :], rhs=v_ap[:, hc], start=False, stop=True), [GM_b[h], xs_b], [bY_b])
                for h in range(8):
                    gq, base = h // 2, (h % 2) * 64
                    sl = slice(base, base + 64)
                    hc = slice(h * 64, (h + 1) * 64)
                    pe(lambda e: e.matmul(bS[sl, gq * 64:(gq + 1) * 64], lhsT=Bn[:, hc], rhs=ETs[:, h, :], start=True, stop=False), [Bn_b, ETs_b], [bS_b])
                    pe(lambda e: e.matmul(bS[sl, gq * 64:(gq + 1) * 64], lhsT=Kt[:, hc], rhs=v_ap[:, hc], start=False, stop=True), [Kt_b, xs_b], [bS_b])
                dve(lambda e: e.tensor_tensor(out=Snew[:].rearrange("p a b -> p (a b)"), in0=bS[:, 0:256], in1=S0s[:].rearrange("p a b -> p (a b)"), op=ALU.add),
                    [bS_b, S0s_b], [Snew_b])
                dve(lambda e: e.tensor_tensor(out=Snew[:], in0=Snew[:], in1=PV[:, :, 1:2].to_broadcast([128, 4, 64]), op=ALU.mult),
                    [Snew_b, PV_b], [Snew_b])
                act(lambda e: e.copy(out=yt[:], in_=bY[:]), [bY_b], [yt_b])
                post(O[r0:r0 + 128, 512:1024])
            Sf, Sf_b = ST[stc % 2]
            with nc.allow_non_contiguous_dma(reason="state transpose store"):
                for h in range(8):
                    k.dma("sp", p_wkv[seq, h].rearrange("v k -> k v"), Sf[(h % 2) * 64:(h % 2) * 64 + 64, h // 2, :], reads=[Sf_b])

        k.barrier()
        es4.close()
        ZS0 = PB * S
        dve(lambda e: e.memset(pt[64:128, :], 0.0), [], [pt_b])
        dve(lambda e: e.memset(pv[64:128, :], 0.0), [], [pv_b])
        k.dma("sp", pt[0:64, :], Z[ZS0:ZS0 + 64, NSA_COLS:INC], writes=[pt_b])
        k.dma("sp", pv[0:16, :], state_shift, writes=[pv_b])
        k.dma("sp", pv[16:64, :], Z[ZS0:ZS0 + 48, NSA_COLS:INC], writes=[pv_b])
        prep()
        act(lambda e: e.activation(out=EN[:], in_=sg[:], func=AF.Exp, scale=-C_DEC), [sg_b], [EN_b])
        dve(lambda e: e.tensor_tensor(out=Bn[:], in0=kap[:], in1=aa[:], op=ALU.mult), [kap_b, aa_b], [Bn_b])
        rsb = k.buf("RS")
        for xi, (src, src_b) in enumerate(((kap[:, :], kap_b), (EN[:, :], EN_b), (Bn[:, :], Bn_b), (kmod[:, :], kmod_b), (r_ap, xs_b), (v_ap, xs_b))):
            k.dma("sp", RS[xi], src[0:64, :], reads=[src_b], writes=[rsb])
        Ssm, Ssm_b = T("Ssm", (128, 8, 8, 64))
        tA, tA_b = T("tA", (128, 8, 8, 64))
        tB, tB_b = T("tB", (128, 8, 8, 64))
        BC = [T(f"BC{i}", (128, 8, 512)) for i in range(4)]
        vcol, vcol_b = T("vcol", (128, 4, 64))
        skc, skc_b = T("skc", (128, 64))
        ycol, ycol_b = T("ycol", (128, 4, 64))
        with nc.allow_non_contiguous_dma(reason="per-step v columns"):
            for t in range(DS):
                for bh in range(2):
                    for bq in range(8):
                        src = bass.AP(tensor=RS.tensor, offset=5 * 64 * 512 + (t * 16 + 2 * bq + bh) * 512, ap=[[1, 64], [64, 8]])
                        k.dma("sp", vcol[bh * 64:(bh + 1) * 64, t, bq * 8:(bq + 1) * 8], src, reads=[rsb], writes=[vcol_b])
        for bh in range(2):
            for bq in range(8):
                k.dma("sp", Ssm[bh * 64:(bh + 1) * 64, bq, :, :], state_wkv[2 * bq + bh].rearrange("h v k -> v h k"), writes=[Ssm_b])
        S2 = Ssm[:].rearrange("p a b c -> p (a b c)")
        S3 = Ssm[:].rearrange("p a b c -> p (a b) c")
        A2 = tA[:].rearrange("p a b c -> p (a b c)")
        A3 = tA[:].rearrange("p a b c -> p (a b) c")
        B2 = tB[:].rearrange("p a b c -> p (a b c)")
        B3 = tB[:].rearrange("p a b c -> p (a b) c")
        for t in range(DS):
            def bload(xi, slot):
                for bh in range(2):
                    src = bass.AP(tensor=RS.tensor, offset=xi * 64 * 512 + (t * 16 + bh) * 512, ap=[[0, 64], [1024, 8], [1, 512]])
                    k.dma("sp", BC[slot][0][bh * 64:(bh + 1) * 64, :, :], src, reads=[rsb], writes=[BC[slot][1]])

            for xi in range(4):
                bload(xi, xi)
            bcf = [BC[i][0][:].rearrange("p a b -> p (a b)") for i in range(4)]
            bcb = [BC[i][1] for i in range(4)]
            dve(lambda e: e.tensor_tensor(out=A2, in0=S2, in1=bcf[0], op=ALU.mult), [Ssm_b, bcb[0]], [tA_b])
            bload(4, 0)
            dve(lambda e: e.tensor_reduce(out=skc[:], in_=A3, axis=AX.X, op=ALU.add), [tA_b], [skc_b])
            dve(lambda e: e.tensor_tensor(out=S2, in0=S2, in1=bcf[1], op=ALU.mult), [Ssm_b, bcb[1]], [Ssm_b])
            dve(lambda e: e.tensor_tensor(out=B3, in0=BC[2][0][:].rearrange("p a (b c) -> p (a b) c", c=64),
                                          in1=skc[:].unsqueeze(2).to_broadcast([128, 64, 64]), op=ALU.mult), [bcb[2], skc_b], [tB_b])
            dve(lambda e: e.tensor_tensor(out=S2, in0=S2, in1=B2, op=ALU.subtract), [Ssm_b, tB_b], [Ssm_b])
            dve(lambda e: e.tensor_tensor(out=A3, in0=BC[3][0][:].rearrange("p a (b c) -> p (a b) c", c=64),
                                          in1=vcol[:, t, :].unsqueeze(2).to_broadcast([128, 64, 64]), op=ALU.mult), [bcb[3], vcol_b], [tA_b])
            dve(lambda e: e.tensor_tensor(out=S2, in0=S2, in1=A2, op=ALU.add), [Ssm_b, tA_b], [Ssm_b])
            dve(lambda e: e.tensor_tensor(out=B2, in0=S2, in1=bcf[0], op=ALU.mult), [Ssm_b, bcb[0]], [tB_b])
            dve(lambda e: e.tensor_reduce(out=ycol[:, t, :], in_=B3, axis=AX.X, op=ALU.add), [tB_b], [ycol_b])
        for bh in range(2):
            for bq in range(8):
                k.dma("sp", s_wkv[2 * bq + bh].rearrange("h v k -> v h k"), Ssm[bh * 64:(bh + 1) * 64, bq, :, :], reads=[Ssm_b])
        with nc.allow_non_contiguous_dma(reason="per-step y columns"):
            for t in range(DS):
                for bh in range(2):
                    for bq in range(8):
                        dst = bass.AP(tensor=RS.tensor, offset=6 * 64 * 512 + (t * 16 + 2 * bq + bh) * 512, ap=[[1, 64], [64, 8]])
                        k.dma("sp", dst, ycol[bh * 64:(bh + 1) * 64, t, bq * 8:(bq + 1) * 8], reads=[ycol_b], writes=[rsb])
        dve(lambda e: e.memset(yt[64:128, :], 0.0), [], [yt_b])
        k.dma("sp", yt[0:64, :], RS[6], reads=[rsb], writes=[yt_b])
        post(O[ZS0:ZS0 + 64, 512:1024], 64)

        k.barrier()


def t5_bucket_np(dist):
    import math
    n = np.maximum(dist, 0)
    nf = np.maximum(n, 1).astype(np.float32)
    large = 16 + (np.log(nf / np.float32(16)) / np.float32(math.log(8.0)) * np.float32(16)).astype(np.int32)
    return np.where(n < 16, n, np.minimum(large, 31)).astype(np.int64)


NEG = -30000.0


def nsa_host_consts():
    f = np.float32
    out = {}
    u = np.arange(768)
    dist = u - 127
    ok = (dist >= 0) & (dist <= 512)
    ohg = np.zeros((33, 768), f)
    ohg[t5_bucket_np(dist)[ok], u[ok]] = 1
    ohg[32, ~ok] = NEG
    out["nsa_ohg"] = ohg
    i = np.arange(128)
    ohc = np.zeros((33, 7, 128), f)
    ohc[31, 6, :] = 1
    for p, dl in enumerate((1, 0, -1, -2, -5, 3)):
        d = i - 63 - 64 * dl
        okp = d >= 0
        ohc[t5_bucket_np(d)[okp], p, i[okp]] = 1
        ohc[32, p, ~okp] = NEG
    out["nsa_ohc"] = ohc
    out["nsa_J"] = np.ascontiguousarray(np.eye(128, dtype=f)[::-1])
    E = np.zeros((32, 16, 128), f)
    for kt in range(16):
        E[2 * kt, kt, :64] = 1
        E[2 * kt + 1, kt, 64:] = 1
    out["nsa_E"] = E
    ext = np.zeros((128, 3, 62), f)
    for c in range(62):
        dl = c - 30
        cur = (i >= 64).astype(np.int64)
        allowed = dl <= cur
        forced = (dl == cur) | (dl == cur - 1)
        ext[:, 0, c] = (allowed & ~forced)
        ext[:, 1, c] = np.where(forced & allowed, 1e4, np.where(~allowed, -1e4, 0.0))
        ext[:, 2, c] = allowed
    out["nsa_ext"] = ext
    out["nsa_bm"] = (np.arange(128)[:, None] // 32 == np.arange(4)[None, :]).astype(f)
    return out


def nsa_prompt_phase(k, nc, W, Z, O, cin, GD):
    with contextlib.ExitStack() as es3:
        sbt = lambda n, s, d=F32: es3.enter_context(nc.sbuf_tensor(k.uniq(n), list(s), d))
        pp = PsumPool(k, nc, es3, 4)
        acc_banks = PsumPool(k, nc, es3, 4)

        def act(fn, r, w):
            return k.op("act", fn, reads=r, writes=w)

        def dve(fn, r, w):
            return k.op("dve", fn, reads=r, writes=w)

        def pool(fn, r, w):
            return k.op("pool", fn, reads=r, writes=w)

        def pe(fn, r, w):
            return k.op("pe", fn, reads=r, writes=w)

        cb = k.buf("nsaconst")
        IDF = sbt("IDF", [128, 128])
        k.dma("sp", IDF[:], cin["ident"], writes=[cb])
        IDB = sbt("IDB", [128, 128], BF16)
        dve(lambda e: e.tensor_copy(out=IDB[:], in_=IDF[:]), [cb], [cb])
        tab33 = sbt("tab33", [33, 8])
        k.dma("sp", tab33[0:32, :], W["rel_bias_table"], writes=[cb])
        dve(lambda e: e.memset(tab33[32:33, :], 1.0), [], [cb])
        OHG = sbt("OHG", [33, 768])
        k.dma("sp", OHG[:], cin["nsa_ohg"], writes=[cb])
        OHC = sbt("OHC", [33, 7, 128])
        k.dma("sp", OHC[:], cin["nsa_ohc"], writes=[cb])
        JM = sbt("JM", [128, 128])
        k.dma("sp", JM[:], cin["nsa_J"], writes=[cb])
        Ef = sbt("Ef", [32, 16, 128])
        k.dma("sp", Ef[:], cin["nsa_E"], writes=[cb])
        Eb = sbt("Eb", [64, 16, 128], BF16)
        dve(lambda e: e.memset(Eb[:], 0.0), [], [cb])
        dve(lambda e: e.tensor_copy(out=Eb[0:32, :, :], in_=Ef[:]), [cb], [cb])
        BM = sbt("BM", [128, 4])
        k.dma("sp", BM[:], cin["nsa_bm"], writes=[cb])
        EXT = sbt("EXT", [128, 3, 62])
        k.dma("sp", EXT[:], cin["nsa_ext"], writes=[cb])
        W1s = sbt("W1s", [64, 64, 64])
        W1 = [W1s, W1s]
        W1_b = k.buf("W1")
        PE_ = [sbt("pek", [64, 64]), sbt("pev", [64, 64])]
        W2 = [sbt("W2k", [64, 64]), sbt("W2v", [64, 64])]
        for x, sfx in enumerate(("k", "v")):
            k.dma("sp", PE_[x][:], W["cmp_pe_" + sfx], writes=[cb])
            k.dma("sp", W2[x][:], W["cmp_w2_" + sfx], writes=[cb])
        hidpe = sbt("hidpe", [64, 2])
        PE2 = sbt("PE2", [64, 2, 64, 2])
        for x in range(2):
            dve(lambda e: e.tensor_copy(out=PE2[:, x, :, :], in_=PE_[x][:, :].unsqueeze(2).to_broadcast([64, 64, 2])), [cb], [cb])
        for x in range(2):
            k.dma("sp", W1s[:], W["cmp_w1_" + "kv"[x]], writes=[W1_b])
            b, b_b = pp.get()
            for d in range(64):
                pe(lambda e: e.matmul(b[0:64, 0:2], lhsT=W1[x][:, d, :], rhs=PE2[:, x, d, :], start=(d == 0), stop=(d == 63)), [cb, W1_b], [b_b])
            dve(lambda e: e.tensor_copy(out=hidpe[:, x:x + 1], in_=b[0:64, 0:1]), [b_b], [cb])
        b, b_b = pp.get()
        for p in range(7):
            pe(lambda e: e.matmul(b[:, p * 8:(p + 1) * 8], lhsT=OHC[:, p, :], rhs=tab33[:, :], start=True, stop=True), [cb], [b_b])
        pat = sbt("pat", [128, 7, 8])
        dve(lambda e: e.tensor_copy(out=pat[:].rearrange("p a b -> p (a b)"), in_=b[:, 0:56]), [b_b], [cb])
        tb31 = pat[:, 6, :]
        Gs8 = sbt("Gs8", [8, 768])
        for half in range(2):
            b, b_b = pp.get()
            pe(lambda e: e.matmul(b[0:8, 0:384], lhsT=tab33[:, :], rhs=OHG[:, half * 384:(half + 1) * 384], start=True, stop=True), [cb], [b_b])
            dve(lambda e: e.tensor_copy(out=Gs8[:, half * 384:(half + 1) * 384], in_=b[0:8, 0:384]), [b_b], [cb])
        gdb = k.buf("GD")
        k.dma("sp", GD, Gs8[:], reads=[cb], writes=[gdb])
        TB = sbt("TB", [128, 8, 3, 128])
        TBb = sbt("TBb", [128, 8, 3, 128], BF16)
        Hh = [sbt(f"Hh{i}", [128, 128]) for i in range(2)]
        Hh_b = [k.buf() for _ in range(2)]
        hi = 0
        for h in range(8):
            for ri, rho in enumerate((0, 128, 512)):
                j = hi % 2
                hi += 1
                src = bass.AP(tensor=GD.tensor, offset=h * 768 + rho, ap=[[1, 128], [1, 128]])
                k.dma("sp", Hh[j][:], src, reads=[gdb], writes=[Hh_b[j]])
                b, b_b = pp.get()
                pe(lambda e: e.matmul(b[:, 0:128], lhsT=JM[:], rhs=Hh[j][:], start=True, stop=True), [cb, Hh_b[j]], [b_b])
                act(lambda e: e.copy(out=TB[:, h, ri, :], in_=b[:, 0:128]), [b_b], [cb])
                if TB_SHIFT:
                    dve(lambda e: e.tensor_scalar(out=TB[:, h, ri, :], in0=TB[:, h, ri, :], scalar1=tb31[:, h:h + 1], scalar2=None, op0=ALU.subtract), [cb], [cb])
        GEXT = sbt("GEXT", [128, 8, 62])
        dve(lambda e: e.tensor_copy(out=GEXT[:, :, 0:28], in_=pat[:, 4, :].unsqueeze(2).to_broadcast([128, 8, 28])), [cb], [cb])
        for c, p in ((28, 3), (29, 2), (30, 1), (31, 0)):
            dve(lambda e: e.tensor_copy(out=GEXT[:, :, c], in_=pat[:, p, :]), [cb], [cb])
        dve(lambda e: e.tensor_copy(out=GEXT[:, :, 32:62], in_=pat[:, 5, :].unsqueeze(2).to_broadcast([128, 8, 30])), [cb], [cb])

        dve(lambda e: e.tensor_copy(out=TBb[:].rearrange("p a b c -> p (a b c)"), in_=TB[:].rearrange("p a b c -> p (a b c)")), [cb], [cb])
        k.barrier()
        tmpS = [sbt(f"tmpS{i}", [128, 256]) for i in range(2)]
        tmpS_b = [k.buf() for _ in range(2)]
        PT = [sbt(f"PT{i}", [128, 512], BF16) for i in range(3)]
        PT_b = [k.buf() for _ in range(3)]
        onr = sbt("onr", [128, 2, 4, 64])
        onr_b = k.buf("onr")
        qT = sbt("qT", [64, 8, S], BF16)
        qT_b = k.buf("qT")
        kT = sbt("kT", [64, 2, 2, S], BF16)
        kT_b = k.buf("kT")
        vaug = sbt("vaug", [128, 2, 16, 2, 66], BF16)
        vaug_b = k.buf("vaug")
        dve(lambda e: e.memset(vaug[:].rearrange("p a b c d -> p (a b c) d")[:, :, 64:65], 1.0), [], [vaug_b])
        gate_all = sbt("gate_all", [128, 16, 24])
        gate_b = k.buf("gate")
        selbT = sbt("selbT", [64, 2, S], BF16)
        selbT_b = k.buf("selbT")
        dve(lambda e: e.memset(selbT[:].rearrange("p a b -> p (a b)"), 0.0), [], [selbT_b])
        o_acc = sbt("o_acc", [128, 16, 512])
        oacc_b = [k.buf() for _ in range(16)]
        zrow = [sbt(f"zrow{i}", [128, NSA_COLS]) for i in range(2)]
        zrow_b = [k.buf() for _ in range(2)]
        Rst = sbt("Rst", [64, 32, 128])
        Rst_b = k.buf("Rst")
        Gs = sbt("Gs", [64, 64])
        Gs_b = k.buf("Gs")
        Gs4 = sbt("Gs4", [64, 2, 4, 32])
        Gs4_b = k.buf("Gs4")
        kccT = sbt("kccT", [64, 2, 32], BF16)
        kcc_b = k.buf("kcc")
        Vbd = sbt("Vbd", [128, 2, 256], BF16)
        Vbd_b = k.buf("Vbd")
        dve(lambda e: e.memset(Vbd[:], 0.0), [], [Vbd_b])
        Sc = sbt("Sc", [128, 4, 32])
        Sc_b = k.buf("Sc")
        Pn = sbt("Pn", [128, 4, 32])
        Pn_b = k.buf("Pn")
        Pnb = sbt("Pnb", [128, 128], BF16)
        Pnb_b = k.buf("Pnb")
        PTc = sbt("PTc", [128, 128], BF16)
        PTc_b = k.buf("PTc")
        st4 = sbt("st4", [128, 8, 4])
        st4_b = k.buf("st4")
        imp = sbt("imp", [128, 4, 32])
        imp_b = k.buf("imp")
        m8 = sbt("m8", [128, 2, 8])
        m8_b = k.buf("m8")
        selq = sbt("selq", [128, 32])
        selq_b = k.buf("selq")
        cnt = {"z": 0, "tmpS": 0, "PT": 0}

        for seq in range(PB):
            base = seq * S
            for ti in range(16):
                r0 = base + ti * 128
                tok = slice(ti * 128, (ti + 1) * 128)
                zi = cnt["z"] % 2
                cnt["z"] += 1
                zr, zr_b = zrow[zi], zrow_b[zi]
                k.dma("sp", zr[:], Z[r0:r0 + 128, 0:NSA_COLS], writes=[zr_b])
                for hb in range(2):
                    b, b_b = pp.get()
                    for hl in range(4):
                        h = hb * 4 + hl
                        pe(lambda e: e.transpose(out=b[0:64, hl * 128:(hl + 1) * 128], in_=zr[:, h * 64:(h + 1) * 64], identity=IDF[:]), [zr_b, cb], [b_b])
                    act(lambda e: e.activation(out=qT[:, hb * 4:hb * 4 + 4, tok], in_=b[0:64, :].rearrange("p (a b) -> p a b", b=128), func=AF.Copy, scale=0.125),
                        [b_b], [qT_b])
                b, b_b = pp.get()
                for ci, c0 in enumerate((768, 832, 1024, 1088)):
                    pe(lambda e: e.transpose(out=b[0:64, ci * 128:(ci + 1) * 128], in_=zr[:, c0:c0 + 64], identity=IDF[:]), [zr_b, cb], [b_b])
                dve(lambda e: e.tensor_copy(out=kT[:, :, :, tok], in_=b[0:64, :].rearrange("p (a g b) -> p a g b", a=2, g=2)), [b_b], [kT_b])
                veng = pool if VAUG_ENG == 0 else dve
                veng(lambda e: e.tensor_copy(out=vaug[:, 0, ti, :, 0:64], in_=zr[:, 896:1024].rearrange("p (g d) -> p g d", d=64)), [zr_b], [vaug_b])
                veng(lambda e: e.tensor_copy(out=vaug[:, 1, ti, :, 0:64], in_=zr[:, 1152:1280].rearrange("p (g d) -> p g d", d=64)), [zr_b], [vaug_b])
                act(lambda e: e.activation(out=gate_all[:, ti, :], in_=zr[:, 1280:1304], func=AF.Sigmoid), [zr_b], [gate_b])
            if S_BARRIERS:
                k.barrier()
            for x in range(2 if (NSA_PARTS >= 2 and not S_SKIP23) else 0):
                k.dma("sp", Rst[:], Z[base:base + S, 512 + 128 * x:640 + 128 * x].rearrange("(b c) x -> c b x", c=64), writes=[Rst_b])
                k.dma("sp", W1s[:], W["cmp_w1_" + "kv"[x]], writes=[W1_b])
                b, b_b = pp.get()
                Rv = Rst[:].rearrange("c b (g d) -> c g b d", d=64)
                for d in range(64):
                    pe(lambda e: e.matmul(b[0:64, 0:64], lhsT=W1[x][:, d, :], rhs=Rv[:, :, :, d], start=(d == 0), stop=(d == 63)), [Rst_b, cb, W1_b], [b_b])
                act(lambda e: e.activation(out=Gs[:], in_=b[0:64, 0:64], func=AF.Gelu_apprx_tanh, bias=hidpe[:, x:x + 1]), [b_b, cb], [Gs_b])
                if x == 0:
                    b2, b2_b = pp.get()
                    pe(lambda e: e.matmul(b2[0:64, 0:64], lhsT=W2[0][:], rhs=Gs[:], start=True, stop=True), [Gs_b, cb], [b2_b])
                    dve(lambda e: e.tensor_copy(out=kccT[:].rearrange("p g b -> p (g b)"), in_=b2[0:64, 0:64]), [b2_b], [kcc_b])
                else:
                    dve(lambda e: e.tensor_copy(out=Gs4[:], in_=Gs[:].rearrange("p (g b) -> p g b", b=32).unsqueeze(2).to_broadcast([64, 2, 4, 32])), [Gs_b], [Gs4_b])
                    b2, b2_b = pp.get()
                    for g in range(2):
                        pe(lambda e: e.matmul(b2[:, g * 64:(g + 1) * 64], lhsT=Gs4[:, g, :, :].rearrange("p a b -> p (a b)"), rhs=W2[1][:], start=True, stop=True),
                           [Gs4_b, cb], [b2_b])
                    for g in range(2):
                        dve(lambda e: e.tensor_tensor(out=Vbd[:, g, :].rearrange("p (a b) -> p a b", b=64),
                                                      in0=b2[:, g * 64:(g + 1) * 64].unsqueeze(1).to_broadcast([128, 4, 64]),
                                                      in1=BM[:, :].unsqueeze(2).to_broadcast([128, 4, 64]), op=ALU.mult), [b2_b, cb], [Vbd_b])
            if S_BARRIERS:
                k.barrier()
            for ti in range(16 if (NSA_PARTS >= 3 and not S_SKIP23) else 0):
                tok = slice(ti * 128, (ti + 1) * 128)
                c0 = 30 - 2 * ti
                for g in range(2):
                    b, b_b = pp.get()
                    for r in range(4):
                        pe(lambda e: e.matmul(b[:, r * 32:(r + 1) * 32], lhsT=qT[:, 4 * g + r, tok], rhs=kccT[:, g, :], start=True, stop=True), [qT_b, kcc_b], [b_b])
                    dve(lambda e: e.tensor_tensor(out=Sc[:], in0=b[:, 0:128].rearrange("p (a b) -> p a b", b=32), in1=GEXT[:, 4 * g:4 * g + 4, c0:c0 + 32], op=ALU.add),
                        [b_b, cb], [Sc_b])
                    dve(lambda e: e.tensor_reduce(out=st4[:, 0, :], in_=Sc[:], axis=AX.X, op=ALU.max), [Sc_b], [st4_b])
                    dve(lambda e: e.tensor_scalar(out=st4[:, 0, :], in0=st4[:, 0, :], scalar1=-100.0, scalar2=-1.0, op0=ALU.max, op1=ALU.mult), [st4_b], [st4_b])
                    for r in range(4):
                        act(lambda e: e.activation(out=Pn[:, r, :], in_=Sc[:, r, :], func=AF.Exp, bias=st4[:, 0, r:r + 1], accum_out=st4[:, 1, r:r + 1]),
                            [Sc_b, st4_b], [Pn_b, st4_b])
                    dve(lambda e: e.tensor_scalar(out=st4[:, 1, :], in0=st4[:, 1, :], scalar1=1e-30, scalar2=None, op0=ALU.max), [st4_b], [st4_b])
                    dve(lambda e: e.reciprocal(out=st4[:, 1, :], in_=st4[:, 1, :]), [st4_b], [st4_b])
                    dve(lambda e: e.tensor_tensor(out=Pn[:], in0=Pn[:], in1=st4[:, 1, :].unsqueeze(2).to_broadcast([128, 4, 32]), op=ALU.mult), [Pn_b, st4_b], [Pn_b])
                    dve(lambda e: e.tensor_reduce(out=imp[:, 0, :], in_=Pn[:].rearrange("p r b -> p b r"), axis=AX.X, op=ALU.add), [Pn_b], [imp_b])
                    dve(lambda e: e.tensor_tensor(out=imp[:, 1, :], in0=imp[:, 0, :], in1=EXT[:, 0, c0:c0 + 32], op=ALU.mult), [imp_b, cb], [imp_b])
                    dve(lambda e: e.tensor_tensor(out=imp[:, 1, :], in0=imp[:, 1, :], in1=EXT[:, 1, c0:c0 + 32], op=ALU.add), [imp_b, cb], [imp_b])
                    dve(lambda e: e.memset(imp[:, 1, 0:1], 1e4), [imp_b], [imp_b])
                    dve(lambda e: e.max(out=m8[:, 0, :], in_=imp[:, 1, :]), [imp_b], [m8_b])
                    dve(lambda e: e.match_replace(out=imp[:, 2, :], in_to_replace=m8[:, 0, :], in_values=imp[:, 1, :], imm_value=-3e4), [imp_b, m8_b], [imp_b])
                    dve(lambda e: e.max(out=m8[:, 1, :], in_=imp[:, 2, :]), [imp_b], [m8_b])
                    dve(lambda e: e.tensor_scalar(out=imp[:, 3, :], in0=imp[:, 1, :], scalar1=m8[:, 1, 7:8], scalar2=None, op0=ALU.is_ge), [imp_b, m8_b], [imp_b])
                    dve(lambda e: e.tensor_tensor(out=imp[:, 3, :], in0=imp[:, 3, :], in1=EXT[:, 2, c0:c0 + 32], op=ALU.mult), [imp_b, cb], [imp_b])
                    dve(lambda e: e.tensor_scalar(out=selq[:], in0=imp[:, 3, :], scalar1=-1.0, scalar2=-NEG, op0=ALU.add, op1=ALU.mult), [imp_b], [selq_b])
                    b2, b2_b = pp.get()
                    pe(lambda e: e.transpose(out=b2[0:32, 0:128], in_=selq[:], identity=IDF[:]), [selq_b, cb], [b2_b])
                    act(lambda e: e.copy(out=selbT[0:32, g, tok], in_=b2[0:32, 0:128]), [b2_b], [selbT_b])
                    dve(lambda e: e.tensor_copy(out=Pnb[:], in_=Pn[:].rearrange("p a b -> p (a b)")), [Pn_b], [Pnb_b])
                    b3, b3_b = pp.get()
                    b3h = b3[:].bitcast(BF16)
                    pe(lambda e: e.transpose(out=b3h[:, 0:128], in_=Pnb[:], identity=IDB[:]), [Pnb_b, cb], [b3_b])
                    act(lambda e: e.copy(out=PTc[:], in_=b3h[:, 0:128]), [b3_b], [PTc_b])
                    b4, b4_b = pp.get()
                    pe(lambda e: e.matmul(b4[:, 0:256], lhsT=PTc[:], rhs=Vbd[:, g, :], start=True, stop=True), [PTc_b, Vbd_b], [b4_b])
                    dve(lambda e: e.tensor_tensor(out=o_acc[:, ti, g * 256:(g + 1) * 256].rearrange("p (a b) -> p a b", b=64),
                                                  in0=b4[:, 0:256].rearrange("p (a b) -> p a b", b=64),
                                                  in1=gate_all[:, ti, 4 * g:4 * g + 4].unsqueeze(2).to_broadcast([128, 4, 64]), op=ALU.mult),
                        [b4_b, gate_b], [oacc_b[ti]])
            if S_BARRIERS:
                k.barrier()
            for qt in range(S4_QT if NSA_PARTS >= 4 else 0):
                q0 = qt * 512
                for h in range(S4_H):
                    g = h // 4
                    accs = [acc_banks.get(), acc_banks.get()]
                    first = [True, True]
                    for kt in range(0, 4 * qt + 4):
                        for br in range(2):
                            if not (S4_BR >> br) & 1:
                                continue
                            jbs = []
                            for jb in range(4):
                                rho = q0 + 128 * jb - 128 * kt
                                if rho < 0 or (br == 1 and rho > 512):
                                    continue
                                jbs.append((jb, rho))
                            if not jbs:
                                continue
                            jlo, jhi = jbs[0][0] * 128, jbs[-1][0] * 128 + 128
                            b, b_b = pp.get()
                            qs = slice(q0 + jlo, q0 + jhi)
                            pe(lambda e: e.matmul(b[:, jlo:jhi], lhsT=kT[:, br, g, kt * 128:(kt + 1) * 128], rhs=qT[:, h, qs], start=True, stop=(br == 1)),
                               [kT_b, qT_b], [b_b])
                            if br == 0 and S4_STEP >= 2:
                                pe(lambda e: e.matmul(b[:, jlo:jhi], lhsT=Eb[:, kt, :], rhs=selbT[:, g, qs], start=False, stop=True), [selbT_b, cb], [b_b])
                            pi = cnt["PT"] % 3
                            cnt["PT"] += 1
                            near = [(jb, rho) for jb, rho in jbs if rho in (0, 128) or (br == 1 and rho == 512)]
                            for ni, (jb, rho) in enumerate(near):
                                ri = {0: 0, 128: 1, 512: 2}[rho]
                                pe(lambda e: e.matmul(b[:, jb * 128:(jb + 1) * 128], lhsT=IDB[:], rhs=TBb[:, h, ri, :], start=False, stop=True, skip_group_check=True),
                                   [cb], [b_b])
                            act(lambda e: e.activation(out=PT[pi][:, jlo:jhi], in_=b[:, jlo:jhi], func=AF.Exp), [b_b], [PT_b[pi]])
                            ab, ab_b = accs[br]
                            for jb, rho in (jbs if S4_STEP >= 4 else []):
                                pe(lambda e: e.matmul(ab[:, jb * 65:(jb + 1) * 65], lhsT=PT[pi][:, jb * 128:(jb + 1) * 128], rhs=vaug[:, br, kt, g, 0:65],
                                                      start=(first[br] or bool(PV_PLAIN)), stop=True, skip_group_check=True), [PT_b[pi], vaug_b], [ab_b])
                                first[br] = False
                    for br in range(2):
                        if not ((S4_BR >> br) & 1) or not S4_NORM:
                            continue
                        ab, ab_b = accs[br]
                        a3 = ab[:, 0:260].rearrange("p (a b) -> p a b", b=65)
                        dve(lambda e: e.tensor_scalar(out=st4[:, 2 + br, :], in0=a3[:, :, 64], scalar1=1e-30, scalar2=None, op0=ALU.max), [ab_b], [st4_b])
                        dve(lambda e: e.reciprocal(out=st4[:, 2 + br, :], in_=st4[:, 2 + br, :]), [st4_b], [st4_b])
                        dve(lambda e: e.tensor_tensor(out=st4[:, 2 + br, :], in0=st4[:, 2 + br, :], in1=gate_all[:, 4 * qt:4 * qt + 4, 8 * (1 + br) + h], op=ALU.mult),
                            [st4_b, gate_b], [st4_b])
                        dve(lambda e: e.tensor_tensor(out=onr[:, br, :, :], in0=a3[:, :, 0:64], in1=st4[:, 2 + br, :].unsqueeze(2).to_broadcast([128, 4, 64]), op=ALU.mult),
                            [ab_b, st4_b], [onr_b])
                        oa = o_acc[:, 4 * qt:4 * qt + 4, h * 64:(h + 1) * 64]
                        pool(lambda e: e.tensor_tensor(out=oa, in0=oa, in1=onr[:, br, :, :], op=ALU.add), [onr_b] + oacc_b[4 * qt:4 * qt + 4], oacc_b[4 * qt:4 * qt + 4])
            k.dma("sp", O[base:base + S, 0:512].rearrange("(t p) c -> p t c", p=128), o_acc[:], reads=oacc_b)
        k.barrier()


def nsa_sample_host_consts():
    f = np.float32
    p = np.arange(128)
    ohs = np.zeros((33, 4, 4, 128), f)
    for t in range(4):
        dists = [8193 + t - 64 * (p + 1), 128 + t - p, np.where(p < 4, t - p, -1), np.where(p >= t, 512 + t - p, -1)]
        for ti, d in enumerate(dists):
            ok = d >= 0
            ohs[t5_bucket_np(d)[ok], ti, t, p[ok]] = 1
            ohs[32, ti, t, ~ok] = NEG
    E2 = np.zeros((128, 64, 128), f)
    for kt in range(64):
        E2[2 * kt, kt, :64] = 1
        E2[2 * kt + 1, kt, 64:] = 1
    return {"nss_ohs": ohs, "nss_E2": E2}


def nsa_sample_phase(k, nc, W, Z, O, cin, pools, page_table, cache_win, PAST, OBR):
    ZS0 = PB * S
    with contextlib.ExitStack() as es3:
        sbt = lambda n, s, d=F32: es3.enter_context(nc.sbuf_tensor(k.uniq(n), list(s), d))
        pp = PsumPool(k, nc, es3, 5)
        accp = PsumPool(k, nc, es3, 3)
        act = lambda fn, r, w: k.op("act", fn, reads=r, writes=w)
        dve = lambda fn, r, w: k.op("dve", fn, reads=r, writes=w)
        pe = lambda fn, r, w: k.op("pe", fn, reads=r, writes=w)
        cb = k.buf("nssconst")
        IDF = sbt("IDF", [128, 128])
        k.dma("sp", IDF[:], cin["ident"], writes=[cb])
        IDB = sbt("IDB", [128, 128], BF16)
        dve(lambda e: e.tensor_copy(out=IDB[:], in_=IDF[:]), [cb], [cb])
        ONESF = sbt("ONESF", [128, 128])
        dve(lambda e: e.memset(ONESF[:], 1.0), [], [cb])
        tab33 = sbt("tab33", [33, 8])
        k.dma("sp", tab33[0:32, :], W["rel_bias_table"], writes=[cb])
        dve(lambda e: e.memset(tab33[32:33, :], 1.0), [], [cb])
        OHS = sbt("OHS", [33, 4, 4, 128])
        k.dma("sp", OHS[:], cin["nss_ohs"], writes=[cb])
        OH31 = sbt("OH31", [33, 128])
        dve(lambda e: e.memset(OH31[:], 0.0), [], [cb])
        k.barrier()
        dve(lambda e: e.memset(OH31[0:32, :], 0.0), [], [cb])
        k.dma("sp", OH31[0:33, :], cin["nsa_ohc"][:, 6, :], writes=[cb])
        k.barrier()
        b, b_b = pp.get()
        pe(lambda e: e.matmul(b[:, 0:8], lhsT=OH31[:, :], rhs=tab33[:, :], start=True, stop=True), [cb], [b_b])
        tb31 = sbt("tb31s", [128, 8])
        dve(lambda e: e.tensor_copy(out=tb31[:], in_=b[:, 0:8]), [b_b], [cb])
        BT = sbt("BT", [128, 4, 32], BF16)
        BTf = sbt("BTf", [128, 4, 32])
        for ti in range(4):
            b, b_b = pp.get()
            for t in range(4):
                pe(lambda e: e.matmul(b[:, t * 8:(t + 1) * 8], lhsT=OHS[:, ti, t, :], rhs=tab33[:, :], start=True, stop=True), [cb], [b_b])
            dve(lambda e: e.tensor_tensor(out=BTf[:, ti, :].rearrange("p (h t) -> p t h", t=4), in0=b[:, 0:32].rearrange("p (t h) -> p t h", h=8),
                                          in1=tb31[:, :].unsqueeze(1).to_broadcast([128, 4, 8]), op=ALU.subtract), [b_b, cb], [cb])
        dve(lambda e: e.tensor_copy(out=BT[:].rearrange("p a b -> p (a b)"), in_=BTf[:].rearrange("p a b -> p (a b)")), [cb], [cb])
        E2 = sbt("E2", [128, 64, 128], BF16)
        with contextlib.ExitStack() as est:
            E2f = est.enter_context(nc.sbuf_tensor(k.uniq("E2f"), [128, 2048], F32))
            for q4 in range(4):
                k.dma("sp", E2f[:], cin["nss_E2"][:, q4 * 16:(q4 + 1) * 16, :].rearrange("p a b -> p (a b)"), writes=[cb])
                k.barrier()
                dve(lambda e: e.tensor_copy(out=E2[:, q4 * 16:(q4 + 1) * 16, :].rearrange("p a b -> p (a b)"), in_=E2f[:]), [cb], [cb])
                k.barrier()
        W1s = sbt("W1s", [64, 64, 64])
        W1_b = k.buf("W1")
        PE_ = [sbt("pek", [64, 64]), sbt("pev", [64, 64])]
        W2 = [sbt("W2k", [64, 64]), sbt("W2v", [64, 64])]
        for x, sfx in enumerate(("k", "v")):
            k.dma("sp", PE_[x][:], W["cmp_pe_" + sfx], writes=[cb])
            k.dma("sp", W2[x][:], W["cmp_w2_" + sfx], writes=[cb])
        PE2 = sbt("PE2", [64, 2, 64, 2])
        hidpe = sbt("hidpe", [64, 2])
        k.barrier()
        for x in range(2):
            dve(lambda e: e.tensor_copy(out=PE2[:, x, :, :], in_=PE_[x][:, :].unsqueeze(2).to_broadcast([64, 64, 2])), [cb], [cb])
            k.dma("sp", W1s[:], W["cmp_w1_" + "kv"[x]], writes=[W1_b])
            b, b_b = pp.get()
            for d in range(64):
                pe(lambda e: e.matmul(b[0:64, 0:2], lhsT=W1s[:, d, :], rhs=PE2[:, x, d, :], start=(d == 0), stop=(d == 63)), [cb, W1_b], [b_b])
            dve(lambda e: e.tensor_copy(out=hidpe[:, x:x + 1], in_=b[0:64, 0:1]), [b_b], [cb])
        pti = sbt("pti", [64, 16], I32)
        with nc.allow_non_contiguous_dma(reason="page table transpose"):
            k.dma("sp", pti[:], page_table.rearrange("b p -> p b"), writes=[cb])
        k.barrier()
        ptf = sbt("ptf", [64, 2, 16])
        idx = sbt("idx", [64, 2, 16], I32)
        dve(lambda e: e.tensor_copy(out=ptf[:, 0, :], in_=pti[:]), [cb], [cb])
        dve(lambda e: e.tensor_scalar(out=ptf[:, 1, :], in0=ptf[:, 0, :], scalar1=2.0, scalar2=1.0, op0=ALU.mult, op1=ALU.add), [cb], [cb])
        dve(lambda e: e.tensor_scalar(out=ptf[:, 0, :], in0=ptf[:, 0, :], scalar1=2.0, scalar2=None, op0=ALU.mult), [cb], [cb])
        dve(lambda e: e.tensor_copy(out=idx[:].rearrange("p a b -> p (a b)"), in_=ptf[:].rearrange("p a b -> p (a b)")), [cb], [cb])
        zs = sbt("zs", [64, NSA_COLS])
        k.dma("sp", zs[:], Z[ZS0:ZS0 + 64, 0:NSA_COLS], writes=[cb])
        k.barrier()
        qTs = sbt("qTs", [64, 8, 64], BF16)
        b, b_b = pp.get()
        for h in range(8):
            pe(lambda e: e.transpose(out=b[0:64, h * 64:(h + 1) * 64], in_=zs[:, h * 64:(h + 1) * 64], identity=IDF[0:64, 0:64]), [cb], [b_b])
        act(lambda e: e.activation(out=qTs[:].rearrange("p a b -> p (a b)"), in_=b[0:64, :], func=AF.Copy, scale=0.125), [b_b], [cb])
        kTn = sbt("kTn", [64, 2, 2, 64], BF16)
        b, b_b = pp.get()
        for ci, c0 in enumerate((768, 832, 1024, 1088)):
            pe(lambda e: e.transpose(out=b[0:64, ci * 64:(ci + 1) * 64], in_=zs[:, c0:c0 + 64], identity=IDF[0:64, 0:64]), [cb], [b_b])
        dve(lambda e: e.tensor_copy(out=kTn[:].rearrange("p a g b -> p (a g b)"), in_=b[0:64, 0:256]), [b_b], [cb])
        Vnf = sbt("Vnf", [4, 2, 16, 128])
        for a, c0 in enumerate((896, 1152)):
            k.dma("sp", Vnf[:, a, :, :], Z[ZS0:ZS0 + 64, c0:c0 + 128].rearrange("(t b) c -> t b c", b=16), writes=[cb])
        Vn = sbt("Vn", [4, 2, 16, 2, 66], BF16)
        dve(lambda e: e.memset(Vn[:].rearrange("p a b c d -> p (a b c d)"), 1.0), [], [cb])
        k.barrier()
        dve(lambda e: e.tensor_copy(out=Vn[:, :, :, :, 0:64].rearrange("p a b g d -> p (a b) g d"), in_=Vnf[:].rearrange("p a b (g d) -> p (a b) g d", d=64)), [cb], [cb])
        gates = sbt("gates", [64, 24])
        act(lambda e: e.activation(out=gates[:], in_=zs[:, 1280:1304], func=AF.Sigmoid), [cb], [cb])
        k.barrier()

        Pg = sbt("Pg", [64, 8192])
        Pg_b = k.buf("Pg")
        past_b = [k.buf("past0"), k.buf("past1")]
        Gs = sbt("Gs", [64, 2, 128])
        Gs_b = k.buf("Gs")
        kccT = sbt("kccT", [64, 2, 128], BF16)
        kcc_b = k.buf("kcc")
        vcc = sbt("vcc", [128, 2, 66], BF16)
        vcc_b = k.buf("vcc")
        dve(lambda e: e.memset(vcc[:].rearrange("p a b -> p (a b)"), 1.0), [], [vcc_b])
        rows = sbt("rows", [128, 64, 128])
        rows_b = k.buf("rows")
        Rst = rows[0:64, :, :]
        Rst_b = rows_b
        kTs = sbt("kTs", [64, 2, 8192], BF16)
        kTs_b = k.buf("kTs")
        vas = sbt("vas", [128, 64, 2, 66], BF16)
        vas_b = k.buf("vas")
        dve(lambda e: e.memset(vas[:].rearrange("p a b c -> p (a b c)"), 1.0), [], [vas_b])
        Ef = sbt("Ef", [128, 32])
        Ef_b = k.buf("Ef")
        PTs = [sbt(f"PTs{i}", [128, 32], BF16) for i in range(3)]
        PTs_b = [k.buf() for _ in range(3)]
        impT = sbt("impT", [128, 16, 8])
        impT_b = k.buf("impT")
        selT = sbt("selT", [128, 16, 2, 4, 4], BF16)
        selT_b = k.buf("selT")
        osb = sbt("osb", [16, 3, 2, 64])
        osb_b = [k.buf("osb") for _ in range(3)]
        obr_b = k.buf("obr")
        st = sbt("st", [128, 8])
        st_b = k.buf("st")
        cnt = {"pt": 0, "past": 0}

        def gather_past(pool_ap, bq):
            slot = cnt["past"] % 2
            cnt["past"] += 1
            dst = PAST[slot].rearrange("(pg h c) x -> pg h (c x)", h=2, c=64)
            for hh in range(2):
                k.dma("pool", Pg[:, :], pool_ap, indirect=bass.IndirectOffsetOnAxis(ap=idx[:, hh, bq:bq + 1], axis=0), reads=[cb], writes=[Pg_b])
                k.dma("sp", dst[:, hh, :], Pg[:, :], reads=[Pg_b], writes=[past_b[slot]])
            return slot

        def q_cols(bq, g):
            return qTs[:, 4 * g:4 * g + 4, bq:64:16]

        def attend_tile(bq, nk, kT_of_g, v_of_g, bias_ti, sel_kt, acc, first):
            ps, ps_b = pp.get()
            for g in range(2):
                pe(lambda e: e.matmul(ps[:nk, g * 16:(g + 1) * 16], lhsT=kT_of_g(g), rhs=q_cols(bq, g), start=(g == 0), stop=True, skip_group_check=True),
                   [kTs_b, kcc_b, cb], [ps_b])
            if bias_ti is not None:
                pe(lambda e: e.matmul(ps[:nk, 0:32], lhsT=IDB[:nk, :nk], rhs=BT[:nk, bias_ti, :], start=False, stop=True, skip_group_check=True), [cb], [ps_b])
            if sel_kt is not None:
                pe(lambda e: e.matmul(ps[:nk, 0:32], lhsT=E2[:, sel_kt, :], rhs=selT[:, bq, :, :, :].rearrange("p g r t -> p (g r t)"), start=False, stop=True,
                                      skip_group_check=True), [selT_b, cb], [ps_b])
            pi = cnt["pt"] % 3
            cnt["pt"] += 1
            act(lambda e: e.activation(out=PTs[pi][:nk, :], in_=ps[:nk, 0:32], func=AF.Exp), [ps_b], [PTs_b[pi]])
            ab, ab_b = acc
            for g in range(2):
                pe(lambda e: e.matmul(ab[0:16, g * 65:(g + 1) * 65], lhsT=PTs[pi][:nk, g * 16:(g + 1) * 16], rhs=v_of_g(g), start=(first and g == 0), stop=True,
                                      skip_group_check=True), [PTs_b[pi], vas_b, vcc_b, cb], [ab_b])
            return ps, ps_b, pi

        def finish_branch(bq, br, acc):
            ab, ab_b = acc
            a3 = ab[0:16, 0:130].rearrange("p (g c) -> p g c", c=65)
            dve(lambda e: e.tensor_scalar(out=st[0:16, 0:2], in0=a3[:, :, 64], scalar1=1e-30, scalar2=None, op0=ALU.max), [ab_b], [st_b])
            dve(lambda e: e.reciprocal(out=st[0:16, 0:2], in_=st[0:16, 0:2]), [st_b], [st_b])
            dve(lambda e: e.tensor_tensor(out=osb[:, br, :, :], in0=a3[:, :, 0:64], in1=st[0:16, 0:2].unsqueeze(2).to_broadcast([16, 2, 64]), op=ALU.mult),
                [ab_b, st_b], [osb_b[br]])
            for g in range(2):
                for r in range(4):
                    dst = OBR[br].rearrange("(t b) (h d) -> t b h d", b=16, d=64)[:, bq, 4 * g + r, :]
                    k.dma("sp", dst, osb[r * 4:(r + 1) * 4, br, g, :], reads=[osb_b[br]], writes=[obr_b])

        def compress(slot, x):
            k.dma("sp", W1s[:], W["cmp_w1_" + "kv"[x]], writes=[W1_b])
            ph, ph_b = pp.get()
            for half in range(2):
                k.dma("sp", Rst, PAST[slot][half * 4096:(half + 1) * 4096, :].rearrange("(b c) x -> c b x", c=64), reads=[past_b[slot]], writes=[Rst_b])
                Rv = Rst.rearrange("c b (g d) -> c g b d", d=64)
                for g in range(2):
                    for d in range(64):
                        pe(lambda e: e.matmul(ph[0:64, g * 128 + half * 64:g * 128 + half * 64 + 64], lhsT=W1s[:, d, :], rhs=Rv[:, g, :, d],
                                              start=(d == 0 and half == 0 and g == 0), stop=True, skip_group_check=True) if False else
                           e.matmul(ph[0:64, g * 128 + half * 64:g * 128 + half * 64 + 64], lhsT=W1s[:, d, :], rhs=Rv[:, g, :, d], start=(d == 0), stop=(d == 63)),
                           [Rst_b, W1_b], [ph_b])
            act(lambda e: e.activation(out=Gs[:].rearrange("p a b -> p (a b)"), in_=ph[0:64, 0:256], func=AF.Gelu_apprx_tanh, bias=hidpe[:, x:x + 1]), [ph_b, cb], [Gs_b])
            p2, p2_b = pp.get()
            if x == 0:
                pe(lambda e: e.matmul(p2[0:64, 0:256], lhsT=W2[0][:], rhs=Gs[:].rearrange("p a b -> p (a b)"), start=True, stop=True), [Gs_b, cb], [p2_b])
                dve(lambda e: e.tensor_copy(out=kccT[:].rearrange("p a b -> p (a b)"), in_=p2[0:64, 0:256]), [p2_b], [kcc_b])
            else:
                for g in range(2):
                    pe(lambda e: e.matmul(p2[:, g * 64:(g + 1) * 64], lhsT=Gs[:, g, :], rhs=W2[1][:], start=True, stop=True), [Gs_b, cb], [p2_b])
                dve(lambda e: e.tensor_copy(out=vcc[:, :, 0:64], in_=p2[:, 0:128].rearrange("p (g d) -> p g d", d=64)), [p2_b], [vcc_b])

        for bq in range(SBT):
            for x in range(2):
                slot = gather_past(pools[x], bq)
                compress(slot, x)
            acc = accp.get()
            ps, ps_b, pi = attend_tile(bq, 128, lambda g: kccT[:, g, :], lambda g: vcc[:, g, 0:65], 0, None, acc, True)
            finish_branch(bq, 0, acc)
            act(lambda e: e.activation(out=Ef[:], in_=ps[:, 0:32], func=AF.Exp), [ps_b], [Ef_b])
            s2, s2_b = pp.get()
            pe(lambda e: e.matmul(s2[:, 0:32], lhsT=ONESF[:], rhs=Ef[:], start=True, stop=True), [Ef_b, cb], [s2_b])
            dve(lambda e: e.tensor_scalar(out=st[:, 0:0 + 8], in0=s2[:, 0:8], scalar1=1.0, scalar2=None, op0=ALU.mult), [s2_b], [st_b]) if False else None
            dve(lambda e: e.reciprocal(out=Ef[:, :], in_=Ef[:, :]) if False else e.tensor_tensor(out=Ef[:], in0=Ef[:], in1=s2[:, 0:32], op=ALU.divide), [Ef_b, s2_b], [Ef_b]) if False else None
            rc = sbt(f"rc{bq}", [128, 32])
            rc_b = k.buf()
            dve(lambda e: e.reciprocal(out=rc[:], in_=s2[:, 0:32]), [s2_b], [rc_b])
            dve(lambda e: e.tensor_tensor(out=Ef[:], in0=Ef[:], in1=rc[:], op=ALU.mult), [Ef_b, rc_b], [Ef_b])
            dve(lambda e: e.tensor_reduce(out=impT[:, bq, :].rearrange("p (g t) -> p g t", t=4), in_=Ef[:].rearrange("p (g r t) -> p g t r", g=2, r=4),
                                          axis=AX.X, op=ALU.add), [Ef_b], [impT_b])
        b, b_b = pp.get()
        pe(lambda e: e.transpose(out=b[:, 0:128], in_=impT[:].rearrange("p a b -> p (a b)"), identity=IDF[:]), [impT_b, cb], [b_b])
        imp = sbt("imp", [128, 4, 128])
        imp_b = k.buf("imp")
        m8 = sbt("m8", [128, 2, 8])
        dve(lambda e: e.tensor_copy(out=imp[:, 0, :], in_=b[:, 0:128]), [b_b], [imp_b])
        dve(lambda e: e.memset(imp[:, 0, 0:1], -1.0), [imp_b], [imp_b])
        dve(lambda e: e.memset(imp[:, 0, 127:128], -1.0), [imp_b], [imp_b])
        dve(lambda e: e.max(out=m8[:, 0, :], in_=imp[:, 0, :]), [imp_b], [imp_b])
        dve(lambda e: e.match_replace(out=imp[:, 1, :], in_to_replace=m8[:, 0, :], in_values=imp[:, 0, :], imm_value=-3.0), [imp_b], [imp_b])
        dve(lambda e: e.max(out=m8[:, 1, :], in_=imp[:, 1, :]), [imp_b], [imp_b])
        dve(lambda e: e.tensor_scalar(out=imp[:, 2, :], in0=imp[:, 0, :], scalar1=m8[:, 1, 4:5], scalar2=None, op0=ALU.is_ge), [imp_b], [imp_b])
        dve(lambda e: e.memset(imp[:, 2, 0:1], 1.0), [imp_b], [imp_b])
        dve(lambda e: e.memset(imp[:, 2, 127:128], 1.0), [imp_b], [imp_b])
        dve(lambda e: e.tensor_scalar(out=imp[:, 3, :], in0=imp[:, 2, :], scalar1=-1.0, scalar2=-NEG, op0=ALU.add, op1=ALU.mult), [imp_b], [imp_b])
        b, b_b = pp.get()
        pe(lambda e: e.transpose(out=b[:, 0:128], in_=imp[:, 3, :], identity=IDF[:]), [imp_b, cb], [b_b])
        dve(lambda e: e.tensor_copy(out=selT[:].rearrange("p b g r t -> p (b g) r t"),
                                    in_=b[:, 0:128].rearrange("p (a t) -> p a t", t=4).unsqueeze(2).to_broadcast([128, 32, 4, 4])), [b_b], [selT_b])
        for bq in range(SBT):
            for br, (pk, pv) in ((1, (pools[2], pools[3])),):
                slot = gather_past(pk, bq)
                k.dma("sp", rows[:], PAST[slot].rearrange("(pg r) x -> r pg x", r=128), reads=[past_b[slot]], writes=[rows_b])
                for pg4 in range(32):
                    b, b_b = pp.get()
                    for j in range(2):
                        for g in range(2):
                            pe(lambda e: e.transpose(out=b[0:64, (j * 2 + g) * 128:(j * 2 + g + 1) * 128], in_=rows[:, pg4 * 2 + j, g * 64:(g + 1) * 64], identity=IDF[:]),
                               [rows_b, cb], [b_b])
                    eng = act if pg4 % 2 == 0 else dve
                    if pg4 % 2 == 0:
                        act(lambda e: e.copy(out=kTs[:, :, pg4 * 256:(pg4 + 1) * 256].rearrange("p g (j k) -> p j g k", j=2), in_=b[0:64, :].rearrange("p (j g k) -> p j g k", j=2, g=2)),
                            [b_b], [kTs_b])
                    else:
                        dve(lambda e: e.tensor_copy(out=kTs[:, :, pg4 * 256:(pg4 + 1) * 256].rearrange("p g (j k) -> p j g k", j=2), in_=b[0:64, :].rearrange("p (j g k) -> p j g k", j=2, g=2)),
                            [b_b], [kTs_b])
                slot = gather_past(pv, bq)
                k.dma("sp", rows[:], PAST[slot].rearrange("(pg r) x -> r pg x", r=128), reads=[past_b[slot]], writes=[rows_b])
                for q4 in range(4):
                    dve(lambda e: e.tensor_copy(out=vas[:, q4 * 16:(q4 + 1) * 16, :, 0:64], in_=rows[:, q4 * 16:(q4 + 1) * 16, :].rearrange("p a (g d) -> p a g d", d=64)),
                        [rows_b], [vas_b])
                acc = accp.get()
                for kt in range(64):
                    attend_tile(bq, 128, lambda g: kTs[:, g, kt * 128:(kt + 1) * 128], lambda g: vas[:, kt, g, 0:65], 1 if kt == 63 else None, kt, acc, kt == 0)
                attend_tile(bq, 4, lambda g: kTn[:, 0, g, bq:64:16], lambda g: Vn[:, 0, bq, g, 0:65], 2, None, acc, False)
                finish_branch(bq, 1, acc)
            k.dma("sp", rows[:, 0:4, :], cache_win[0][bq].rearrange("(a r) x -> r a x", r=128), writes=[rows_b])
            b, b_b = pp.get()
            b2, b2_b = pp.get()
            for a in range(4):
                for g in range(2):
                    tgt = b if a < 2 else b2
                    pe(lambda e: e.transpose(out=tgt[0:64, ((a % 2) * 2 + g) * 128:((a % 2) * 2 + g + 1) * 128], in_=rows[:, a, g * 64:(g + 1) * 64], identity=IDF[:]),
                       [rows_b, cb], [b_b if a < 2 else b2_b])
            act(lambda e: e.copy(out=kTs[:, :, 0:256].rearrange("p g (j k) -> p j g k", j=2), in_=b[0:64, :].rearrange("p (j g k) -> p j g k", j=2, g=2)), [b_b], [kTs_b])
            act(lambda e: e.copy(out=kTs[:, :, 256:512].rearrange("p g (j k) -> p j g k", j=2), in_=b2[0:64, :].rearrange("p (j g k) -> p j g k", j=2, g=2)), [b2_b], [kTs_b])
            k.dma("sp", rows[:, 4:8, :], cache_win[1][bq].rearrange("(a r) x -> r a x", r=128), writes=[rows_b])
            dve(lambda e: e.tensor_copy(out=vas[:, 0:4, :, 0:64], in_=rows[:, 4:8, :].rearrange("p a (g d) -> p a g d", d=64)), [rows_b], [vas_b])
            acc = accp.get()
            for kt in range(4):
                attend_tile(bq, 128, lambda g: kTs[:, g, kt * 128:(kt + 1) * 128], lambda g: vas[:, kt, g, 0:65], {0: 3, 3: 1}.get(kt), None, acc, kt == 0)
            attend_tile(bq, 4, lambda g: kTn[:, 1, g, bq:64:16], lambda g: Vn[:, 1, bq, g, 0:65], 2, None, acc, False)
            finish_branch(bq, 2, acc)
        ob3 = rows[0:64, 0:12, :].rearrange("p (a c) x -> p a (c x)", a=3)
        ob3_b = rows_b
        k.dma("sp", ob3, OBR.rearrange("a t c -> t a c"), reads=[obr_b], writes=[ob3_b])
        for br in range(3):
            dve(lambda e: e.tensor_tensor(out=ob3[:, br, :].rearrange("p (h d) -> p h d", d=64), in0=ob3[:, br, :].rearrange("p (h d) -> p h d", d=64),
                                          in1=gates[:, 8 * br:8 * br + 8].unsqueeze(2).to_broadcast([64, 8, 64]), op=ALU.mult), [ob3_b, cb], [ob3_b])
        dve(lambda e: e.tensor_tensor(out=ob3[:, 0, :], in0=ob3[:, 0, :], in1=ob3[:, 1, :], op=ALU.add), [ob3_b], [ob3_b])
        dve(lambda e: e.tensor_tensor(out=ob3[:, 0, :], in0=ob3[:, 0, :], in1=ob3[:, 2, :], op=ALU.add), [ob3_b], [ob3_b])
        k.dma("sp", O[ZS0:ZS0 + 64, 0:512], ob3[:, 0, :], reads=[ob3_b])
        k.barrier()


def build_program():
    nc = bass.Bass("TRN2", target_bir_lowering=False)
    din = {}
    dout = {}

    def inp(name, shape, dt=F32):
        din[name] = nc.dram_tensor(name, shape, dt, kind="ExternalInput").ap()
        return din[name]

    def outp(name, shape):
        dout[name] = nc.dram_tensor(name, shape, F32, kind="ExternalOutput").ap()
        return dout[name]

    xp = inp("x_prompt", [PB * S, D])
    xsm = inp("x_sample", [NS_TOK, D])
    cache_win_k = inp("cache_win_k", [SBT, WIN, 128])
    cache_win_v = inp("cache_win_v", [SBT, WIN, 128])
    ident_in = inp("ident", [128, 128])
    W = {}
    for f in ("ffn1", "ffn2"):
        W[f + "_norm"] = inp(f + "_norm", [D])
        W[f + "_wg"] = inp(f + "_wg", [D, DFF])
        W[f + "_wu"] = inp(f + "_wu", [D, DFF])
        W[f + "_wd"] = inp(f + "_wd", [DFF, D])
    W["mix_norm"] = inp("mix_norm", [D])
    W["w_in"] = inp("w_in", [D, INC])
    W["final_norm"] = inp("final_norm", [D])
    W["w_out"] = inp("w_out", [D, D])
    for n, shp in (("shift_mu", [RWC]), ("decay_w0", [512]), ("decay_w2", [64, 512]), ("aaa_a0", [512]), ("aaa_a2", [64, 512]),
                   ("gate_g2", [128, 512]), ("k_k", [512]), ("k_a", [512]), ("r_k", [512]), ("ln_x_w", [512]), ("ln_x_b", [512])):
        W[n] = inp(n, shp)
    rwc_in = inp("rw_consts", [128, 10, 128])
    W["rel_bias_table"] = inp("rel_bias_table", [32, 8])
    for sfx in ("k", "v"):
        W["cmp_pe_" + sfx] = inp("cmp_pe_" + sfx, [64, 64])
        W["cmp_w1_" + sfx] = inp("cmp_w1_" + sfx, [64, 64, 64])
        W["cmp_w2_" + sfx] = inp("cmp_w2_" + sfx, [64, 64])
    cin = {"ident": ident_in}
    if NSA_SAMPLE:
        cin["nss_ohs"] = inp("nss_ohs", [33, 4, 4, 128])
        cin["nss_E2"] = inp("nss_E2", [128, 64, 128])
        pools_in = [inp(n, [20480, 8192]) for n in ("cache_cmp_k", "cache_cmp_v", "cache_slc_k", "cache_slc_v")]
        page_table_in = inp("page_table", [SBT, 64], I32)
        PAST = nc.dram_tensor("PAST", [2, 8192, 128], F32, kind="Internal").ap()
        OBR = nc.dram_tensor("OBR", [3, 64, 512], F32, kind="Internal").ap()
    for n, shp in (("nsa_ohg", [33, 768]), ("nsa_ohc", [33, 7, 128]), ("nsa_J", [128, 128]), ("nsa_E", [32, 16, 128]), ("nsa_ext", [128, 3, 62]), ("nsa_bm", [128, 4])):
        cin[n] = inp(n, shp)
    state_shift_in = inp("state_shift", [SBT, RWC])
    state_wkv_in = inp("state_wkv", [SBT, 8, 64, 64])

    y_prompt = outp("y_prompt", [PB * S, D])
    y_sample = outp("y_sample", [NS_TOK, D])
    p_kv = [outp(n, [PB * S, 128]) for n in ("p_cmp_k", "p_cmp_v", "p_slc_k", "p_slc_v")]
    p_win = [outp(n, [PB, WIN, 128]) for n in ("p_win_k", "p_win_v")]
    p_wkv = outp("p_wkv", [PB, 8, 64, 64])
    p_shift = outp("p_shift", [PB, RWC])
    s_kv = [outp(n, [NS_TOK, 128]) for n in ("s_cmp_k", "s_cmp_v", "s_slc_k", "s_slc_v")]
    s_win = [outp(n, [SBT, WIN, 128]) for n in ("s_win_k", "s_win_v")]
    s_wkv = outp("s_wkv", [SBT, 8, 64, 64])
    s_shift = outp("s_shift", [SBT, RWC])

    NTOK = PB * S + NS_TOK
    X1 = nc.dram_tensor("X1", [NTOK, D], F32, kind="Internal").ap()
    Z = nc.dram_tensor("Z", [NTOK, INC], F32, kind="Internal").ap()
    RS = nc.dram_tensor("RS", [7, 64, 512], F32, kind="Internal").ap()
    GD = nc.dram_tensor("GD", [8, 768], F32, kind="Internal").ap()
    if DEBUG:
        O = outp("dbg_O", [NTOK, D])
    else:
        O = nc.dram_tensor("O", [NTOK, D], F32, kind="Internal").ap()

    tiles = []
    for i in range(PB * S // 512):
        tiles.append((i * 512, 512, xp[i * 512:(i + 1) * 512, :]))
    tiles.append((PB * S, NS_TOK, xsm))

    with contextlib.ExitStack() as es:
        k = K(nc, es)
        ident_f = k.sb("ident_f", [128, 128], F32)
        ident = k.sb("ident", [128, 128], BF16)
        eps_t = k.sb("eps_t", [128, 1], F32)
        cbuf = k.buf("consts")
        k.dma("sp", ident_f[:], ident_in, writes=[cbuf])
        k.op("dve", lambda e: e.tensor_copy(out=ident[:], in_=ident_f[:]), reads=[cbuf], writes=[cbuf])
        k.op("dve", lambda e: e.memset(eps_t[:], EPS), writes=[cbuf])
        consts = {"eps": eps_t}
        gts = {}
        for n in ("ffn1_norm", "mix_norm", "ffn2_norm"):
            gts[n] = k.sb("gT_" + n, [128, 8], F32)
            with nc.allow_non_contiguous_dma(reason="tiny norm vector"):
                k.dma("sp", gts[n][:], W[n].rearrange("(c p) -> p c", p=128), writes=[cbuf])

        sq = k.sb("sq", [128, D], F32)
        ss = k.sb("ss", [128, 1], F32)
        rstd = k.sb("rstd", [128, 1], F32)
        xs = k.sb("xs", [128, D], BF16)
        xs_b, ss_b = k.buf("xs"), k.buf("ss")

        def ffn_phase(pfx, src_tiles, dst_rows, final=False):
            with contextlib.ExitStack() as es2:
                sb2 = lambda n, s, d: es2.enter_context(nc.sbuf_tensor(k.uniq(n), s, d))
                ps2 = lambda n, s, d: es2.enter_context(nc.psum_tensor(k.uniq(n), s, d))
                wg = sb2("wg", [128, 8, DFF], BF16)
                wu = sb2("wu", [128, 8, DFF], BF16)
                wd = sb2("wd", [128, NFC, D], BF16)
                wb = k.buf("ffn_w")
                for kc in range(8):
                    load_cast(k, wg[:, kc, :], W[pfx + "_wg"][kc * 128:(kc + 1) * 128, :], wb)
                    load_cast(k, wu[:, kc, :], W[pfx + "_wu"][kc * 128:(kc + 1) * 128, :], wb)
                for fc in range(NFC):
                    load_cast(k, wd[:, fc, :], W[pfx + "_wd"][fc * 128:(fc + 1) * 128, :], wb)
                hT = [sb2(f"hT{i}", [128, 8, 512], BF16) for i in range(2)]
                hT_b = [[k.buf() for _ in range(4)] for _ in range(2)]
                aT = sb2("aT", [128, NFC, 512], BF16)
                aT_b = [k.buf() for _ in range(NFC)]
                xa = [sb2(f"xa{i}", [128, D], F32) for i in range(2)]
                xa_b = [k.buf() for _ in range(2)]
                xr = [sb2(f"xr{i}", [128, D], F32) for i in range(2)]
                xr_b = [k.buf() for _ in range(2)]
                sg = [sb2(f"sg{i}", [128, 512], F32) for i in range(2)]
                sg_b = [k.buf() for _ in range(2)]
                pT = [ps2(f"pT{i}", [128, 8, 128], BF16) for i in range(2)]
                pT_b = [k.buf() for _ in range(2)]
                pg = [ps2(f"pg{i}", [128, 512], F32) for i in range(2)]
                pu = [ps2(f"pu{i}", [128, 512], F32) for i in range(2)]
                pgu_b = [k.buf() for _ in range(2)]
                po = [ps2(f"po{i}", [128, 512], F32) for i in range(2)]
                po_b = [k.buf() for _ in range(2)]
                cnt = {"xa": 0, "xr": 0, "gu": 0, "po": 0}
                if final:
                    gbc = sb2("gbc", [128, D], F32)
                    ss2 = sb2("ss2", [128, 2], F32)
                    fin_b = k.buf("fin")
                    k.dma("sp", gbc[:], W["final_norm"].partition_broadcast(128), writes=[wb])

                def norm_tile(ti):
                    row0, nt, src = src_tiles[ti]
                    hb = ti % 2
                    for st in range((nt + 127) // 128):
                        np_ = min(128, nt - st * 128)
                        i = cnt["xa"] % 2
                        cnt["xa"] += 1
                        k.dma("sp", xa[i][:np_, :], src[st * 128:st * 128 + np_, :], writes=[xa_b[i]])
                        rms_to_hT(k, (sq, ss, rstd, xs, xs_b, ss_b, pT[i], pT_b[i], ident), xa[i][:np_, :], xa_b[i], np_,
                                  gts[pfx + "_norm"], hT[hb], hT_b[hb][st], st * 128, consts)

                norm_tile(0)
                for ti in range(len(src_tiles)):
                    row0, nt, src = src_tiles[ti]
                    hb = ti % 2
                    nst = (nt + 127) // 128
                    hbufs = hT_b[hb][:nst]
                    for fc in range(NFC):
                        j = cnt["gu"] % 2
                        cnt["gu"] += 1
                        for kc in range(8):
                            k.op("pe", lambda e: e.matmul(pg[j][:, :nt], lhsT=wg[:, kc, fc * 128:(fc + 1) * 128], rhs=hT[hb][:, kc, :nt],
                                                          start=(kc == 0), stop=(kc == 7)), reads=hbufs + [wb], writes=[pgu_b[j]])
                        for kc in range(8):
                            k.op("pe", lambda e: e.matmul(pu[j][:, :nt], lhsT=wu[:, kc, fc * 128:(fc + 1) * 128], rhs=hT[hb][:, kc, :nt],
                                                          start=(kc == 0), stop=(kc == 7)), reads=hbufs + [wb], writes=[pgu_b[j]])
                        k.op("act", lambda e: e.activation(out=sg[j][:, :nt], in_=pg[j][:, :nt], func=AF.Silu),
                             reads=[pgu_b[j]], writes=[sg_b[j]])
                        k.op("dve", lambda e: e.tensor_tensor(out=aT[:, fc, :nt], in0=sg[j][:, :nt], in1=pu[j][:, :nt], op=ALU.mult),
                             reads=[sg_b[j], pgu_b[j]], writes=[aT_b[fc]])
                    if ti + 1 < len(src_tiles):
                        norm_tile(ti + 1)
                    for st in range(nst):
                        np_ = min(128, nt - st * 128)
                        i = cnt["xr"] % 2
                        cnt["xr"] += 1
                        k.dma("sp", xr[i][:np_, :], src[st * 128:st * 128 + np_, :], writes=[xr_b[i]])
                        for dh in range(2):
                            j = cnt["po"] % 2
                            cnt["po"] += 1
                            for fc in range(NFC):
                                k.op("pe", lambda e: e.matmul(po[j][:np_, :], lhsT=aT[:, fc, st * 128:st * 128 + np_],
                                                              rhs=wd[:, fc, dh * 512:(dh + 1) * 512], start=(fc == 0), stop=(fc == NFC - 1)),
                                     reads=[aT_b[fc], wb], writes=[po_b[j]])
                            k.op("dve", lambda e: e.scalar_tensor_tensor(out=xr[i][:np_, dh * 512:(dh + 1) * 512], in0=po[j][:np_, :], scalar=0.5,
                                                                         in1=xr[i][:np_, dh * 512:(dh + 1) * 512], op0=ALU.mult, op1=ALU.add),
                                 reads=[po_b[j], xr_b[i]], writes=[xr_b[i]])
                        if final:
                            k.op("act", lambda e: e.activation(out=sq[:np_, :], in_=xr[i][:np_, :], func=AF.Square, accum_out=ss2[:np_, 0:1]),
                                 reads=[xr_b[i]], writes=[fin_b])
                            k.op("act", lambda e: e.activation(out=ss2[:np_, 1:2], in_=ss2[:np_, 0:1], func=AF.Sqrt, scale=1.0 / D, bias=consts["eps"][:np_, :]),
                                 reads=[fin_b], writes=[fin_b])
                            k.op("dve", lambda e: e.reciprocal(out=ss2[:np_, 1:2], in_=ss2[:np_, 1:2]), reads=[fin_b], writes=[fin_b])
                            k.op("dve", lambda e: e.scalar_tensor_tensor(out=xr[i][:np_, :], in0=xr[i][:np_, :], scalar=ss2[:np_, 1:2], in1=gbc[:np_, :],
                                                                         op0=ALU.mult, op1=ALU.mult), reads=[xr_b[i], fin_b, wb], writes=[xr_b[i]])
                        k.dma("sp", dst_rows(row0 + st * 128, np_), xr[i][:np_, :], reads=[xr_b[i]])
                k.barrier()

        ffn_phase("ffn1", tiles, lambda r0, n: X1[r0:r0 + n, :])

        x1_tiles = [(r0, nt, X1[r0:r0 + nt, :]) for (r0, nt, _) in tiles]
        with contextlib.ExitStack() as es2:
            sb2 = lambda n, s, d: es2.enter_context(nc.sbuf_tensor(k.uniq(n), s, d))
            ps2 = lambda n, s, d: es2.enter_context(nc.psum_tensor(k.uniq(n), s, d))
            win = sb2("win", [128, 8, INC], BF16)
            winb = k.buf("win")
            for kc in range(8):
                load_cast(k, win[:, kc, :], W["w_in"][kc * 128:(kc + 1) * 128, :], winb)
            hT = [sb2(f"hT{i}", [128, 8, 128], BF16) for i in range(2)]
            hT_b = [k.buf() for _ in range(2)]
            xa = [sb2(f"xa{i}", [128, D], F32) for i in range(2)]
            xa_b = [k.buf() for _ in range(2)]
            zt = [sb2(f"zt{i}", [128, INC], F32) for i in range(2)]
            zt_b = [k.buf() for _ in range(2)]
            pT = [ps2(f"pT{i}", [128, 8, 128], BF16) for i in range(2)]
            pT_b = [k.buf() for _ in range(2)]
            pz = [ps2(f"pz{i}", [128, 512], F32) for i in range(4)]
            pz_b = [k.buf() for _ in range(4)]
            n_sub = 0
            n_pz = 0
            cgroups = [(c0, min(512, INC - c0)) for c0 in range(0, INC, 512)]
            for (row0, nt, src) in x1_tiles:
                for st in range((nt + 127) // 128):
                    np_ = min(128, nt - st * 128)
                    i = n_sub % 2
                    n_sub += 1
                    r0 = row0 + st * 128
                    k.dma("sp", xa[i][:np_, :], src[st * 128:st * 128 + np_, :], writes=[xa_b[i]])
                    rms_to_hT(k, (sq, ss, rstd, xs, xs_b, ss_b, pT[i], pT_b[i], ident), xa[i][:np_, :], xa_b[i], np_,
                              gts["mix_norm"], hT[i], hT_b[i], 0, consts)
                    for gi, (c0, cw) in enumerate(cgroups):
                        j = n_pz % 4
                        n_pz += 1
                        for kc in range(8):
                            k.op("pe", lambda e: e.matmul(pz[j][:np_, :cw], lhsT=hT[i][:, kc, :np_], rhs=win[:, kc, c0:c0 + cw],
                                                          start=(kc == 0), stop=(kc == 7)), reads=[hT_b[i], winb], writes=[pz_b[j]])
                        eng = "act" if gi % 2 == 0 else "dve"
                        if eng == "act":
                            k.op("act", lambda e: e.copy(out=zt[i][:np_, c0:c0 + cw], in_=pz[j][:np_, :cw]), reads=[pz_b[j]], writes=[zt_b[i]])
                        else:
                            k.op("dve", lambda e: e.tensor_copy(out=zt[i][:np_, c0:c0 + cw], in_=pz[j][:np_, :cw]), reads=[pz_b[j]], writes=[zt_b[i]])
                    k.dma("sp", Z[r0:r0 + np_, :], zt[i][:np_, :], reads=[zt_b[i]])
                    if r0 < PB * S:
                        for oi in range(4):
                            k.dma("sp", p_kv[oi][r0:r0 + np_, :], zt[i][:np_, 512 + 128 * oi:640 + 128 * oi], reads=[zt_b[i]])
                        seq, pos = divmod(r0, S)
                        if pos >= S - WIN:
                            for oi in range(2):
                                k.dma("sp", p_win[oi][seq, pos - (S - WIN):pos - (S - WIN) + np_, :],
                                      zt[i][:np_, 1024 + 128 * oi:1152 + 128 * oi], reads=[zt_b[i]])
                        if pos + np_ == S:
                            k.dma("sp", p_shift[seq:seq + 1, :], zt[i][np_ - 1:np_, NSA_COLS:INC], reads=[zt_b[i]])
                    else:
                        for oi in range(4):
                            k.dma("sp", s_kv[oi][:, :], zt[i][:np_, 512 + 128 * oi:640 + 128 * oi], reads=[zt_b[i]])
            k.barrier()

        rwkv_phase(k, nc, W, Z, O, p_wkv, rwc_in, state_shift_in, state_wkv_in, s_wkv, RS)
        nsa_prompt_phase(k, nc, W, Z, O, cin, GD)
        if NSA_SAMPLE:
            nsa_sample_phase(k, nc, W, Z, O, cin, pools_in, page_table_in, (cache_win_k, cache_win_v), PAST, OBR)

        with contextlib.ExitStack() as es2:
            sb2 = lambda n, s, d: es2.enter_context(nc.sbuf_tensor(k.uniq(n), s, d))
            ps2 = lambda n, s, d: es2.enter_context(nc.psum_tensor(k.uniq(n), s, d))
            wo = sb2("wo", [128, 8, D], BF16)
            wob = k.buf("wo")
            for kc in range(8):
                load_cast(k, wo[:, kc, :], W["w_out"][kc * 128:(kc + 1) * 128, :], wob)
            ot_ = [sb2(f"ot{i}", [128, D], F32) for i in range(2)]
            ot_b = [k.buf() for _ in range(2)]
            ob = [sb2(f"ob{i}", [128, D], BF16) for i in range(2)]
            ob_b = [k.buf() for _ in range(2)]
            oT = [sb2(f"oT{i}", [128, 8, 128], BF16) for i in range(2)]
            oT_b = [k.buf() for _ in range(2)]
            x1t = [sb2(f"x1t{i}", [128, D], F32) for i in range(2)]
            x1t_b = [k.buf() for _ in range(2)]
            pTo = [ps2(f"pTo{i}", [128, 8, 128], BF16) for i in range(2)]
            pTo_b = [k.buf() for _ in range(2)]
            pw_ = [ps2(f"pwo{i}", [128, 512], F32) for i in range(4)]
            pw_b = [k.buf() for _ in range(4)]
            n_sub = 0
            n_pw = 0
            for (row0, nt, _) in tiles:
                for st in range((nt + 127) // 128):
                    np_ = min(128, nt - st * 128)
                    i = n_sub % 2
                    n_sub += 1
                    r0 = row0 + st * 128
                    k.dma("sp", ot_[i][:np_, :], O[r0:r0 + np_, :], writes=[ot_b[i]])
                    k.dma("sp", x1t[i][:np_, :], X1[r0:r0 + np_, :], writes=[x1t_b[i]])
                    k.op("act", lambda e: e.copy(out=ob[i][:np_, :], in_=ot_[i][:np_, :]), reads=[ot_b[i]], writes=[ob_b[i]])
                    for kc in range(8):
                        k.op("pe", lambda e: e.transpose(out=pTo[i][:, kc, :np_], in_=ob[i][:np_, kc * 128:(kc + 1) * 128], identity=ident[:np_, :np_]),
                             reads=[ob_b[i]], writes=[pTo_b[i]])
                    k.op("dve", lambda e: e.tensor_copy(out=oT[i][:, :, :np_], in_=pTo[i][:, :, :np_]), reads=[pTo_b[i]], writes=[oT_b[i]])
                    for dh in range(2):
                        j = n_pw % 4
                        n_pw += 1
                        for kc in range(8):
                            k.op("pe", lambda e: e.matmul(pw_[j][:np_, :], lhsT=oT[i][:, kc, :np_], rhs=wo[:, kc, dh * 512:(dh + 1) * 512],
                                                          start=(kc == 0), stop=(kc == 7)), reads=[oT_b[i], wob], writes=[pw_b[j]])
                        k.op("dve", lambda e: e.tensor_tensor(out=x1t[i][:np_, dh * 512:(dh + 1) * 512], in0=pw_[j][:np_, :],
                                                              in1=x1t[i][:np_, dh * 512:(dh + 1) * 512], op=ALU.add),
                             reads=[pw_b[j], x1t_b[i]], writes=[x1t_b[i]])
                    k.dma("sp", X1[r0:r0 + np_, :], x1t[i][:np_, :], reads=[x1t_b[i]])
            k.barrier()

        def y_rows(r0, n):
            if r0 < PB * S:
                return y_prompt[r0:r0 + n, :]
            return y_sample[r0 - PB * S:r0 - PB * S + n, :]

        ffn_phase("ffn2", [(r0, nt, X1[r0:r0 + nt, :]) for (r0, nt, _) in tiles], y_rows, final=True)

        Zs = Z[PB * S:PB * S + NS_TOK, :].rearrange("(t b) c -> b t c", t=DS)
        for oi, cw in enumerate((cache_win_k, cache_win_v)):
            k.dma("sp", s_win[oi][:, 0:WIN - DS, :], cw[:, DS:WIN, :])
            k.dma("sp", s_win[oi][:, WIN - DS:WIN, :], Zs[:, :, 1024 + 128 * oi:1152 + 128 * oi])
        k.dma("sp", s_shift[:, :], Zs[:, DS - 1, NSA_COLS:INC])
        k.finish()
    return nc, list(dout.keys())


_CACHE = {}


def kernel(**inputs):
    if "nc" not in _CACHE:
        _CACHE["nc"] = build_program()
    nc, out_names = _CACHE["nc"]
    f = lambda a: np.ascontiguousarray(np.asarray(a, dtype=np.float32))
    shared = {"ident": np.eye(128, dtype=np.float32)}
    for n in ("ffn1_norm", "ffn1_wg", "ffn1_wu", "ffn1_wd", "ffn2_norm", "ffn2_wg", "ffn2_wu", "ffn2_wd", "mix_norm", "w_in", "w_out"):
        shared[n] = f(inputs[n][0])
    shared["final_norm"] = f(inputs["final_norm"])
    for n in ("shift_mu", "decay_w0", "decay_w2", "aaa_a0", "aaa_a2", "gate_g2", "k_k", "k_a", "ln_x_w", "ln_x_b"):
        shared[n] = f(inputs[n][0])
    shared["r_k"] = f(inputs["r_k"][0]).reshape(512)
    shared["rw_consts"] = rw_host_consts()
    shared["rel_bias_table"] = f(inputs["rel_bias_table"])
    for n in ("cmp_pe_k", "cmp_w1_k", "cmp_w2_k", "cmp_pe_v", "cmp_w1_v", "cmp_w2_v"):
        shared[n] = f(inputs[n][0])
    shared.update(nsa_host_consts())
    if NSA_SAMPLE:
        shared.update(nsa_sample_host_consts())
        for n in ("cache_cmp_k", "cache_cmp_v", "cache_slc_k", "cache_slc_v"):
            shared[n] = np.asarray(inputs[n], dtype=np.float32).reshape(20480, 8192)
    in_maps = []
    for c in range(NCORES):
        m = dict(shared)
        m["x_prompt"] = f(inputs["x_prompt"][PB * c:PB * (c + 1)]).reshape(PB * S, D)
        m["x_sample"] = f(np.asarray(inputs["x_sample"][SBT * c:SBT * (c + 1)]).transpose(1, 0, 2)).reshape(NS_TOK, D)
        if NSA_SAMPLE:
            m["page_table"] = np.ascontiguousarray(np.asarray(inputs["page_table"][SBT * c:SBT * (c + 1)], dtype=np.int32))
        m["state_shift"] = f(inputs["state_shift"][0, SBT * c:SBT * (c + 1)])
        m["state_wkv"] = f(inputs["state_wkv"][0, SBT * c:SBT * (c + 1)])
        m["cache_win_k"] = f(inputs["cache_win_k"][0, SBT * c:SBT * (c + 1)]).reshape(SBT, WIN, 128)
        m["cache_win_v"] = f(inputs["cache_win_v"][0, SBT * c:SBT * (c + 1)]).reshape(SBT, WIN, 128)
        in_maps.append(m)
    if _CACHE.get("dev1"):
        _CACHE["in0"] = in_maps[0]
        res = run_bass_kernel_spmd(nc, in_maps[:1], core_ids=[0])
        _CACHE["last"] = res.results
        return None
    res = run_bass_kernel_spmd(nc, in_maps, core_ids=list(range(NCORES)))
    R = res.results
    _CACHE["last"] = R
    cat = lambda n: np.concatenate([np.asarray(r[n]) for r in R], axis=0)
    cats = lambda n, w: np.concatenate([np.asarray(r[n]).reshape(DS, SBT, w).transpose(1, 0, 2) for r in R], axis=0)
    B, BS = PB * NCORES, SBT * NCORES
    outs = (
        cat("y_prompt").reshape(B, S, D),
        cats("y_sample", D).reshape(BS, DS, D),
        cat("p_cmp_k").reshape(1, B, S, 2, 64), cat("p_cmp_v").reshape(1, B, S, 2, 64),
        cat("p_slc_k").reshape(1, B, S, 2, 64), cat("p_slc_v").reshape(1, B, S, 2, 64),
        cat("p_win_k").reshape(1, B, WIN, 2, 64), cat("p_win_v").reshape(1, B, WIN, 2, 64),
        cat("p_wkv").reshape(1, B, 8, 64, 64), cat("p_shift").reshape(1, B, RWC),
        cats("s_cmp_k", 128).reshape(1, BS, DS, 2, 64), cats("s_cmp_v", 128).reshape(1, BS, DS, 2, 64),
        cats("s_slc_k", 128).reshape(1, BS, DS, 2, 64), cats("s_slc_v", 128).reshape(1, BS, DS, 2, 64),
        cat("s_win_k").reshape(1, BS, WIN, 2, 64), cat("s_win_v").reshape(1, BS, WIN, 2, 64),
        cat("s_wkv").reshape(1, BS, 8, 64, 64), cat("s_shift").reshape(1, BS, RWC),
    )
    return tuple(np.ascontiguousarray(o.astype(np.float32)) for o in outs)
```

```python
import contextlib
import numpy as np
import concourse.bass as bass
import concourse.mybir as mybir
from concourse.bass_utils import run_bass_kernel_spmd

F32 = mybir.dt.float32
BF16 = mybir.dt.bfloat16
I32 = mybir.dt.int32
AF = mybir.ActivationFunctionType
ALU = mybir.AluOpType
AX = mybir.AxisListType

NCORES = 8
D = 1024
DFF = 2816
NFC = DFF // 128
INC = 3096
NSA_COLS = 1304
RWC = 1792
PB = 2
S = 2048
SBT = 16
DS = 4
NS_TOK = SBT * DS
WIN = 512
EPS = 1e-6
SAME_ENG_SYNC = True
DEBUG = False
NSA_PARTS = 4
NSA_SAMPLE = 1
S4_QT = 4
S4_H = 8
S4_BR = 3
S4_NORM = 1
S4_STEP = 9
S4_SUB = 3
S4_NOBIAS = 1
S_BARRIERS = 0
S_SKIP23 = 0
PV_PLAIN = 0
VAUG_ENG = 0
MAX_WAITS = 2
TB_SHIFT = 1


class Buf:
    __slots__ = ("name", "w", "r")

    def __init__(self, name):
        self.name = name
        self.w = None
        self.r = {}


class EngW:
    def __init__(self, e, sem, is_pe=False):
        self.e = e
        self.sem = sem
        self.cnt = 0
        self.seen = {}
        self.is_pe = is_pe
        self.nwait = 0

    def wait(self, ev):
        if ev is None:
            return
        sem, val = ev
        if sem is self.sem and (self.is_pe or not SAME_ENG_SYNC):
            return
        k = id(sem)
        if self.seen.get(k, 0) >= val:
            return
        if MAX_WAITS and self.nwait >= MAX_WAITS:
            self.e.nop(nofuse=True)
            self.nwait = 0
        self.e.wait_ge(sem, val)
        self.nwait += 1
        self.seen[k] = val


class K:
    def __init__(self, nc, es, nring=40):
        self.nc = nc
        self.es = es
        mk = lambda n: es.enter_context(nc.semaphore(n))
        self.eng = {
            "pe": EngW(nc.tensor, mk("s_pe"), True),
            "act": EngW(nc.scalar, mk("s_act")),
            "dve": EngW(nc.vector, mk("s_dve")),
            "pool": EngW(nc.gpsimd, mk("s_pool")),
            "sp": EngW(nc.sync, mk("s_sp")),
        }
        self.ring = [[mk(f"s_dma{i}"), 0] for i in range(nring)]
        self.ring_i = 0
        self.sw_ring = [[mk(f"s_swdma{i}"), 0] for i in range(8)]
        self.sw_ring_i = 0
        self.nbuf = 0

    def buf(self, name=None):
        self.nbuf += 1
        return Buf(name or f"b{self.nbuf}")

    def uniq(self, name):
        self.nbuf += 1
        return f"{name}_u{self.nbuf}"

    def sb(self, name, shape, dt):
        return self.es.enter_context(self.nc.sbuf_tensor(self.uniq(name), shape, dt))

    def _pre(self, E, reads, writes):
        for b in reads:
            E.wait(b.w)
        for b in writes:
            E.wait(b.w)
            for ev in b.r.values():
                E.wait(ev)

    def _post(self, ev, reads, writes):
        for b in reads:
            b.r[id(ev[0])] = ev
        for b in writes:
            b.w = ev
            b.r = {}

    def op(self, eng, fn, reads=(), writes=()):
        E = self.eng[eng]
        self._pre(E, reads, writes)
        ins = fn(E.e)
        E.nwait = 0
        E.cnt += 1
        ins.then_inc(E.sem, 1)
        ev = (E.sem, E.cnt)
        self._post(ev, reads, writes)
        return ev

    def dma(self, q, out, in_, reads=(), writes=(), indirect=None, **kw):
        E = self.eng[q]
        self._pre(E, reads, writes)
        if q == "pool":
            slot = self.sw_ring[self.sw_ring_i]
            self.sw_ring_i = (self.sw_ring_i + 1) % len(self.sw_ring)
        else:
            slot = self.ring[self.ring_i]
            self.ring_i = (self.ring_i + 1) % len(self.ring)
        E.wait((slot[0], slot[1]) if slot[1] else None)
        if indirect is not None:
            ins = E.e.indirect_dma_start(out=out, out_offset=None, in_=in_, in_offset=indirect, **kw)
        else:
            ins = E.e.dma_start(out=out, in_=in_, **kw)
        slot[1] += 16
        ins.then_inc(slot[0], 16)
        ev = (slot[0], slot[1])
        self._post(ev, reads, writes)
        return ev

    def barrier(self):
        evs = [(E.sem, E.cnt) for E in self.eng.values() if E.cnt] + [(s[0], s[1]) for s in self.ring + self.sw_ring if s[1]]
        for E in self.eng.values():
            for ev in evs:
                if ev[0] is not E.sem:
                    E.wait(ev)

    def finish(self):
        E = self.eng["sp"]
        for s in self.ring + self.sw_ring:
            if s[1]:
                E.wait((s[0], s[1]))
        for E2 in self.eng.values():
            if E2.cnt and E2 is not E:
                E.wait((E2.sem, E2.cnt))


def load_cast(k, dst, src, wbuf, maxel=2048):
    n = src.shape[-1]
    for c0 in range(0, n, maxel):
        c1 = min(n, c0 + maxel)
        k.dma("pool", dst[:, c0:c1], src[:, c0:c1], writes=[wbuf])


def rms_to_hT(k, es_bufs, x_ap, xbuf, np_, gT, hT, hT_buf, tok0, consts):
    sq, ss, rstd, xs, xs_b, ss_b, pT, pT_b, ident = es_bufs
    k.op("act", lambda e: e.activation(out=sq[:np_, :], in_=x_ap, func=AF.Square, accum_out=ss[:np_, :]),
         reads=[xbuf], writes=[ss_b])
    k.op("act", lambda e: e.activation(out=rstd[:np_, :], in_=ss[:np_, :], func=AF.Sqrt, scale=1.0 / D, bias=consts["eps"][:np_, :]),
         reads=[ss_b], writes=[ss_b])
    k.op("dve", lambda e: e.reciprocal(out=rstd[:np_, :], in_=rstd[:np_, :]), reads=[ss_b], writes=[ss_b])
    k.op("act", lambda e: e.activation(out=xs[:np_, :], in_=x_ap, func=AF.Copy, scale=rstd[:np_, :]),
         reads=[xbuf, ss_b], writes=[xs_b])
    for kc in range(8):
        k.op("pe", lambda e: e.transpose(out=pT[:, kc, :np_], in_=xs[:np_, kc * 128:(kc + 1) * 128], identity=ident[:np_, :np_]),
             reads=[xs_b], writes=[pT_b])
    k.op("dve", lambda e: e.tensor_tensor(out=hT[:, :, tok0:tok0 + np_], in0=pT[:, :, :np_],
                                          in1=gT[:, :].unsqueeze(2).to_broadcast([128, 8, np_]), op=ALU.mult),
         reads=[pT_b], writes=[hT_buf])


class PsumPool:
    def __init__(self, k, nc, es, n=8):
        self.k = k
        self.banks = []
        for i in range(n):
            t = es.enter_context(nc.psum_tensor(k.uniq("bank"), [128, 512], F32))
            self.banks.append((t, k.buf(f"bank{i}")))
        self.i = 0

    def get(self):
        b = self.banks[self.i]
        self.i = (self.i + 1) % len(self.banks)
        return b


def rw_host_consts():
    i = np.arange(128)[:, None]
    t = np.arange(128)[None, :]
    f = np.float32
    U1 = (i <= t).astype(f) - (i <= 63).astype(f)
    U2 = (i < t).astype(f) - (i <= 63).astype(f)
    sel2 = np.zeros((128, 128), f)
    sel2[:64, 0] = 1
    sel2[64:, 1] = 1
    st = (i < t).astype(f)
    inc = (i <= t).astype(f)
    maskB = (t < i).astype(f)
    ident = np.eye(128, dtype=f)
    ones = np.ones((128, 128), f)
    return np.ascontiguousarray(np.stack([U1, U2, sel2, st, st, inc, inc, maskB, ident, ones], axis=1))


C_DEC = float(np.exp(-0.5))
GN_EPS = 64e-5


def rwkv_phase(k, nc, W, Z, O, p_wkv, rwc_in, state_shift, state_wkv, s_wkv, RS):
    with contextlib.ExitStack() as es3:
        sb = lambda n, s, d=F32: es3.enter_context(nc.sbuf_tensor(k.uniq(n), s, d))
        pp = PsumPool(k, nc, es3)
        cb = k.buf("rwconst")
        RC = sb("RC", [128, 10, 128])
        k.dma("sp", RC[:], rwc_in, writes=[cb])
        U1, U2, SEL2 = RC[:, 0, :], RC[:, 1, :], RC[:, 2, 0:2]
        MASKA = RC[:, 3:7, :]
        MASKB = RC[:, 7, :]
        IDF = RC[:, 8, :]
        ONES1 = RC[0:1, 9, :]
        bc = {}
        for n, w in (("shift_mu", RWC), ("k_k", 512), ("k_a", 512), ("r_k", 512), ("ln_x_w", 512), ("ln_x_b", 512)):
            bc[n] = sb("bc_" + n, [128, w])
            k.dma("sp", bc[n][:], W[n].partition_broadcast(128), writes=[cb])
        w2a2 = sb("w2a2", [128, 512])
        k.dma("sp", w2a2[0:64, :], W["decay_w2"], writes=[cb])
        k.dma("sp", w2a2[64:128, :], W["aaa_a2"], writes=[cb])
        g2 = sb("g2", [128, 512])
        k.dma("sp", g2[:], W["gate_g2"], writes=[cb])
        w0a0 = sb("w0a0", [1, 2, 512])
        k.dma("sp", w0a0[0:1, 0, :], W["decay_w0"].unsqueeze(0), writes=[cb])
        k.dma("sp", w0a0[0:1, 1, :], W["aaa_a0"].unsqueeze(0), writes=[cb])

        es4 = contextlib.ExitStack()

        def T(n, shape=(128, 512), tmp_stack=False):
            st = es4 if tmp_stack else es3
            return st.enter_context(nc.sbuf_tensor(k.uniq(n), list(shape), F32)), k.buf(n)

        def TP(n, shape=(128, 512)):
            return T(n, shape, True)

        pt, pt_b = T("pt", (128, RWC))
        pv, pv_b = T("pv", (128, RWC))
        xs, xs_b = T("xs", (128, RWC))
        loraT, lora_b = T("loraT", (128, 2, 128))
        sg, sg_b = T("sg")
        aa, aa_b = T("aa")
        gg, gg_b = T("gg")
        kk, kk_b = T("kk")
        tmp, tmp_b = T("tmp")
        sm, sm_b = T("sm", (128, 8, 4))
        kap, kap_b = T("kap")
        kmod, kmod_b = T("kmod")
        EN, EN_b = T("EN")
        Bn, Bn_b = T("Bn")
        GM_b = [k.buf() for _ in range(8)]
        NT_hb = [[k.buf() for _ in range(2)] for _ in range(2)]
        AA_hb = [[k.buf() for _ in range(2)] for _ in range(2)]
        Tm_b = [k.buf() for _ in range(2)]
        yt, yt_b = T("yt")
        ysq, ysq_b = T("ysq")
        ot, ot_b = T("ot")
        EP, EP_b = TP("EP")
        EPX, EPX_b = TP("EPX")
        PV, PV_b = TP("PV", (128, 4, 2))
        Kc, Kc_b = TP("Kc")
        Kt, Kt_b = TP("Kt")
        Rt, Rt_b = TP("Rt")
        FT = [TP(f"FT{i}", (128, 4, 128)) for i in range(4)]
        GM, _ = TP("GM", (128, 8, 4, 128))
        NT = [TP(f"NT{i}", (128, 8, 128)) for i in range(2)]
        AA = [TP(f"AA{i}", (128, 8, 128)) for i in range(2)]
        Tm, _ = TP("Tm", (128, 8, 128))
        ST = [TP(f"ST{i}", (128, 4, 64)) for i in range(2)]
        S0s, S0s_b = TP("S0s", (128, 4, 64))
        XTs, XTs_b = TP("XTs", (128, 8, 64))
        ETs, ETs_b = TP("ETs", (128, 8, 64))

        def act(fn, r, w):
            return k.op("act", fn, reads=r, writes=w)

        def dve(fn, r, w):
            return k.op("dve", fn, reads=r, writes=w)

        def pool(fn, r, w):
            return k.op("pool", fn, reads=r, writes=w)

        def pe(fn, r, w):
            return k.op("pe", fn, reads=r, writes=w)

        r_ap, k_ap, v_ap = xs[:, 0:512], xs[:, 512:1024], xs[:, 1024:1536]

        def prep():
            pool(lambda e: e.tensor_tensor(out=pv[:], in0=pv[:], in1=pt[:], op=ALU.subtract), [pt_b, pv_b], [pv_b])
            pool(lambda e: e.tensor_tensor(out=pv[:], in0=pv[:], in1=bc["shift_mu"][:], op=ALU.mult), [pv_b, cb], [pv_b])
            dve(lambda e: e.tensor_tensor(out=xs[:], in0=pv[:], in1=pt[:], op=ALU.add), [pv_b, pt_b], [xs_b])
            bT, bT_b = pp.get()
            for c in range(2):
                pe(lambda e: e.transpose(out=bT[:, c * 128:(c + 1) * 128], in_=xs[:, 1536 + c * 128:1664 + c * 128], identity=IDF),
                   [xs_b, cb], [bT_b])
            act(lambda e: e.activation(out=loraT[0:64, 0, :], in_=bT[0:64, 0:128], func=AF.Tanh), [bT_b], [lora_b])
            act(lambda e: e.activation(out=loraT[64:128, 0, :], in_=bT[64:128, 0:128], func=AF.Copy), [bT_b], [lora_b])
            act(lambda e: e.activation(out=loraT[:, 1, :], in_=bT[:, 128:256], func=AF.Sigmoid), [bT_b], [lora_b])
            pw, pw_b = pp.get()
            pe(lambda e: e.matmul(pw[:], lhsT=loraT[0:64, 0, :], rhs=w2a2[0:64, :], start=True, stop=False), [lora_b, cb], [pw_b])
            pe(lambda e: e.matmul(pw[:], lhsT=ONES1, rhs=w0a0[0:1, 0, :], start=False, stop=True), [cb], [pw_b])
            pa, pa_b = pp.get()
            pe(lambda e: e.matmul(pa[:], lhsT=loraT[64:128, 0, :], rhs=w2a2[64:128, :], start=True, stop=False), [lora_b, cb], [pa_b])
            pe(lambda e: e.matmul(pa[:], lhsT=ONES1, rhs=w0a0[0:1, 1, :], start=False, stop=True), [cb], [pa_b])
            pg, pg_b = pp.get()
            pe(lambda e: e.matmul(pg[:], lhsT=loraT[:, 1, :], rhs=g2[:], start=True, stop=True), [lora_b, cb], [pg_b])
            act(lambda e: e.activation(out=sg[:], in_=pw[:], func=AF.Sigmoid), [pw_b], [sg_b])
            act(lambda e: e.activation(out=aa[:], in_=pa[:], func=AF.Sigmoid), [pa_b], [aa_b])
            act(lambda e: e.copy(out=gg[:], in_=pg[:]), [pg_b], [gg_b])
            dve(lambda e: e.tensor_tensor(out=kk[:], in0=k_ap, in1=bc["k_k"][:], op=ALU.mult), [xs_b, cb], [kk_b])
            pool(lambda e: e.tensor_tensor(out=tmp[:], in0=kk[:], in1=kk[:], op=ALU.mult), [kk_b], [tmp_b])
            dve(lambda e: e.tensor_reduce(out=sm[:, :, 0], in_=tmp[:].rearrange("p (h d) -> p h d", d=64), axis=AX.X, op=ALU.add), [tmp_b], [sm_b])
            act(lambda e: e.activation(out=sm[:, :, 1], in_=sm[:, :, 0], func=AF.Sqrt), [sm_b], [sm_b])
            dve(lambda e: e.tensor_scalar(out=sm[:, :, 1], in0=sm[:, :, 1], scalar1=1e-12, scalar2=None, op0=ALU.max), [sm_b], [sm_b])
            dve(lambda e: e.reciprocal(out=sm[:, :, 1], in_=sm[:, :, 1]), [sm_b], [sm_b])
            dve(lambda e: e.tensor_tensor(out=kap[:].rearrange("p (h d) -> p h d", d=64), in0=kk[:].rearrange("p (h d) -> p h d", d=64),
                                          in1=sm[:, :, 1:2].to_broadcast([128, 8, 64]), op=ALU.mult), [kk_b, sm_b], [kap_b])
            dve(lambda e: e.scalar_tensor_tensor(out=tmp[:], in0=aa[:], scalar=-1.0, in1=bc["k_a"][:], op0=ALU.add, op1=ALU.mult), [aa_b, cb], [tmp_b])
            dve(lambda e: e.scalar_tensor_tensor(out=kmod[:], in0=tmp[:], scalar=1.0, in1=k_ap, op0=ALU.add, op1=ALU.mult), [tmp_b, xs_b], [kmod_b])

        def post(dst_ap, np_=128):
            y3 = yt[:].rearrange("p (h d) -> p h d", d=64)
            dve(lambda e: e.tensor_reduce(out=sm[:, :, 2], in_=y3, axis=AX.X, op=ALU.add), [yt_b], [sm_b])
            pool(lambda e: e.tensor_tensor(out=ysq[:], in0=yt[:], in1=yt[:], op=ALU.mult), [yt_b], [ysq_b])
            dve(lambda e: e.tensor_reduce(out=sm[:, :, 3], in_=ysq[:].rearrange("p (h d) -> p h d", d=64), axis=AX.X, op=ALU.add), [ysq_b], [sm_b])
            dve(lambda e: e.tensor_scalar(out=sm[:, :, 2], in0=sm[:, :, 2], scalar1=1.0 / 64, scalar2=None, op0=ALU.mult), [sm_b], [sm_b])
            dve(lambda e: e.tensor_tensor(out=sm[:, :, 0], in0=sm[:, :, 2], in1=sm[:, :, 2], op=ALU.mult), [sm_b], [sm_b])
            dve(lambda e: e.scalar_tensor_tensor(out=sm[:, :, 3], in0=sm[:, :, 3], scalar=1.0 / 64, in1=sm[:, :, 0], op0=ALU.mult, op1=ALU.subtract),
                [sm_b], [sm_b])
            dve(lambda e: e.tensor_scalar(out=sm[:, :, 3], in0=sm[:, :, 3], scalar1=GN_EPS, scalar2=None, op0=ALU.add), [sm_b], [sm_b])
            act(lambda e: e.activation(out=sm[:, :, 3], in_=sm[:, :, 3], func=AF.Sqrt), [sm_b], [sm_b])
            dve(lambda e: e.reciprocal(out=sm[:, :, 3], in_=sm[:, :, 3]), [sm_b], [sm_b])
            dve(lambda e: e.tensor_tensor(out=y3, in0=y3, in1=sm[:, :, 2:3].to_broadcast([128, 8, 64]), op=ALU.subtract), [yt_b, sm_b], [yt_b])
            dve(lambda e: e.tensor_tensor(out=y3, in0=y3, in1=sm[:, :, 3:4].to_broadcast([128, 8, 64]), op=ALU.mult), [yt_b, sm_b], [yt_b])
            pool(lambda e: e.tensor_tensor(out=yt[:], in0=yt[:], in1=bc["ln_x_w"][:], op=ALU.mult), [yt_b, cb], [yt_b])
            pool(lambda e: e.tensor_tensor(out=yt[:], in0=yt[:], in1=bc["ln_x_b"][:], op=ALU.add), [yt_b, cb], [yt_b])
            pool(lambda e: e.tensor_tensor(out=tmp[:], in0=r_ap, in1=kmod[:], op=ALU.mult), [xs_b, kmod_b], [tmp_b])
            pool(lambda e: e.tensor_tensor(out=tmp[:], in0=tmp[:], in1=bc["r_k"][:], op=ALU.mult), [tmp_b, cb], [tmp_b])
            dve(lambda e: e.tensor_reduce(out=sm[:, :, 0], in_=tmp[:].rearrange("p (h d) -> p h d", d=64), axis=AX.X, op=ALU.add), [tmp_b], [sm_b])
            dve(lambda e: e.tensor_tensor(out=ysq[:].rearrange("p (h d) -> p h d", d=64), in0=v_ap.rearrange("p (h d) -> p h d", d=64),
                                          in1=sm[:, :, 0:1].to_broadcast([128, 8, 64]), op=ALU.mult), [xs_b, sm_b], [ysq_b])
            dve(lambda e: e.tensor_tensor(out=ot[:], in0=yt[:], in1=ysq[:], op=ALU.add), [yt_b, ysq_b], [ot_b])
            dve(lambda e: e.tensor_tensor(out=ot[:], in0=ot[:], in1=gg[:], op=ALU.mult), [ot_b, gg_b], [ot_b])
            k.dma("sp", dst_ap, ot[:np_, :], reads=[ot_b])

        for seq in range(PB):
            stc = 0
            dve(lambda e: e.memset(ST[0][0][:], 0.0), [], [ST[0][1]])
            for ti in range(S // 128):
                r0 = seq * S + ti * 128
                k.dma("sp", pt[:], Z[r0:r0 + 128, NSA_COLS:INC], writes=[pt_b])
                if ti == 0:
                    dve(lambda e: e.memset(pv[0:1, :], 0.0), [], [pv_b])
                    k.dma("sp", pv[1:128, :], Z[r0:r0 + 127, NSA_COLS:INC], writes=[pv_b])
                else:
                    k.dma("sp", pv[:], Z[r0 - 1:r0 + 127, NSA_COLS:INC], writes=[pv_b])
                prep()
                lpi, lpi_b = pp.get()
                pe(lambda e: e.matmul(lpi[:], lhsT=U1, rhs=sg[:], start=True, stop=True), [sg_b, cb], [lpi_b])
                lpe, lpe_b = pp.get()
                pe(lambda e: e.matmul(lpe[:], lhsT=U2, rhs=sg[:], start=True, stop=True), [sg_b, cb], [lpe_b])
                ppv, ppv_b = pp.get()
                for gq in range(4):
                    pe(lambda e: e.matmul(ppv[:, gq * 2:gq * 2 + 2], lhsT=sg[:, gq * 128:(gq + 1) * 128], rhs=SEL2, start=True, stop=True),
                       [sg_b, cb], [ppv_b])
                act(lambda e: e.activation(out=EN[:], in_=lpi[:], func=AF.Exp, scale=C_DEC), [lpi_b], [EN_b])
                act(lambda e: e.activation(out=EP[:], in_=lpi[:], func=AF.Exp, scale=-C_DEC), [lpi_b], [EP_b])
                act(lambda e: e.activation(out=EPX[:], in_=lpe[:], func=AF.Exp, scale=-C_DEC), [lpe_b], [EPX_b])
                act(lambda e: e.activation(out=PV[:].rearrange("p a b -> p (a b)"), in_=ppv[:, 0:8], func=AF.Exp, scale=-C_DEC), [ppv_b], [PV_b])
                dve(lambda e: e.tensor_tensor(out=Kc[:], in0=kap[:], in1=EPX[:], op=ALU.mult), [kap_b, EPX_b], [Kc_b])
                pool(lambda e: e.tensor_tensor(out=tmp[:], in0=kap[:], in1=aa[:], op=ALU.mult), [kap_b, aa_b], [tmp_b])
                dve(lambda e: e.scalar_tensor_tensor(out=Bn[:], in0=tmp[:], scalar=-1.0, in1=EN[:], op0=ALU.mult, op1=ALU.mult), [tmp_b, EN_b], [Bn_b])
                pool(lambda e: e.tensor_tensor(out=Kt[:], in0=kmod[:], in1=EN[:], op=ALU.mult), [kmod_b, EN_b], [Kt_b])
                pool(lambda e: e.tensor_tensor(out=Rt[:], in0=r_ap, in1=EP[:], op=ALU.mult), [xs_b, EP_b], [Rt_b])
                for xi, (src, src_b) in enumerate(((Kc, Kc_b), (Bn, Bn_b), (Kt, Kt_b), (Rt, Rt_b))):
                    bk, bk_b = pp.get()
                    for gq in range(4):
                        pe(lambda e: e.transpose(out=bk[:, gq * 128:(gq + 1) * 128], in_=src[:, gq * 128:(gq + 1) * 128], identity=IDF),
                           [src_b, cb], [bk_b])
                    if xi % 2 == 0:
                        act(lambda e: e.copy(out=FT[xi][0][:].rearrange("p a b -> p (a b)"), in_=bk[:]), [bk_b], [FT[xi][1]])
                    else:
                        dve(lambda e: e.tensor_copy(out=FT[xi][0][:].rearrange("p a b -> p (a b)"), in_=bk[:]), [bk_b], [FT[xi][1]])
                KcT, BnT, KtT, RtT = (FT[i][0] for i in range(4))
                ftb = [FT[i][1] for i in range(4)]
                for hb in range(2):
                    b2, b2_b = pp.get()
                    for hl in range(4):
                        h = hb * 4 + hl
                        gq, base = h // 2, (h % 2) * 64
                        sl = slice(base, base + 64)
                        b1, b1_b = pp.get()
                        for gi, (l, r) in enumerate(((BnT, KcT), (KtT, KcT), (BnT, RtT), (KtT, RtT))):
                            pe(lambda e: e.matmul(b1[:, gi * 128:(gi + 1) * 128], lhsT=l[sl, gq, :], rhs=r[sl, gq, :], start=True, stop=True),
                               ftb, [b1_b])
                        dve(lambda e: e.tensor_tensor(out=GM[:, h, :, :], in0=b1[:].rearrange("p (a b) -> p a b", b=128), in1=MASKA, op=ALU.mult),
                            [b1_b, cb], [GM_b[h]])
                        pe(lambda e: e.matmul(b2[:, hl * 128:(hl + 1) * 128], lhsT=KcT[sl, gq, :], rhs=BnT[sl, gq, :], start=True, stop=True),
                           ftb, [b2_b])
                    dve(lambda e: e.tensor_tensor(out=NT[0][0][:, hb * 4:hb * 4 + 4, :], in0=b2[:].rearrange("p (a b) -> p a b", b=128),
                                                  in1=MASKB.unsqueeze(1).to_broadcast([128, 4, 128]), op=ALU.mult), [b2_b, cb], [NT_hb[0][hb]])
                for hb in range(2):
                    hs = slice(hb * 4, hb * 4 + 4)
                    gmb = GM_b[hb * 4:hb * 4 + 4]
                    dve(lambda e: e.tensor_tensor(out=Tm[:, hs, :], in0=GM[:, hs, 0, :], in1=IDF.unsqueeze(1).to_broadcast([128, 4, 128]), op=ALU.add),
                        gmb + [cb], [Tm_b[hb]])
                    cur = None
                    for rnd in range(6):
                        last = rnd == 5
                        new = rnd % 2
                        if rnd == 0:
                            Aold = lambda h: GM[:, h, 0, :]
                            Aold_b = gmb
                            ATold = lambda h: NT[0][0][:, h, :]
                            ATold_b = [NT_hb[0][hb]]
                            nA, nA_b = AA[0][0], AA_hb[0][hb]
                            nAT, nAT_b = NT[1][0], NT_hb[1][hb]
                        else:
                            pa_i, pt_i = (rnd - 1) % 2, rnd % 2
                            Aold = (lambda ii: (lambda h: AA[ii][0][:, h, :]))(pa_i)
                            Aold_b = [AA_hb[pa_i][hb]]
                            ATold = (lambda ii: (lambda h: NT[ii][0][:, h, :]))(pt_i)
                            ATold_b = [NT_hb[pt_i][hb]]
                            nA, nA_b = AA[1 - pa_i][0], AA_hb[1 - pa_i][hb]
                            nAT, nAT_b = NT[1 - pt_i][0], NT_hb[1 - pt_i][hb]
                        if not last:
                            bA, bA_b = pp.get()
                            for hl in range(4):
                                h = hb * 4 + hl
                                pe(lambda e: e.matmul(bA[:, hl * 128:(hl + 1) * 128], lhsT=ATold(h), rhs=Aold(h), start=True, stop=True),
                                   Aold_b + ATold_b, [bA_b])
                            act(lambda e: e.copy(out=nA[:, hs, :], in_=bA[:].rearrange("p (a b) -> p a b", b=128)), [bA_b], [nA_b])
                        bAT, bAT_b = pp.get()
                        for hl in range(4):
                            h = hb * 4 + hl
                            pe(lambda e: e.matmul(bAT[:, hl * 128:(hl + 1) * 128], lhsT=Aold(h), rhs=ATold(h), start=True, stop=True),
                               Aold_b + ATold_b, [bAT_b])
                        act(lambda e: e.copy(out=nAT[:, hs, :], in_=bAT[:].rearrange("p (a b) -> p a b", b=128)), [bAT_b], [nAT_b])
                        bTT, bTT_b = pp.get()
                        for hl in range(4):
                            h = hb * 4 + hl
                            pe(lambda e: e.matmul(bTT[:, hl * 128:(hl + 1) * 128], lhsT=nAT[:, h, :], rhs=Tm[:, h, :], start=True, stop=True),
                               [nAT_b, Tm_b[hb]], [bTT_b])
                        dve(lambda e: e.tensor_tensor(out=Tm[:, hs, :], in0=Tm[:, hs, :], in1=bTT[:].rearrange("p (a b) -> p a b", b=128), op=ALU.add),
                            [bTT_b, Tm_b[hb]], [Tm_b[hb]])
                Sold, Sold_b = ST[stc % 2]
                Snew, Snew_b = ST[(stc + 1) % 2]
                stc += 1
                dve(lambda e: e.tensor_tensor(out=S0s[:], in0=Sold[:], in1=PV[:, :, 0:1].to_broadcast([128, 4, 64]), op=ALU.mult),
                    [Sold_b, PV_b], [S0s_b])
                bX, bX_b = pp.get()
                for h in range(8):
                    gq, base = h // 2, (h % 2) * 64
                    sl = slice(base, base + 64)
                    pe(lambda e: e.matmul(bX[:, h * 64:(h + 1) * 64], lhsT=KcT[sl, gq, :], rhs=S0s[sl, gq, :], start=True, stop=False),
                       ftb + [S0s_b], [bX_b])
                    pe(lambda e: e.matmul(bX[:, h * 64:(h + 1) * 64], lhsT=GM[:, h, 1, :], rhs=v_ap[:, h * 64:(h + 1) * 64], start=False, stop=True),
                       [GM_b[h], xs_b], [bX_b])
                act(lambda e: e.copy(out=XTs[:].rearrange("p a b -> p (a b)"), in_=bX[:]), [bX_b], [XTs_b])
                bE, bE_b = pp.get()
                for h in range(8):
                    pe(lambda e: e.matmul(bE[:, h * 64:(h + 1) * 64], lhsT=Tm[:, h, :], rhs=XTs[:, h, :], start=True, stop=True),
                       [Tm_b[h // 4], XTs_b], [bE_b])
                dve(lambda e: e.tensor_copy(out=ETs[:].rearrange("p a b -> p (a b)"), in_=bE[:]), [bE_b], [ETs_b])
                bY, bY_b = pp.get()
                bS, bS_b = pp.get()
                for h in range(8):
                    gq, base = h // 2, (h % 2) * 64
                    sl = slice(base, base + 64)
                    hc = slice(h * 64, (h + 1) * 64)
                    pe(lambda e: e.matmul(bY[:, hc], lhsT=RtT[sl, gq, :], rhs=S0s[sl, gq, :], start=True, stop=False), ftb + [S0s_b], [bY_b])
                    pe(lambda e: e.matmul(bY[:, hc], lhsT=GM[:, h, 2, :], rhs=ETs[:, h, :], start=False, stop=False), [GM_b[h], ETs_b], [bY_b])
                    pe(lambda e: e.matmul(bY[:, hc], lhsT=GM[:, h, 3, :], rhs=v_ap[:, hc], start=False, stop=True), [GM_b[h], xs_b], [bY_b])
                for h in range(8):
                    gq, base = h // 2, (h % 2) * 64
                    sl = slice(base, base + 64)
                    hc = slice(h * 64, (h + 1) * 64)
                    pe(lambda e: e.matmul(bS[sl, gq * 64:(gq + 1) * 64], lhsT=Bn[:, hc], rhs=ETs[:, h, :], start=True, stop=False), [Bn_b, ETs_b], [bS_b])
                    pe(lambda e: e.matmul(bS[sl, gq * 64:(gq + 1) * 64], lhsT=Kt[:, hc], rhs=v_ap[:, hc], start=False, stop=True), [Kt_b, xs_b], [bS_b])
                dve(lambda e: e.tensor_tensor(out=Snew[:].rearrange("p a b -> p (a b)"), in0=bS[:, 0:256], in1=S0s[:].rearrange("p a b -> p (a b)"), op=ALU.add),
                    [bS_b, S0s_b], [Snew_b])
                dve(lambda e: e.tensor_tensor(out=Snew[:], in0=Snew[:], in1=PV[:, :, 1:2].to_broadcast([128, 4, 64]), op=ALU.mult),
                    [Snew_b, PV_b], [Snew_b])
                act(lambda e: e.copy(out=yt[:], in_=bY[:]), [bY_b], [yt_b])
                post(O[r0:r0 + 128, 512:1024])
            Sf, Sf_b = ST[stc % 2]
            with nc.allow_non_contiguous_dma(reason="state transpose store"):
                for h in range(8):
                    k.dma("sp", p_wkv[seq, h].rearrange("v k -> k v"), Sf[(h % 2) * 64:(h % 2) * 64 + 64, h // 2, :], reads=[Sf_b])

        k.barrier()
        es4.close()
        ZS0 = PB * S
        dve(lambda e: e.memset(pt[64:128, :], 0.0), [], [pt_b])
        dve(lambda e: e.memset(pv[64:128, :], 0.0), [], [pv_b])
        k.dma("sp", pt[0:64, :], Z[ZS0:ZS0 + 64, NSA_COLS:INC], writes=[pt_b])
        k.dma("sp", pv[0:16, :], state_shift, writes=[pv_b])
        k.dma("sp", pv[16:64, :], Z[ZS0:ZS0 + 48, NSA_COLS:INC], writes=[pv_b])
        prep()
        act(lambda e: e.activation(out=EN[:], in_=sg[:], func=AF.Exp, scale=-C_DEC), [sg_b], [EN_b])
        dve(lambda e: e.tensor_tensor(out=Bn[:], in0=kap[:], in1=aa[:], op=ALU.mult), [kap_b, aa_b], [Bn_b])
        rsb = k.buf("RS")
        for xi, (src, src_b) in enumerate(((kap[:, :], kap_b), (EN[:, :], EN_b), (Bn[:, :], Bn_b), (kmod[:, :], kmod_b), (r_ap, xs_b), (v_ap, xs_b))):
            k.dma("sp", RS[xi], src[0:64, :], reads=[src_b], writes=[rsb])
        Ssm, Ssm_b = T("Ssm", (128, 8, 8, 64))
        tA, tA_b = T("tA", (128, 8, 8, 64))
        tB, tB_b = T("tB", (128, 8, 8, 64))
        BC = [T(f"BC{i}", (128, 8, 512)) for i in range(4)]
        vcol, vcol_b = T("vcol", (128, 4, 64))
        skc, skc_b = T("skc", (128, 64))
        ycol, ycol_b = T("ycol", (128, 4, 64))
        with nc.allow_non_contiguous_dma(reason="per-step v columns"):
            for t in range(DS):
                for bh in range(2):
                    for bq in range(8):
                        src = bass.AP(tensor=RS.tensor, offset=5 * 64 * 512 + (t * 16 + 2 * bq + bh) * 512, ap=[[1, 64], [64, 8]])
                        k.dma("sp", vcol[bh * 64:(bh + 1) * 64, t, bq * 8:(bq + 1) * 8], src, reads=[rsb], writes=[vcol_b])
        for bh in range(2):
            for bq in range(8):
                k.dma("sp", Ssm[bh * 64:(bh + 1) * 64, bq, :, :], state_wkv[2 * bq + bh].rearrange("h v k -> v h k"), writes=[Ssm_b])
        S2 = Ssm[:].rearrange("p a b c -> p (a b c)")
        S3 = Ssm[:].rearrange("p a b c -> p (a b) c")
        A2 = tA[:].rearrange("p a b c -> p (a b c)")
        A3 = tA[:].rearrange("p a b c -> p (a b) c")
        B2 = tB[:].rearrange("p a b c -> p (a b c)")
        B3 = tB[:].rearrange("p a b c -> p (a b) c")
        for t in range(DS):
            def bload(xi, slot):
                for bh in range(2):
                    src = bass.AP(tensor=RS.tensor, offset=xi * 64 * 512 + (t * 16 + bh) * 512, ap=[[0, 64], [1024, 8], [1, 512]])
                    k.dma("sp", BC[slot][0][bh * 64:(bh + 1) * 64, :, :], src, reads=[rsb], writes=[BC[slot][1]])

            for xi in range(4):
                bload(xi, xi)
            bcf = [BC[i][0][:].rearrange("p a b -> p (a b)") for i in range(4)]
            bcb = [BC[i][1] for i in range(4)]
            dve(lambda e: e.tensor_tensor(out=A2, in0=S2, in1=bcf[0], op=ALU.mult), [Ssm_b, bcb[0]], [tA_b])
            bload(4, 0)
            dve(lambda e: e.tensor_reduce(out=skc[:], in_=A3, axis=AX.X, op=ALU.add), [tA_b], [skc_b])
            dve(lambda e: e.tensor_tensor(out=S2, in0=S2, in1=bcf[1], op=ALU.mult), [Ssm_b, bcb[1]], [Ssm_b])
            dve(lambda e: e.tensor_tensor(out=B3, in0=BC[2][0][:].rearrange("p a (b c) -> p (a b) c", c=64),
                                          in1=skc[:].unsqueeze(2).to_broadcast([128, 64, 64]), op=ALU.mult), [bcb[2], skc_b], [tB_b])
            dve(lambda e: e.tensor_tensor(out=S2, in0=S2, in1=B2, op=ALU.subtract), [Ssm_b, tB_b], [Ssm_b])
            dve(lambda e: e.tensor_tensor(out=A3, in0=BC[3][0][:].rearrange("p a (b c) -> p (a b) c", c=64),
                                          in1=vcol[:, t, :].unsqueeze(2).to_broadcast([128, 64, 64]), op=ALU.mult), [bcb[3], vcol_b], [tA_b])
            dve(lambda e: e.tensor_tensor(out=S2, in0=S2, in1=A2, op=ALU.add), [Ssm_b, tA_b], [Ssm_b])
            dve(lambda e: e.tensor_tensor(out=B2, in0=S2, in1=bcf[0], op=ALU.mult), [Ssm_b, bcb[0]], [tB_b])
            dve(lambda e: e.tensor_reduce(out=ycol[:, t, :], in_=B3, axis=AX.X, op=ALU.add), [tB_b], [ycol_b])
        for bh in range(2):
            for bq in range(8):
                k.dma("sp", s_wkv[2 * bq + bh].rearrange("h v k -> v h k"), Ssm[bh * 64:(bh + 1) * 64, bq, :, :], reads=[Ssm_b])
        with nc.allow_non_contiguous_dma(reason="per-step y columns"):
            for t in range(DS):
                for bh in range(2):
                    for bq in range(8):
                        dst = bass.AP(tensor=RS.tensor, offset=6 * 64 * 512 + (t * 16 + 2 * bq + bh) * 512, ap=[[1, 64], [64, 8]])
                        k.dma("sp", dst, ycol[bh * 64:(bh + 1) * 64, t, bq * 8:(bq + 1) * 8], reads=[ycol_b], writes=[rsb])
        dve(lambda e: e.memset(yt[64:128, :], 0.0), [], [yt_b])
        k.dma("sp", yt[0:64, :], RS[6], reads=[rsb], writes=[yt_b])
        post(O[ZS0:ZS0 + 64, 512:1024], 64)

        k.barrier()


def t5_bucket_np(dist):
    import math
    n = np.maximum(dist, 0)
    nf = np.maximum(n, 1).astype(np.float32)
    large = 16 + (np.log(nf / np.float32(16)) / np.float32(math.log(8.0)) * np.float32(16)).astype(np.int32)
    return np.where(n < 16, n, np.minimum(large, 31)).astype(np.int64)


NEG = -30000.0


def nsa_host_consts():
    f = np.float32
    out = {}
    u = np.arange(768)
    dist = u - 127
    ok = (dist >= 0) & (dist <= 512)
    ohg = np.zeros((33, 768), f)
    ohg[t5_bucket_np(dist)[ok], u[ok]] = 1
    ohg[32, ~ok] = NEG
    out["nsa_ohg"] = ohg
    i = np.arange(128)
    ohc = np.zeros((33, 7, 128), f)
    ohc[31, 6, :] = 1
    for p, dl in enumerate((1, 0, -1, -2, -5, 3)):
        d = i - 63 - 64 * dl
        okp = d >= 0
        ohc[t5_bucket_np(d)[okp], p, i[okp]] = 1
        ohc[32, p, ~okp] = NEG
    out["nsa_ohc"] = ohc
    out["nsa_J"] = np.ascontiguousarray(np.eye(128, dtype=f)[::-1])
    E = np.zeros((32, 16, 128), f)
    for kt in range(16):
        E[2 * kt, kt, :64] = 1
        E[2 * kt + 1, kt, 64:] = 1
    out["nsa_E"] = E
    ext = np.zeros((128, 3, 62), f)
    for c in range(62):
        dl = c - 30
        cur = (i >= 64).astype(np.int64)
        allowed = dl <= cur
        forced = (dl == cur) | (dl == cur - 1)
        ext[:, 0, c] = (allowed & ~forced)
        ext[:, 1, c] = np.where(forced & allowed, 1e4, np.where(~allowed, -1e4, 0.0))
        ext[:, 2, c] = allowed
    out["nsa_ext"] = ext
    out["nsa_bm"] = (np.arange(128)[:, None] // 32 == np.arange(4)[None, :]).astype(f)
    return out


def nsa_prompt_phase(k, nc, W, Z, O, cin, GD):
    with contextlib.ExitStack() as es3:
        sbt = lambda n, s, d=F32: es3.enter_context(nc.sbuf_tensor(k.uniq(n), list(s), d))
        pp = PsumPool(k, nc, es3, 4)
        acc_banks = PsumPool(k, nc, es3, 4)

        def act(fn, r, w):
            return k.op("act", fn, reads=r, writes=w)

        def dve(fn, r, w):
            return k.op("dve", fn, reads=r, writes=w)

        def pool(fn, r, w):
            return k.op("pool", fn, reads=r, writes=w)

        def pe(fn, r, w):
            return k.op("pe", fn, reads=r, writes=w)

        cb = k.buf("nsaconst")
        IDF = sbt("IDF", [128, 128])
        k.dma("sp", IDF[:], cin["ident"], writes=[cb])
        IDB = sbt("IDB", [128, 128], BF16)
        dve(lambda e: e.tensor_copy(out=IDB[:], in_=IDF[:]), [cb], [cb])
        tab33 = sbt("tab33", [33, 8])
        k.dma("sp", tab33[0:32, :], W["rel_bias_table"], writes=[cb])
        dve(lambda e: e.memset(tab33[32:33, :], 1.0), [], [cb])
        OHG = sbt("OHG", [33, 768])
        k.dma("sp", OHG[:], cin["nsa_ohg"], writes=[cb])
        OHC = sbt("OHC", [33, 7, 128])
        k.dma("sp", OHC[:], cin["nsa_ohc"], writes=[cb])
        JM = sbt("JM", [128, 128])
        k.dma("sp", JM[:], cin["nsa_J"], writes=[cb])
        Ef = sbt("Ef", [32, 16, 128])
        k.dma("sp", Ef[:], cin["nsa_E"], writes=[cb])
        Eb = sbt("Eb", [64, 16, 128], BF16)
        dve(lambda e: e.memset(Eb[:], 0.0), [], [cb])
        dve(lambda e: e.tensor_copy(out=Eb[0:32, :, :], in_=Ef[:]), [cb], [cb])
        BM = sbt("BM", [128, 4])
        k.dma("sp", BM[:], cin["nsa_bm"], writes=[cb])
        EXT = sbt("EXT", [128, 3, 62])
        k.dma("sp", EXT[:], cin["nsa_ext"], writes=[cb])
        W1s = sbt("W1s", [64, 64, 64])
        W1 = [W1s, W1s]
        W1_b = k.buf("W1")
        PE_ = [sbt("pek", [64, 64]), sbt("pev", [64, 64])]
        W2 = [sbt("W2k", [64, 64]), sbt("W2v", [64, 64])]
        for x, sfx in enumerate(("k", "v")):
            k.dma("sp", PE_[x][:], W["cmp_pe_" + sfx], writes=[cb])
            k.dma("sp", W2[x][:], W["cmp_w2_" + sfx], writes=[cb])
        hidpe = sbt("hidpe", [64, 2])
        PE2 = sbt("PE2", [64, 2, 64, 2])
        for x in range(2):
            dve(lambda e: e.tensor_copy(out=PE2[:, x, :, :], in_=PE_[x][:, :].unsqueeze(2).to_broadcast([64, 64, 2])), [cb], [cb])
        for x in range(2):
            k.dma("sp", W1s[:], W["cmp_w1_" + "kv"[x]], writes=[W1_b])
            b, b_b = pp.get()
            for d in range(64):
                pe(lambda e: e.matmul(b[0:64, 0:2], lhsT=W1[x][:, d, :], rhs=PE2[:, x, d, :], start=(d == 0), stop=(d == 63)), [cb, W1_b], [b_b])
            dve(lambda e: e.tensor_copy(out=hidpe[:, x:x + 1], in_=b[0:64, 0:1]), [b_b], [cb])
        b, b_b = pp.get()
        for p in range(7):
            pe(lambda e: e.matmul(b[:, p * 8:(p + 1) * 8], lhsT=OHC[:, p, :], rhs=tab33[:, :], start=True, stop=True), [cb], [b_b])
        pat = sbt("pat", [128, 7, 8])
        dve(lambda e: e.tensor_copy(out=pat[:].rearrange("p a b -> p (a b)"), in_=b[:, 0:56]), [b_b], [cb])
        tb31 = pat[:, 6, :]
        Gs8 = sbt("Gs8", [8, 768])
        for half in range(2):
            b, b_b = pp.get()
            pe(lambda e: e.matmul(b[0:8, 0:384], lhsT=tab33[:, :], rhs=OHG[:, half * 384:(half + 1) * 384], start=True, stop=True), [cb], [b_b])
            dve(lambda e: e.tensor_copy(out=Gs8[:, half * 384:(half + 1) * 384], in_=b[0:8, 0:384]), [b_b], [cb])
        gdb = k.buf("GD")
        k.dma("sp", GD, Gs8[:], reads=[cb], writes=[gdb])
        TB = sbt("TB", [128, 8, 3, 128])
        TBb = sbt("TBb", [128, 8, 3, 128], BF16)
        Hh = [sbt(f"Hh{i}", [128, 128]) for i in range(2)]
        Hh_b = [k.buf() for _ in range(2)]
        hi = 0
        for h in range(8):
            for ri, rho in enumerate((0, 128, 512)):
                j = hi % 2
                hi += 1
                src = bass.AP(tensor=GD.tensor, offset=h * 768 + rho, ap=[[1, 128], [1, 128]])
                k.dma("sp", Hh[j][:], src, reads=[gdb], writes=[Hh_b[j]])
                b, b_b = pp.get()
                pe(lambda e: e.matmul(b[:, 0:128], lhsT=JM[:], rhs=Hh[j][:], start=True, stop=True), [cb, Hh_b[j]], [b_b])
                act(lambda e: e.copy(out=TB[:, h, ri, :], in_=b[:, 0:128]), [b_b], [cb])
                if TB_SHIFT:
                    dve(lambda e: e.tensor_scalar(out=TB[:, h, ri, :], in0=TB[:, h, ri, :], scalar1=tb31[:, h:h + 1], scalar2=None, op0=ALU.subtract), [cb], [cb])
        GEXT = sbt("GEXT", [128, 8, 62])
        dve(lambda e: e.tensor_copy(out=GEXT[:, :, 0:28], in_=pat[:, 4, :].unsqueeze(2).to_broadcast([128, 8, 28])), [cb], [cb])
        for c, p in ((28, 3), (29, 2), (30, 1), (31, 0)):
            dve(lambda e: e.tensor_copy(out=GEXT[:, :, c], in_=pat[:, p, :]), [cb], [cb])
        dve(lambda e: e.tensor_copy(out=GEXT[:, :, 32:62], in_=pat[:, 5, :].unsqueeze(2).to_broadcast([128, 8, 30])), [cb], [cb])

        dve(lambda e: e.tensor_copy(out=TBb[:].rearrange("p a b c -> p (a b c)"), in_=TB[:].rearrange("p a b c -> p (a b c)")), [cb], [cb])
        k.barrier()
        tmpS = [sbt(f"tmpS{i}", [128, 256]) for i in range(2)]
        tmpS_b = [k.buf() for _ in range(2)]
        PT = [sbt(f"PT{i}", [128, 512], BF16) for i in range(3)]
        PT_b = [k.buf() for _ in range(3)]
        onr = sbt("onr", [128, 2, 4, 64])
        onr_b = k.buf("onr")
        qT = sbt("qT", [64, 8, S], BF16)
        qT_b = k.buf("qT")
        kT = sbt("kT", [64, 2, 2, S], BF16)
        kT_b = k.buf("kT")
        vaug = sbt("vaug", [128, 2, 16, 2, 66], BF16)
        vaug_b = k.buf("vaug")
        dve(lambda e: e.memset(vaug[:].rearrange("p a b c d -> p (a b c) d")[:, :, 64:65], 1.0), [], [vaug_b])
        gate_all = sbt("gate_all", [128, 16, 24])
        gate_b = k.buf("gate")
        selbT = sbt("selbT", [64, 2, S], BF16)
        selbT_b = k.buf("selbT")
        dve(lambda e: e.memset(selbT[:].rearrange("p a b -> p (a b)"), 0.0), [], [selbT_b])
        o_acc = sbt("o_acc", [128, 16, 512])
        oacc_b = [k.buf() for _ in range(16)]
        zrow = [sbt(f"zrow{i}", [128, NSA_COLS]) for i in range(2)]
        zrow_b = [k.buf() for _ in range(2)]
        Rst = sbt("Rst", [64, 32, 128])
        Rst_b = k.buf("Rst")
        Gs = sbt("Gs", [64, 64])
        Gs_b = k.buf("Gs")
        Gs4 = sbt("Gs4", [64, 2, 4, 32])
        Gs4_b = k.buf("Gs4")
        kccT = sbt("kccT", [64, 2, 32], BF16)
        kcc_b = k.buf("kcc")
        Vbd = sbt("Vbd", [128, 2, 256], BF16)
        Vbd_b = k.buf("Vbd")
        dve(lambda e: e.memset(Vbd[:], 0.0), [], [Vbd_b])
        Sc = sbt("Sc", [128, 4, 32])
        Sc_b = k.buf("Sc")
        Pn = sbt("Pn", [128, 4, 32])
        Pn_b = k.buf("Pn")
        Pnb = sbt("Pnb", [128, 128], BF16)
        Pnb_b = k.buf("Pnb")
        PTc = sbt("PTc", [128, 128], BF16)
        PTc_b = k.buf("PTc")
        st4 = sbt("st4", [128, 8, 4])
        st4_b = k.buf("st4")
        imp = sbt("imp", [128, 4, 32])
        imp_b = k.buf("imp")
        m8 = sbt("m8", [128, 2, 8])
        m8_b = k.buf("m8")
        selq = sbt("selq", [128, 32])
        selq_b = k.buf("selq")
        cnt = {"z": 0, "tmpS": 0, "PT": 0}

        for seq in range(PB):
            base = seq * S
            for ti in range(16):
                r0 = base + ti * 128
                tok = slice(ti * 128, (ti + 1) * 128)
                zi = cnt["z"] % 2
                cnt["z"] += 1
                zr, zr_b = zrow[zi], zrow_b[zi]
                k.dma("sp", zr[:], Z[r0:r0 + 128, 0:NSA_COLS], writes=[zr_b])
                for hb in range(2):
                    b, b_b = pp.get()
                    for hl in range(4):
                        h = hb * 4 + hl
                        pe(lambda e: e.transpose(out=b[0:64, hl * 128:(hl + 1) * 128], in_=zr[:, h * 64:(h + 1) * 64], identity=IDF[:]), [zr_b, cb], [b_b])
                    act(lambda e: e.activation(out=qT[:, hb * 4:hb * 4 + 4, tok], in_=b[0:64, :].rearrange("p (a b) -> p a b", b=128), func=AF.Copy, scale=0.125),
                        [b_b], [qT_b])
                b, b_b = pp.get()
                for ci, c0 in enumerate((768, 832, 1024, 1088)):
                    pe(lambda e: e.transpose(out=b[0:64, ci * 128:(ci + 1) * 128], in_=zr[:, c0:c0 + 64], identity=IDF[:]), [zr_b, cb], [b_b])
                dve(lambda e: e.tensor_copy(out=kT[:, :, :, tok], in_=b[0:64, :].rearrange("p (a g b) -> p a g b", a=2, g=2)), [b_b], [kT_b])
                veng = pool if VAUG_ENG == 0 else dve
                veng(lambda e: e.tensor_copy(out=vaug[:, 0, ti, :, 0:64], in_=zr[:, 896:1024].rearrange("p (g d) -> p g d", d=64)), [zr_b], [vaug_b])
                veng(lambda e: e.tensor_copy(out=vaug[:, 1, ti, :, 0:64], in_=zr[:, 1152:1280].rearrange("p (g d) -> p g d", d=64)), [zr_b], [vaug_b])
                act(lambda e: e.activation(out=gate_all[:, ti, :], in_=zr[:, 1280:1304], func=AF.Sigmoid), [zr_b], [gate_b])
            if S_BARRIERS:
                k.barrier()
            for x in range(2 if (NSA_PARTS >= 2 and not S_SKIP23) else 0):
                k.dma("sp", Rst[:], Z[base:base + S, 512 + 128 * x:640 + 128 * x].rearrange("(b c) x -> c b x", c=64), writes=[Rst_b])
                k.dma("sp", W1s[:], W["cmp_w1_" + "kv"[x]], writes=[W1_b])
                b, b_b = pp.get()
                Rv = Rst[:].rearrange("c b (g d) -> c g b d", d=64)
                for d in range(64):
                    pe(lambda e: e.matmul(b[0:64, 0:64], lhsT=W1[x][:, d, :], rhs=Rv[:, :, :, d], start=(d == 0), stop=(d == 63)), [Rst_b, cb, W1_b], [b_b])
                act(lambda e: e.activation(out=Gs[:], in_=b[0:64, 0:64], func=AF.Gelu_apprx_tanh, bias=hidpe[:, x:x + 1]), [b_b, cb], [Gs_b])
                if x == 0:
                    b2, b2_b = pp.get()
                    pe(lambda e: e.matmul(b2[0:64, 0:64], lhsT=W2[0][:], rhs=Gs[:], start=True, stop=True), [Gs_b, cb], [b2_b])
                    dve(lambda e: e.tensor_copy(out=kccT[:].rearrange("p g b -> p (g b)"), in_=b2[0:64, 0:64]), [b2_b], [kcc_b])
                else:
                    dve(lambda e: e.tensor_copy(out=Gs4[:], in_=Gs[:].rearrange("p (g b) -> p g b", b=32).unsqueeze(2).to_broadcast([64, 2, 4, 32])), [Gs_b], [Gs4_b])
                    b2, b2_b = pp.get()
                    for g in range(2):
                        pe(lambda e: e.matmul(b2[:, g * 64:(g + 1) * 64], lhsT=Gs4[:, g, :, :].rearrange("p a b -> p (a b)"), rhs=W2[1][:], start=True, stop=True),
                           [Gs4_b, cb], [b2_b])
                    for g in range(2):
                        dve(lambda e: e.tensor_tensor(out=Vbd[:, g, :].rearrange("p (a b) -> p a b", b=64),
                                                      in0=b2[:, g * 64:(g + 1) * 64].unsqueeze(1).to_broadcast([128, 4, 64]),
                                                      in1=BM[:, :].unsqueeze(2).to_broadcast([128, 4, 64]), op=ALU.mult), [b2_b, cb], [Vbd_b])
            if S_BARRIERS:
                k.barrier()
            for ti in range(16 if (NSA_PARTS >= 3 and not S_SKIP23) else 0):
                tok = slice(ti * 128, (ti + 1) * 128)
                c0 = 30 - 2 * ti
                for g in range(2):
                    b, b_b = pp.get()
                    for r in range(4):
                        pe(lambda e: e.matmul(b[:, r * 32:(r + 1) * 32], lhsT=qT[:, 4 * g + r, tok], rhs=kccT[:, g, :], start=True, stop=True), [qT_b, kcc_b], [b_b])
                    dve(lambda e: e.tensor_tensor(out=Sc[:], in0=b[:, 0:128].rearrange("p (a b) -> p a b", b=32), in1=GEXT[:, 4 * g:4 * g + 4, c0:c0 + 32], op=ALU.add),
                        [b_b, cb], [Sc_b])
                    dve(lambda e: e.tensor_reduce(out=st4[:, 0, :], in_=Sc[:], axis=AX.X, op=ALU.max), [Sc_b], [st4_b])
                    dve(lambda e: e.tensor_scalar(out=st4[:, 0, :], in0=st4[:, 0, :], scalar1=-100.0, scalar2=-1.0, op0=ALU.max, op1=ALU.mult), [st4_b], [st4_b])
                    for r in range(4):
                        act(lambda e: e.activation(out=Pn[:, r, :], in_=Sc[:, r, :], func=AF.Exp, bias=st4[:, 0, r:r + 1], accum_out=st4[:, 1, r:r + 1]),
                            [Sc_b, st4_b], [Pn_b, st4_b])
                    dve(lambda e: e.tensor_scalar(out=st4[:, 1, :], in0=st4[:, 1, :], scalar1=1e-30, scalar2=None, op0=ALU.max), [st4_b], [st4_b])
                    dve(lambda e: e.reciprocal(out=st4[:, 1, :], in_=st4[:, 1, :]), [st4_b], [st4_b])
                    dve(lambda e: e.tensor_tensor(out=Pn[:], in0=Pn[:], in1=st4[:, 1, :].unsqueeze(2).to_broadcast([128, 4, 32]), op=ALU.mult), [Pn_b, st4_b], [Pn_b])
                    dve(lambda e: e.tensor_reduce(out=imp[:, 0, :], in_=Pn[:].rearrange("p r b -> p b r"), axis=AX.X, op=ALU.add), [Pn_b], [imp_b])
                    dve(lambda e: e.tensor_tensor(out=imp[:, 1, :], in0=imp[:, 0, :], in1=EXT[:, 0, c0:c0 + 32], op=ALU.mult), [imp_b, cb], [imp_b])
                    dve(lambda e: e.tensor_tensor(out=imp[:, 1, :], in0=imp[:, 1, :], in1=EXT[:, 1, c0:c0 + 32], op=ALU.add), [imp_b, cb], [imp_b])
                    dve(lambda e: e.memset(imp[:, 1, 0:1], 1e4), [imp_b], [imp_b])
                    dve(lambda e: e.max(out=m8[:, 0, :], in_=imp[:, 1, :]), [imp_b], [m8_b])
                    dve(lambda e: e.match_replace(out=imp[:, 2, :], in_to_replace=m8[:, 0, :], in_values=imp[:, 1, :], imm_value=-3e4), [imp_b, m8_b], [imp_b])
                    dve(lambda e: e.max(out=m8[:, 1, :], in_=imp[:, 2, :]), [imp_b], [m8_b])
                    dve(lambda e: e.tensor_scalar(out=imp[:, 3, :], in0=imp[:, 1, :], scalar1=m8[:, 1, 7:8], scalar2=None, op0=ALU.is_ge), [imp_b, m8_b], [imp_b])
                    dve(lambda e: e.tensor_tensor(out=imp[:, 3, :], in0=imp[:, 3, :], in1=EXT[:, 2, c0:c0 + 32], op=ALU.mult), [imp_b, cb], [imp_b])
                    dve(lambda e: e.tensor_scalar(out=selq[:], in0=imp[:, 3, :], scalar1=-1.0, scalar2=-NEG, op0=ALU.add, op1=ALU.mult), [imp_b], [selq_b])
                    b2, b2_b = pp.get()
                    pe(lambda e: e.transpose(out=b2[0:32, 0:128], in_=selq[:], identity=IDF[:]), [selq_b, cb], [b2_b])
                    act(lambda e: e.copy(out=selbT[0:32, g, tok], in_=b2[0:32, 0:128]), [b2_b], [selbT_b])
                    dve(lambda e: e.tensor_copy(out=Pnb[:], in_=Pn[:].rearrange("p a b -> p (a b)")), [Pn_b], [Pnb_b])
                    b3, b3_b = pp.get()
                    b3h = b3[:].bitcast(BF16)
                    pe(lambda e: e.transpose(out=b3h[:, 0:128], in_=Pnb[:], identity=IDB[:]), [Pnb_b, cb], [b3_b])
                    act(lambda e: e.copy(out=PTc[:], in_=b3h[:, 0:128]), [b3_b], [PTc_b])
                    b4, b4_b = pp.get()
                    pe(lambda e: e.matmul(b4[:, 0:256], lhsT=PTc[:], rhs=Vbd[:, g, :], start=True, stop=True), [PTc_b, Vbd_b], [b4_b])
                    dve(lambda e: e.tensor_tensor(out=o_acc[:, ti, g * 256:(g + 1) * 256].rearrange("p (a b) -> p a b", b=64),
                                                  in0=b4[:, 0:256].rearrange("p (a b) -> p a b", b=64),
                                                  in1=gate_all[:, ti, 4 * g:4 * g + 4].unsqueeze(2).to_broadcast([128, 4, 64]), op=ALU.mult),
                        [b4_b, gate_b], [oacc_b[ti]])
            if S_BARRIERS:
                k.barrier()
            for qt in range(S4_QT if NSA_PARTS >= 4 else 0):
                q0 = qt * 512
                for h in range(S4_H):
                    g = h // 4
                    accs = [acc_banks.get(), acc_banks.get()]
                    first = [True, True]
                    for kt in range(0, 4 * qt + 4):
                        for br in range(2):
                            if not (S4_BR >> br) & 1:
                                continue
                            jbs = []
                            for jb in range(4):
                                rho = q0 + 128 * jb - 128 * kt
                                if rho < 0 or (br == 1 and rho > 512):
                                    continue
                                jbs.append((jb, rho))
                            if not jbs:
                                continue
                            jlo, jhi = jbs[0][0] * 128, jbs[-1][0] * 128 + 128
                            b, b_b = pp.get()
                            qs = slice(q0 + jlo, q0 + jhi)
                            pe(lambda e: e.matmul(b[:, jlo:jhi], lhsT=kT[:, br, g, kt * 128:(kt + 1) * 128], rhs=qT[:, h, qs], start=True, stop=(br == 1)),
                               [kT_b, qT_b], [b_b])
                            if br == 0 and S4_STEP >= 2:
                                pe(lambda e: e.matmul(b[:, jlo:jhi], lhsT=Eb[:, kt, :], rhs=selbT[:, g, qs], start=False, stop=True), [selbT_b, cb], [b_b])
                            pi = cnt["PT"] % 3
                            cnt["PT"] += 1
                            near = [(jb, rho) for jb, rho in jbs if rho in (0, 128) or (br == 1 and rho == 512)]
                            for ni, (jb, rho) in enumerate(near):
                                ri = {0: 0, 128: 1, 512: 2}[rho]
                                pe(lambda e: e.matmul(b[:, jb * 128:(jb + 1) * 128], lhsT=IDB[:], rhs=TBb[:, h, ri, :], start=False, stop=True, skip_group_check=True),
                                   [cb], [b_b])
                            act(lambda e: e.activation(out=PT[pi][:, jlo:jhi], in_=b[:, jlo:jhi], func=AF.Exp), [b_b], [PT_b[pi]])
                            ab, ab_b = accs[br]
                            for jb, rho in (jbs if S4_STEP >= 4 else []):
                                pe(lambda e: e.matmul(ab[:, jb * 65:(jb + 1) * 65], lhsT=PT[pi][:, jb * 128:(jb + 1) * 128], rhs=vaug[:, br, kt, g, 0:65],
                                                      start=(first[br] or bool(PV_PLAIN)), stop=True, skip_group_check=True), [PT_b[pi], vaug_b], [ab_b])
                                first[br] = False
                    for br in range(2):
                        if not ((S4_BR >> br) & 1) or not S4_NORM:
                            continue
                        ab, ab_b = accs[br]
                        a3 = ab[:, 0:260].rearrange("p (a b) -> p a b", b=65)
                        dve(lambda e: e.tensor_scalar(out=st4[:, 2 + br, :], in0=a3[:, :, 64], scalar1=1e-30, scalar2=None, op0=ALU.max), [ab_b], [st4_b])
                        dve(lambda e: e.reciprocal(out=st4[:, 2 + br, :], in_=st4[:, 2 + br, :]), [st4_b], [st4_b])
                        dve(lambda e: e.tensor_tensor(out=st4[:, 2 + br, :], in0=st4[:, 2 + br, :], in1=gate_all[:, 4 * qt:4 * qt + 4, 8 * (1 + br) + h], op=ALU.mult),
                            [st4_b, gate_b], [st4_b])
                        dve(lambda e: e.tensor_tensor(out=onr[:, br, :, :], in0=a3[:, :, 0:64], in1=st4[:, 2 + br, :].unsqueeze(2).to_broadcast([128, 4, 64]), op=ALU.mult),
                            [ab_b, st4_b], [onr_b])
                        oa = o_acc[:, 4 * qt:4 * qt + 4, h * 64:(h + 1) * 64]
                        pool(lambda e: e.tensor_tensor(out=oa, in0=oa, in1=onr[:, br, :, :], op=ALU.add), [onr_b] + oacc_b[4 * qt:4 * qt + 4], oacc_b[4 * qt:4 * qt + 4])
            k.dma("sp", O[base:base + S, 0:512].rearrange("(t p) c -> p t c", p=128), o_acc[:], reads=oacc_b)
        k.barrier()


def nsa_sample_host_consts():
    f = np.float32
    p = np.arange(128)
    ohs = np.zeros((33, 4, 4, 128), f)
    for t in range(4):
        dists = [8193 + t - 64 * (p + 1), 128 + t - p, np.where(p < 4, t - p, -1), np.where(p >= t, 512 + t - p, -1)]
        for ti, d in enumerate(dists):
            ok = d >= 0
            ohs[t5_bucket_np(d)[ok], ti, t, p[ok]] = 1
            ohs[32, ti, t, ~ok] = NEG
    E2 = np.zeros((128, 64, 128), f)
    for kt in range(64):
        E2[2 * kt, kt, :64] = 1
        E2[2 * kt + 1, kt, 64:] = 1
    return {"nss_ohs": ohs, "nss_E2": E2}


def nsa_sample_phase(k, nc, W, Z, O, cin, pools, page_table, cache_win, PAST, OBR):
    ZS0 = PB * S
    with contextlib.ExitStack() as es3:
        sbt = lambda n, s, d=F32: es3.enter_context(nc.sbuf_tensor(k.uniq(n), list(s), d))
        pp = PsumPool(k, nc, es3, 5)
        accp = PsumPool(k, nc, es3, 3)
        act = lambda fn, r, w: k.op("act", fn, reads=r, writes=w)
        dve = lambda fn, r, w: k.op("dve", fn, reads=r, writes=w)
        pe = lambda fn, r, w: k.op("pe", fn, reads=r, writes=w)
        cb = k.buf("nssconst")
        IDF = sbt("IDF", [128, 128])
        k.dma("sp", IDF[:], cin["ident"], writes=[cb])
        IDB = sbt("IDB", [128, 128], BF16)
        dve(lambda e: e.tensor_copy(out=IDB[:], in_=IDF[:]), [cb], [cb])
        ONESF = sbt("ONESF", [128, 128])
        dve(lambda e: e.memset(ONESF[:], 1.0), [], [cb])
        tab33 = sbt("tab33", [33, 8])
        k.dma("sp", tab33[0:32, :], W["rel_bias_table"], writes=[cb])
        dve(lambda e: e.memset(tab33[32:33, :], 1.0), [], [cb])
        OHS = sbt("OHS", [33, 4, 4, 128])
        k.dma("sp", OHS[:], cin["nss_ohs"], writes=[cb])
        OH31 = sbt("OH31", [33, 128])
        dve(lambda e: e.memset(OH31[:], 0.0), [], [cb])
        k.barrier()
        dve(lambda e: e.memset(OH31[0:32, :], 0.0), [], [cb])
        k.dma("sp", OH31[0:33, :], cin["nsa_ohc"][:, 6, :], writes=[cb])
        k.barrier()
        b, b_b = pp.get()
        pe(lambda e: e.matmul(b[:, 0:8], lhsT=OH31[:, :], rhs=tab33[:, :], start=True, stop=True), [cb], [b_b])
        tb31 = sbt("tb31s", [128, 8])
        dve(lambda e: e.tensor_copy(out=tb31[:], in_=b[:, 0:8]), [b_b], [cb])
        BT = sbt("BT", [128, 4, 32], BF16)
        BTf = sbt("BTf", [128, 4, 32])
        for ti in range(4):
            b, b_b = pp.get()
            for t in range(4):
                pe(lambda e: e.matmul(b[:, t * 8:(t + 1) * 8], lhsT=OHS[:, ti, t, :], rhs=tab33[:, :], start=True, stop=True), [cb], [b_b])
            dve(lambda e: e.tensor_tensor(out=BTf[:, ti, :].rearrange("p (h t) -> p t h", t=4), in0=b[:, 0:32].rearrange("p (t h) -> p t h", h=8),
                                          in1=tb31[:, :].unsqueeze(1).to_broadcast([128, 4, 8]), op=ALU.subtract), [b_b, cb], [cb])
        dve(lambda e: e.tensor_copy(out=BT[:].rearrange("p a b -> p (a b)"), in_=BTf[:].rearrange("p a b -> p (a b)")), [cb], [cb])
        E2 = sbt("E2", [128, 64, 128], BF16)
        with contextlib.ExitStack() as est:
            E2f = est.enter_context(nc.sbuf_tensor(k.uniq("E2f"), [128, 2048], F32))
            for q4 in range(4):
                k.dma("sp", E2f[:], cin["nss_E2"][:, q4 * 16:(q4 + 1) * 16, :].rearrange("p a b -> p (a b)"), writes=[cb])
                k.barrier()
                dve(lambda e: e.tensor_copy(out=E2[:, q4 * 16:(q4 + 1) * 16, :].rearrange("p a b -> p (a b)"), in_=E2f[:]), [cb], [cb])
                k.barrier()
        W1s = sbt("W1s", [64, 64, 64])
        W1_b = k.buf("W1")
        PE_ = [sbt("pek", [64, 64]), sbt("pev", [64, 64])]
        W2 = [sbt("W2k", [64, 64]), sbt("W2v", [64, 64])]
        for x, sfx in enumerate(("k", "v")):
            k.dma("sp", PE_[x][:], W["cmp_pe_" + sfx], writes=[cb])
            k.dma("sp", W2[x][:], W["cmp_w2_" + sfx], writes=[cb])
        PE2 = sbt("PE2", [64, 2, 64, 2])
        hidpe = sbt("hidpe", [64, 2])
        k.barrier()
        for x in range(2):
            dve(lambda e: e.tensor_copy(out=PE2[:, x, :, :], in_=PE_[x][:, :].unsqueeze(2).to_broadcast([64, 64, 2])), [cb], [cb])
            k.dma("sp", W1s[:], W["cmp_w1_" + "kv"[x]], writes=[W1_b])
            b, b_b = pp.get()
            for d in range(64):
                pe(lambda e: e.matmul(b[0:64, 0:2], lhsT=W1s[:, d, :], rhs=PE2[:, x, d, :], start=(d == 0), stop=(d == 63)), [cb, W1_b], [b_b])
            dve(lambda e: e.tensor_copy(out=hidpe[:, x:x + 1], in_=b[0:64, 0:1]), [b_b], [cb])
        pti = sbt("pti", [64, 16], I32)
        with nc.allow_non_contiguous_dma(reason="page table transpose"):
            k.dma("sp", pti[:], page_table.rearrange("b p -> p b"), writes=[cb])
        k.barrier()
        ptf = sbt("ptf", [64, 5, 16])
        idx = sbt("idx", [64, 4, 16], I32)
        dve(lambda e: e.tensor_copy(out=ptf[:, 4, :], in_=pti[:]), [cb], [cb])
        for qq in range(4):
            dve(lambda e: e.tensor_scalar(out=ptf[:, qq, :], in0=ptf[:, 4, :], scalar1=4.0, scalar2=float(qq), op0=ALU.mult, op1=ALU.add), [cb], [cb])
        dve(lambda e: e.tensor_copy(out=idx[:].rearrange("p a b -> p (a b)"), in_=ptf[:, 0:4, :].rearrange("p a b -> p (a b)")), [cb], [cb])
        zs = sbt("zs", [64, NSA_COLS])
        k.dma("sp", zs[:], Z[ZS0:ZS0 + 64, 0:NSA_COLS], writes=[cb])
        k.barrier()
        qTs = sbt("qTs", [64, 8, 64], BF16)
        b, b_b = pp.get()
        for h in range(8):
            pe(lambda e: e.transpose(out=b[0:64, h * 64:(h + 1) * 64], in_=zs[:, h * 64:(h + 1) * 64], identity=IDF[0:64, 0:64]), [cb], [b_b])
        act(lambda e: e.activation(out=qTs[:].rearrange("p a b -> p (a b)"), in_=b[0:64, :], func=AF.Copy, scale=0.125), [b_b], [cb])
        kTn = sbt("kTn", [64, 2, 2, 64], BF16)
        b, b_b = pp.get()
        for ci, c0 in enumerate((768, 832, 1024, 1088)):
            pe(lambda e: e.transpose(out=b[0:64, ci * 64:(ci + 1) * 64], in_=zs[:, c0:c0 + 64], identity=IDF[0:64, 0:64]), [cb], [b_b])
        dve(lambda e: e.tensor_copy(out=kTn[:].rearrange("p a g b -> p (a g b)"), in_=b[0:64, 0:256]), [b_b], [cb])
        Vnf = sbt("Vnf", [4, 2, 16, 128])
        for a, c0 in enumerate((896, 1152)):
            k.dma("sp", Vnf[:, a, :, :], Z[ZS0:ZS0 + 64, c0:c0 + 128].rearrange("(t b) c -> t b c", b=16), writes=[cb])
        Vn = sbt("Vn", [4, 2, 16, 2, 66], BF16)
        dve(lambda e: e.memset(Vn[:].rearrange("p a b c d -> p (a b c d)"), 1.0), [], [cb])
        k.barrier()
        dve(lambda e: e.tensor_copy(out=Vn[:, :, :, :, 0:64].rearrange("p a b g d -> p (a b) g d"), in_=Vnf[:].rearrange("p a b (g d) -> p (a b) g d", d=64)), [cb], [cb])
        gates = sbt("gates", [64, 24])
        act(lambda e: e.activation(out=gates[:], in_=zs[:, 1280:1304], func=AF.Sigmoid), [cb], [cb])
        k.barrier()

        Pg = sbt("Pg", [64, 2, 4096])
        Pg_b = [k.buf("Pg0"), k.buf("Pg1")]
        past_b = [k.buf("past0"), k.buf("past1")]
        Gs = sbt("Gs", [64, 2, 128])
        Gs_b = k.buf("Gs")
        kccT = sbt("kccT", [64, 2, 128], BF16)
        kcc_b = k.buf("kcc")
        vcc = sbt("vcc", [128, 2, 66], BF16)
        vcc_b = k.buf("vcc")
        dve(lambda e: e.memset(vcc[:].rearrange("p a b -> p (a b)"), 1.0), [], [vcc_b])
        rows = sbt("rows", [128, 64, 128])
        rows_b = k.buf("rows")
        Rst = rows[0:64, :, :]
        Rst_b = rows_b
        kTs = sbt("kTs", [64, 2, 8192], BF16)
        kTs_b = k.buf("kTs")
        vas = sbt("vas", [128, 64, 2, 66], BF16)
        vas_b = k.buf("vas")
        dve(lambda e: e.memset(vas[:].rearrange("p a b c -> p (a b c)"), 1.0), [], [vas_b])
        Ef = sbt("Ef", [128, 32])
        Ef_b = k.buf("Ef")
        PTs = [sbt(f"PTs{i}", [128, 32], BF16) for i in range(3)]
        PTs_b = [k.buf() for _ in range(3)]
        impT = sbt("impT", [128, 16, 8])
        impT_b = k.buf("impT")
        selT = sbt("selT", [128, 16, 2, 4, 4], BF16)
        selT_b = k.buf("selT")
        osb = sbt("osb", [16, 3, 2, 64])
        osb_b = [k.buf("osb") for _ in range(3)]
        obr_b = k.buf("obr")
        st = sbt("st", [128, 8])
        st_b = k.buf("st")
        cnt = {"pt": 0, "past": 0}

        def gather_past(pool_ap, bq):
            slot = cnt["past"] % 2
            cnt["past"] += 1
            dst = PAST[slot].rearrange("(pg h c) x -> pg h (c x)", h=4, c=32)
            for hh in range(4):
                k.dma("pool", Pg[:, hh % 2, :], pool_ap, indirect=bass.IndirectOffsetOnAxis(ap=idx[:, hh, bq:bq + 1], axis=0), reads=[cb], writes=[Pg_b[hh % 2]])
                k.dma("sp", dst[:, hh, :], Pg[:, hh % 2, :], reads=[Pg_b[hh % 2]], writes=[past_b[slot]])
            return slot

        def q_cols(bq, g):
            return qTs[:, 4 * g:4 * g + 4, bq:64:16]

        def attend_tile(bq, nk, kT_of_g, v_of_g, bias_ti, sel_kt, acc, first):
            ps, ps_b = pp.get()
            for g in range(2):
                pe(lambda e: e.matmul(ps[:nk, g * 16:(g + 1) * 16], lhsT=kT_of_g(g), rhs=q_cols(bq, g), start=(g == 0), stop=True, skip_group_check=True),
                   [kTs_b, kcc_b, cb], [ps_b])
            if bias_ti is not None:
                pe(lambda e: e.matmul(ps[:nk, 0:32], lhsT=IDB[:nk, :nk], rhs=BT[:nk, bias_ti, :], start=False, stop=True, skip_group_check=True), [cb], [ps_b])
            if sel_kt is not None:
                pe(lambda e: e.matmul(ps[:nk, 0:32], lhsT=E2[:, sel_kt, :], rhs=selT[:, bq, :, :, :].rearrange("p g r t -> p (g r t)"), start=False, stop=True,
                                      skip_group_check=True), [selT_b, cb], [ps_b])
            pi = cnt["pt"] % 3
            cnt["pt"] += 1
            act(lambda e: e.activation(out=PTs[pi][:nk, :], in_=ps[:nk, 0:32], func=AF.Exp), [ps_b], [PTs_b[pi]])
            ab, ab_b = acc
            for g in range(2):
                pe(lambda e: e.matmul(ab[0:16, g * 65:(g + 1) * 65], lhsT=PTs[pi][:nk, g * 16:(g + 1) * 16], rhs=v_of_g(g), start=(first and g == 0), stop=True,
                                      skip_group_check=True), [PTs_b[pi], vas_b, vcc_b, cb], [ab_b])
            return ps, ps_b, pi

        def finish_branch(bq, br, acc):
            ab, ab_b = acc
            a3 = ab[0:16, 0:130].rearrange("p (g c) -> p g c", c=65)
            dve(lambda e: e.tensor_scalar(out=st[0:16, 0:2], in0=a3[:, :, 64], scalar1=1e-30, scalar2=None, op0=ALU.max), [ab_b], [st_b])
            dve(lambda e: e.reciprocal(out=st[0:16, 0:2], in_=st[0:16, 0:2]), [st_b], [st_b])
            dve(lambda e: e.tensor_tensor(out=osb[:, br, :, :], in0=a3[:, :, 0:64], in1=st[0:16, 0:2].unsqueeze(2).to_broadcast([16, 2, 64]), op=ALU.mult),
                [ab_b, st_b], [osb_b[br]])
            for g in range(2):
                for r in range(4):
                    dst = OBR[br].rearrange("(t b) (h d) -> t b h d", b=16, d=64)[:, bq, 4 * g + r, :]
                    k.dma("sp", dst, osb[r * 4:(r + 1) * 4, br, g, :], reads=[osb_b[br]], writes=[obr_b])

        def compress(slot, x):
            k.dma("sp", W1s[:], W["cmp_w1_" + "kv"[x]], writes=[W1_b])
            ph, ph_b = pp.get()
            for half in range(2):
                k.dma("sp", Rst, PAST[slot][half * 4096:(half + 1) * 4096, :].rearrange("(b c) x -> c b x", c=64), reads=[past_b[slot]], writes=[Rst_b])
                Rv = Rst.rearrange("c b (g d) -> c g b d", d=64)
                for g in range(2):
                    for d in range(64):
                        pe(lambda e: e.matmul(ph[0:64, g * 128 + half * 64:g * 128 + half * 64 + 64], lhsT=W1s[:, d, :], rhs=Rv[:, g, :, d],
                                              start=(d == 0 and half == 0 and g == 0), stop=True, skip_group_check=True) if False else
                           e.matmul(ph[0:64, g * 128 + half * 64:g * 128 + half * 64 + 64], lhsT=W1s[:, d, :], rhs=Rv[:, g, :, d], start=(d == 0), stop=(d == 63)),
                           [Rst_b, W1_b], [ph_b])
            act(lambda e: e.activation(out=Gs[:].rearrange("p a b -> p (a b)"), in_=ph[0:64, 0:256], func=AF.Gelu_apprx_tanh, bias=hidpe[:, x:x + 1]), [ph_b, cb], [Gs_b])
            p2, p2_b = pp.get()
            if x == 0:
                pe(lambda e: e.matmul(p2[0:64, 0:256], lhsT=W2[0][:], rhs=Gs[:].rearrange("p a b -> p (a b)"), start=True, stop=True), [Gs_b, cb], [p2_b])
                dve(lambda e: e.tensor_copy(out=kccT[:].rearrange("p a b -> p (a b)"), in_=p2[0:64, 0:256]), [p2_b], [kcc_b])
            else:
                for g in range(2):
                    pe(lambda e: e.matmul(p2[:, g * 64:(g + 1) * 64], lhsT=Gs[:, g, :], rhs=W2[1][:], start=True, stop=True), [Gs_b, cb], [p2_b])
                dve(lambda e: e.tensor_copy(out=vcc[:, :, 0:64], in_=p2[:, 0:128].rearrange("p (g d) -> p g d", d=64)), [p2_b], [vcc_b])

        for bq in range(SBT):
            for x in range(2):
                slot = gather_past(pools[x], bq)
                compress(slot, x)
            acc = accp.get()
            ps, ps_b, pi = attend_tile(bq, 128, lambda g: kccT[:, g, :], lambda g: vcc[:, g, 0:65], 0, None, acc, True)
            finish_branch(bq, 0, acc)
            act(lambda e: e.activation(out=Ef[:], in_=ps[:, 0:32], func=AF.Exp), [ps_b], [Ef_b])
            s2, s2_b = pp.get()
            pe(lambda e: e.matmul(s2[:, 0:32], lhsT=ONESF[:], rhs=Ef[:], start=True, stop=True), [Ef_b, cb], [s2_b])
            dve(lambda e: e.tensor_scalar(out=st[:, 0:0 + 8], in0=s2[:, 0:8], scalar1=1.0, scalar2=None, op0=ALU.mult), [s2_b], [st_b]) if False else None
            dve(lambda e: e.reciprocal(out=Ef[:, :], in_=Ef[:, :]) if False else e.tensor_tensor(out=Ef[:], in0=Ef[:], in1=s2[:, 0:32], op=ALU.divide), [Ef_b, s2_b], [Ef_b]) if False else None
            rc = sbt(f"rc{bq}", [128, 32])
            rc_b = k.buf()
            dve(lambda e: e.reciprocal(out=rc[:], in_=s2[:, 0:32]), [s2_b], [rc_b])
            dve(lambda e: e.tensor_tensor(out=Ef[:], in0=Ef[:], in1=rc[:], op=ALU.mult), [Ef_b, rc_b], [Ef_b])
            dve(lambda e: e.tensor_reduce(out=impT[:, bq, :].rearrange("p (g t) -> p g t", t=4), in_=Ef[:].rearrange("p (g r t) -> p g t r", g=2, r=4),
                                          axis=AX.X, op=ALU.add), [Ef_b], [impT_b])
        b, b_b = pp.get()
        pe(lambda e: e.transpose(out=b[:, 0:128], in_=impT[:].rearrange("p a b -> p (a b)"), identity=IDF[:]), [impT_b, cb], [b_b])
        imp = sbt("imp", [128, 4, 128])
        imp_b = k.buf("imp")
        m8 = sbt("m8", [128, 2, 8])
        dve(lambda e: e.tensor_copy(out=imp[:, 0, :], in_=b[:, 0:128]), [b_b], [imp_b])
        dve(lambda e: e.memset(imp[:, 0, 0:1], -1.0), [imp_b], [imp_b])
        dve(lambda e: e.memset(imp[:, 0, 127:128], -1.0), [imp_b], [imp_b])
        dve(lambda e: e.max(out=m8[:, 0, :], in_=imp[:, 0, :]), [imp_b], [imp_b])
        dve(lambda e: e.match_replace(out=imp[:, 1, :], in_to_replace=m8[:, 0, :], in_values=imp[:, 0, :], imm_value=-3.0), [imp_b], [imp_b])
        dve(lambda e: e.max(out=m8[:, 1, :], in_=imp[:, 1, :]), [imp_b], [imp_b])
        dve(lambda e: e.tensor_scalar(out=imp[:, 2, :], in0=imp[:, 0, :], scalar1=m8[:, 1, 4:5], scalar2=None, op0=ALU.is_ge), [imp_b], [imp_b])
        dve(lambda e: e.memset(imp[:, 2, 0:1], 1.0), [imp_b], [imp_b])
        dve(lambda e: e.memset(imp[:, 2, 127:128], 1.0), [imp_b], [imp_b])
        dve(lambda e: e.tensor_scalar(out=imp[:, 3, :], in0=imp[:, 2, :], scalar1=-1.0, scalar2=-NEG, op0=ALU.add, op1=ALU.mult), [imp_b], [imp_b])
        b, b_b = pp.get()
        pe(lambda e: e.transpose(out=b[:, 0:128], in_=imp[:, 3, :], identity=IDF[:]), [imp_b, cb], [b_b])
        dve(lambda e: e.tensor_copy(out=selT[:].rearrange("p b g r t -> p (b g) r t"),
                                    in_=b[:, 0:128].rearrange("p (a t) -> p a t", t=4).unsqueeze(2).to_broadcast([128, 32, 4, 4])), [b_b], [selT_b])
        for bq in range(SBT):
            for br, (pk, pv) in ((1, (pools[2], pools[3])),):
                slot = gather_past(pk, bq)
                k.dma("sp", rows[:], PAST[slot].rearrange("(pg r) x -> r pg x", r=128), reads=[past_b[slot]], writes=[rows_b])
                for pg4 in range(32):
                    b, b_b = pp.get()
                    for j in range(2):
                        for g in range(2):
                            pe(lambda e: e.transpose(out=b[0:64, (j * 2 + g) * 128:(j * 2 + g + 1) * 128], in_=rows[:, pg4 * 2 + j, g * 64:(g + 1) * 64], identity=IDF[:]),
                               [rows_b, cb], [b_b])
                    eng = act if pg4 % 2 == 0 else dve
                    if pg4 % 2 == 0:
                        act(lambda e: e.copy(out=kTs[:, :, pg4 * 256:(pg4 + 1) * 256].rearrange("p g (j k) -> p j g k", j=2), in_=b[0:64, :].rearrange("p (j g k) -> p j g k", j=2, g=2)),
                            [b_b], [kTs_b])
                    else:
                        dve(lambda e: e.tensor_copy(out=kTs[:, :, pg4 * 256:(pg4 + 1) * 256].rearrange("p g (j k) -> p j g k", j=2), in_=b[0:64, :].rearrange("p (j g k) -> p j g k", j=2, g=2)),
                            [b_b], [kTs_b])
                slot = gather_past(pv, bq)
                k.dma("sp", rows[:], PAST[slot].rearrange("(pg r) x -> r pg x", r=128), reads=[past_b[slot]], writes=[rows_b])
                for q4 in range(4):
                    dve(lambda e: e.tensor_copy(out=vas[:, q4 * 16:(q4 + 1) * 16, :, 0:64], in_=rows[:, q4 * 16:(q4 + 1) * 16, :].rearrange("p a (g d) -> p a g d", d=64)),
                        [rows_b], [vas_b])
                acc = accp.get()
                for kt in range(64):
                    attend_tile(bq, 128, lambda g: kTs[:, g, kt * 128:(kt + 1) * 128], lambda g: vas[:, kt, g, 0:65], 1 if kt == 63 else None, kt, acc, kt == 0)
                attend_tile(bq, 4, lambda g: kTn[:, 0, g, bq:64:16], lambda g: Vn[:, 0, bq, g, 0:65], 2, None, acc, False)
                finish_branch(bq, 1, acc)
            k.dma("sp", rows[:, 0:4, :], cache_win[0][bq].rearrange("(a r) x -> r a x", r=128), writes=[rows_b])
            b, b_b = pp.get()
            b2, b2_b = pp.get()
            for a in range(4):
                for g in range(2):
                    tgt = b if a < 2 else b2
                    pe(lambda e: e.transpose(out=tgt[0:64, ((a % 2) * 2 + g) * 128:((a % 2) * 2 + g + 1) * 128], in_=rows[:, a, g * 64:(g + 1) * 64], identity=IDF[:]),
                       [rows_b, cb], [b_b if a < 2 else b2_b])
            act(lambda e: e.copy(out=kTs[:, :, 0:256].rearrange("p g (j k) -> p j g k", j=2), in_=b[0:64, :].rearrange("p (j g k) -> p j g k", j=2, g=2)), [b_b], [kTs_b])
            act(lambda e: e.copy(out=kTs[:, :, 256:512].rearrange("p g (j k) -> p j g k", j=2), in_=b2[0:64, :].rearrange("p (j g k) -> p j g k", j=2, g=2)), [b2_b], [kTs_b])
            k.dma("sp", rows[:, 4:8, :], cache_win[1][bq].rearrange("(a r) x -> r a x", r=128), writes=[rows_b])
            dve(lambda e: e.tensor_copy(out=vas[:, 0:4, :, 0:64], in_=rows[:, 4:8, :].rearrange("p a (g d) -> p a g d", d=64)), [rows_b], [vas_b])
            acc = accp.get()
            for kt in range(4):
                attend_tile(bq, 128, lambda g: kTs[:, g, kt * 128:(kt + 1) * 128], lambda g: vas[:, kt, g, 0:65], {0: 3, 3: 1}.get(kt), None, acc, kt == 0)
            attend_tile(bq, 4, lambda g: kTn[:, 1, g, bq:64:16], lambda g: Vn[:, 1, bq, g, 0:65], 2, None, acc, False)
            finish_branch(bq, 2, acc)
        ob3 = rows[0:64, 0:12, :].rearrange("p (a c) x -> p a (c x)", a=3)
        ob3_b = rows_b
        k.dma("sp", ob3, OBR.rearrange("a t c -> t a c"), reads=[obr_b], writes=[ob3_b])
        for br in range(3):
            dve(lambda e: e.tensor_tensor(out=ob3[:, br, :].rearrange("p (h d) -> p h d", d=64), in0=ob3[:, br, :].rearrange("p (h d) -> p h d", d=64),
                                          in1=gates[:, 8 * br:8 * br + 8].unsqueeze(2).to_broadcast([64, 8, 64]), op=ALU.mult), [ob3_b, cb], [ob3_b])
        dve(lambda e: e.tensor_tensor(out=ob3[:, 0, :], in0=ob3[:, 0, :], in1=ob3[:, 1, :], op=ALU.add), [ob3_b], [ob3_b])
        dve(lambda e: e.tensor_tensor(out=ob3[:, 0, :], in0=ob3[:, 0, :], in1=ob3[:, 2, :], op=ALU.add), [ob3_b], [ob3_b])
        k.dma("sp", O[ZS0:ZS0 + 64, 0:512], ob3[:, 0, :], reads=[ob3_b])
        k.barrier()


def build_program():
    nc = bass.Bass("TRN2", target_bir_lowering=False)
    din = {}
    dout = {}

    def inp(name, shape, dt=F32):
        din[name] = nc.dram_tensor(name, shape, dt, kind="ExternalInput").ap()
        return din[name]

    def outp(name, shape):
        dout[name] = nc.dram_tensor(name, shape, F32, kind="ExternalOutput").ap()
        return dout[name]

    xp = inp("x_prompt", [PB * S, D])
    xsm = inp("x_sample", [NS_TOK, D])
    cache_win_k = inp("cache_win_k", [SBT, WIN, 128])
    cache_win_v = inp("cache_win_v", [SBT, WIN, 128])
    ident_in = inp("ident", [128, 128])
    W = {}
    for f in ("ffn1", "ffn2"):
        W[f + "_norm"] = inp(f + "_norm", [D])
        W[f + "_wg"] = inp(f + "_wg", [D, DFF])
        W[f + "_wu"] = inp(f + "_wu", [D, DFF])
        W[f + "_wd"] = inp(f + "_wd", [DFF, D])
    W["mix_norm"] = inp("mix_norm", [D])
    W["w_in"] = inp("w_in", [D, INC])
    W["final_norm"] = inp("final_norm", [D])
    W["w_out"] = inp("w_out", [D, D])
    for n, shp in (("shift_mu", [RWC]), ("decay_w0", [512]), ("decay_w2", [64, 512]), ("aaa_a0", [512]), ("aaa_a2", [64, 512]),
                   ("gate_g2", [128, 512]), ("k_k", [512]), ("k_a", [512]), ("r_k", [512]), ("ln_x_w", [512]), ("ln_x_b", [512])):
        W[n] = inp(n, shp)
    rwc_in = inp("rw_consts", [128, 10, 128])
    W["rel_bias_table"] = inp("rel_bias_table", [32, 8])
    for sfx in ("k", "v"):
        W["cmp_pe_" + sfx] = inp("cmp_pe_" + sfx, [64, 64])
        W["cmp_w1_" + sfx] = inp("cmp_w1_" + sfx, [64, 64, 64])
        W["cmp_w2_" + sfx] = inp("cmp_w2_" + sfx, [64, 64])
    cin = {"ident": ident_in}
    if NSA_SAMPLE:
        cin["nss_ohs"] = inp("nss_ohs", [33, 4, 4, 128])
        cin["nss_E2"] = inp("nss_E2", [128, 64, 128])
        pools_in = [inp(n, [40960, 4096]) for n in ("cache_cmp_k", "cache_cmp_v", "cache_slc_k", "cache_slc_v")]
        page_table_in = inp("page_table", [SBT, 64], I32)
        PAST = nc.dram_tensor("PAST", [2, 8192, 128], F32, kind="Internal").ap()
        OBR = nc.dram_tensor("OBR", [3, 64, 512], F32, kind="Internal").ap()
    for n, shp in (("nsa_ohg", [33, 768]), ("nsa_ohc", [33, 7, 128]), ("nsa_J", [128, 128]), ("nsa_E", [32, 16, 128]), ("nsa_ext", [128, 3, 62]), ("nsa_bm", [128, 4])):
        cin[n] = inp(n, shp)
    state_shift_in = inp("state_shift", [SBT, RWC])
    state_wkv_in = inp("state_wkv", [SBT, 8, 64, 64])

    y_prompt = outp("y_prompt", [PB * S, D])
    y_sample = outp("y_sample", [NS_TOK, D])
    p_kv = [outp(n, [PB * S, 128]) for n in ("p_cmp_k", "p_cmp_v", "p_slc_k", "p_slc_v")]
    p_win = [outp(n, [PB, WIN, 128]) for n in ("p_win_k", "p_win_v")]
    p_wkv = outp("p_wkv", [PB, 8, 64, 64])
    p_shift = outp("p_shift", [PB, RWC])
    s_kv = [outp(n, [NS_TOK, 128]) for n in ("s_cmp_k", "s_cmp_v", "s_slc_k", "s_slc_v")]
    s_win = [outp(n, [SBT, WIN, 128]) for n in ("s_win_k", "s_win_v")]
    s_wkv = outp("s_wkv", [SBT, 8, 64, 64])
    s_shift = outp("s_shift", [SBT, RWC])

    NTOK = PB * S + NS_TOK
    X1 = nc.dram_tensor("X1", [NTOK, D], F32, kind="Internal").ap()
    Z = nc.dram_tensor("Z", [NTOK, INC], F32, kind="Internal").ap()
    RS = nc.dram_tensor("RS", [7, 64, 512], F32, kind="Internal").ap()
    GD = nc.dram_tensor("GD", [8, 768], F32, kind="Internal").ap()
    if DEBUG:
        O = outp("dbg_O", [NTOK, D])
    else:
        O = nc.dram_tensor("O", [NTOK, D], F32, kind="Internal").ap()

    tiles = []
    for i in range(PB * S // 512):
        tiles.append((i * 512, 512, xp[i * 512:(i + 1) * 512, :]))
    tiles.append((PB * S, NS_TOK, xsm))

    with contextlib.ExitStack() as es:
        k = K(nc, es)
        ident_f = k.sb("ident_f", [128, 128], F32)
        ident = k.sb("ident", [128, 128], BF16)
        eps_t = k.sb("eps_t", [128, 1], F32)
        cbuf = k.buf("consts")
        k.dma("sp", ident_f[:], ident_in, writes=[cbuf])
        k.op("dve", lambda e: e.tensor_copy(out=ident[:], in_=ident_f[:]), reads=[cbuf], writes=[cbuf])
        k.op("dve", lambda e: e.memset(eps_t[:], EPS), writes=[cbuf])
        consts = {"eps": eps_t}
        gts = {}
        for n in ("ffn1_norm", "mix_norm", "ffn2_norm"):
            gts[n] = k.sb("gT_" + n, [128, 8], F32)
            with nc.allow_non_contiguous_dma(reason="tiny norm vector"):
                k.dma("sp", gts[n][:], W[n].rearrange("(c p) -> p c", p=128), writes=[cbuf])

        sq = k.sb("sq", [128, D], F32)
        ss = k.sb("ss", [128, 1], F32)
        rstd = k.sb("rstd", [128, 1], F32)
        xs = k.sb("xs", [128, D], BF16)
        xs_b, ss_b = k.buf("xs"), k.buf("ss")

        def ffn_phase(pfx, src_tiles, dst_rows, final=False):
            with contextlib.ExitStack() as es2:
                sb2 = lambda n, s, d: es2.enter_context(nc.sbuf_tensor(k.uniq(n), s, d))
                ps2 = lambda n, s, d: es2.enter_context(nc.psum_tensor(k.uniq(n), s, d))
                wg = sb2("wg", [128, 8, DFF], BF16)
                wu = sb2("wu", [128, 8, DFF], BF16)
                wd = sb2("wd", [128, NFC, D], BF16)
                wb = k.buf("ffn_w")
                for kc in range(8):
                    load_cast(k, wg[:, kc, :], W[pfx + "_wg"][kc * 128:(kc + 1) * 128, :], wb)
                    load_cast(k, wu[:, kc, :], W[pfx + "_wu"][kc * 128:(kc + 1) * 128, :], wb)
                for fc in range(NFC):
                    load_cast(k, wd[:, fc, :], W[pfx + "_wd"][fc * 128:(fc + 1) * 128, :], wb)
                hT = [sb2(f"hT{i}", [128, 8, 512], BF16) for i in range(2)]
                hT_b = [[k.buf() for _ in range(4)] for _ in range(2)]
                aT = sb2("aT", [128, NFC, 512], BF16)
                aT_b = [k.buf() for _ in range(NFC)]
                xa = [sb2(f"xa{i}", [128, D], F32) for i in range(2)]
                xa_b = [k.buf() for _ in range(2)]
                xr = [sb2(f"xr{i}", [128, D], F32) for i in range(2)]
                xr_b = [k.buf() for _ in range(2)]
                sg = [sb2(f"sg{i}", [128, 512], F32) for i in range(2)]
                sg_b = [k.buf() for _ in range(2)]
                pT = [ps2(f"pT{i}", [128, 8, 128], BF16) for i in range(2)]
                pT_b = [k.buf() for _ in range(2)]
                pg = [ps2(f"pg{i}", [128, 512], F32) for i in range(2)]
                pu = [ps2(f"pu{i}", [128, 512], F32) for i in range(2)]
                pgu_b = [k.buf() for _ in range(2)]
                po = [ps2(f"po{i}", [128, 512], F32) for i in range(2)]
                po_b = [k.buf() for _ in range(2)]
                cnt = {"xa": 0, "xr": 0, "gu": 0, "po": 0}
                if final:
                    gbc = sb2("gbc", [128, D], F32)
                    ss2 = sb2("ss2", [128, 2], F32)
                    fin_b = k.buf("fin")
                    k.dma("sp", gbc[:], W["final_norm"].partition_broadcast(128), writes=[wb])

                def norm_tile(ti):
                    row0, nt, src = src_tiles[ti]
                    hb = ti % 2
                    for st in range((nt + 127) // 128):
                        np_ = min(128, nt - st * 128)
                        i = cnt["xa"] % 2
                        cnt["xa"] += 1
                        k.dma("sp", xa[i][:np_, :], src[st * 128:st * 128 + np_, :], writes=[xa_b[i]])
                        rms_to_hT(k, (sq, ss, rstd, xs, xs_b, ss_b, pT[i], pT_b[i], ident), xa[i][:np_, :], xa_b[i], np_,
                                  gts[pfx + "_norm"], hT[hb], hT_b[hb][st], st * 128, consts)

                norm_tile(0)
                for ti in range(len(src_tiles)):
                    row0, nt, src = src_tiles[ti]
                    hb = ti % 2
                    nst = (nt + 127) // 128
                    hbufs = hT_b[hb][:nst]
                    for fc in range(NFC):
                        j = cnt["gu"] % 2
                        cnt["gu"] += 1
                        for kc in range(8):
                            k.op("pe", lambda e: e.matmul(pg[j][:, :nt], lhsT=wg[:, kc, fc * 128:(fc + 1) * 128], rhs=hT[hb][:, kc, :nt],
                                                          start=(kc == 0), stop=(kc == 7)), reads=hbufs + [wb], writes=[pgu_b[j]])
                        for kc in range(8):
                            k.op("pe", lambda e: e.matmul(pu[j][:, :nt], lhsT=wu[:, kc, fc * 128:(fc + 1) * 128], rhs=hT[hb][:, kc, :nt],
                                                          start=(kc == 0), stop=(kc == 7)), reads=hbufs + [wb], writes=[pgu_b[j]])
                        k.op("act", lambda e: e.activation(out=sg[j][:, :nt], in_=pg[j][:, :nt], func=AF.Silu),
                             reads=[pgu_b[j]], writes=[sg_b[j]])
                        k.op("dve", lambda e: e.tensor_tensor(out=aT[:, fc, :nt], in0=sg[j][:, :nt], in1=pu[j][:, :nt], op=ALU.mult),
                             reads=[sg_b[j], pgu_b[j]], writes=[aT_b[fc]])
                    if ti + 1 < len(src_tiles):
                        norm_tile(ti + 1)
                    for st in range(nst):
                        np_ = min(128, nt - st * 128)
                        i = cnt["xr"] % 2
                        cnt["xr"] += 1
                        k.dma("sp", xr[i][:np_, :], src[st * 128:st * 128 + np_, :], writes=[xr_b[i]])
                        for dh in range(2):
                            j = cnt["po"] % 2
                            cnt["po"] += 1
                            for fc in range(NFC):
                                k.op("pe", lambda e: e.matmul(po[j][:np_, :], lhsT=aT[:, fc, st * 128:st * 128 + np_],
                                                              rhs=wd[:, fc, dh * 512:(dh + 1) * 512], start=(fc == 0), stop=(fc == NFC - 1)),
                                     reads=[aT_b[fc], wb], writes=[po_b[j]])
                            k.op("dve", lambda e: e.scalar_tensor_tensor(out=xr[i][:np_, dh * 512:(dh + 1) * 512], in0=po[j][:np_, :], scalar=0.5,
                                                                         in1=xr[i][:np_, dh * 512:(dh + 1) * 512], op0=ALU.mult, op1=ALU.add),
                                 reads=[po_b[j], xr_b[i]], writes=[xr_b[i]])
                        if final:
                            k.op("act", lambda e: e.activation(out=sq[:np_, :], in_=xr[i][:np_, :], func=AF.Square, accum_out=ss2[:np_, 0:1]),
                                 reads=[xr_b[i]], writes=[fin_b])
                            k.op("act", lambda e: e.activation(out=ss2[:np_, 1:2], in_=ss2[:np_, 0:1], func=AF.Sqrt, scale=1.0 / D, bias=consts["eps"][:np_, :]),
                                 reads=[fin_b], writes=[fin_b])
                            k.op("dve", lambda e: e.reciprocal(out=ss2[:np_, 1:2], in_=ss2[:np_, 1:2]), reads=[fin_b], writes=[fin_b])
                            k.op("dve", lambda e: e.scalar_tensor_tensor(out=xr[i][:np_, :], in0=xr[i][:np_, :], scalar=ss2[:np_, 1:2], in1=gbc[:np_, :],
                                                                         op0=ALU.mult, op1=ALU.mult), reads=[xr_b[i], fin_b, wb], writes=[xr_b[i]])
                        k.dma("sp", dst_rows(row0 + st * 128, np_), xr[i][:np_, :], reads=[xr_b[i]])
                k.barrier()

        ffn_phase("ffn1", tiles, lambda r0, n: X1[r0:r0 + n, :])

        x1_tiles = [(r0, nt, X1[r0:r0 + nt, :]) for (r0, nt, _) in tiles]
        with contextlib.ExitStack() as es2:
            sb2 = lambda n, s, d: es2.enter_context(nc.sbuf_tensor(k.uniq(n), s, d))
            ps2 = lambda n, s, d: es2.enter_context(nc.psum_tensor(k.uniq(n), s, d))
            win = sb2("win", [128, 8, INC], BF16)
            winb = k.buf("win")
            for kc in range(8):
                load_cast(k, win[:, kc, :], W["w_in"][kc * 128:(kc + 1) * 128, :], winb)
            hT = [sb2(f"hT{i}", [128, 8, 128], BF16) for i in range(2)]
            hT_b = [k.buf() for _ in range(2)]
            xa = [sb2(f"xa{i}", [128, D], F32) for i in range(2)]
            xa_b = [k.buf() for _ in range(2)]
            zt = [sb2(f"zt{i}", [128, INC], F32) for i in range(2)]
            zt_b = [k.buf() for _ in range(2)]
            pT = [ps2(f"pT{i}", [128, 8, 128], BF16) for i in range(2)]
            pT_b = [k.buf() for _ in range(2)]
            pz = [ps2(f"pz{i}", [128, 512], F32) for i in range(4)]
            pz_b = [k.buf() for _ in range(4)]
            n_sub = 0
            n_pz = 0
            cgroups = [(c0, min(512, INC - c0)) for c0 in range(0, INC, 512)]
            for (row0, nt, src) in x1_tiles:
                for st in range((nt + 127) // 128):
                    np_ = min(128, nt - st * 128)
                    i = n_sub % 2
                    n_sub += 1
                    r0 = row0 + st * 128
                    k.dma("sp", xa[i][:np_, :], src[st * 128:st * 128 + np_, :], writes=[xa_b[i]])
                    rms_to_hT(k, (sq, ss, rstd, xs, xs_b, ss_b, pT[i], pT_b[i], ident), xa[i][:np_, :], xa_b[i], np_,
                              gts["mix_norm"], hT[i], hT_b[i], 0, consts)
                    for gi, (c0, cw) in enumerate(cgroups):
                        j = n_pz % 4
                        n_pz += 1
                        for kc in range(8):
                            k.op("pe", lambda e: e.matmul(pz[j][:np_, :cw], lhsT=hT[i][:, kc, :np_], rhs=win[:, kc, c0:c0 + cw],
                                                          start=(kc == 0), stop=(kc == 7)), reads=[hT_b[i], winb], writes=[pz_b[j]])
                        eng = "act" if gi % 2 == 0 else "dve"
                        if eng == "act":
                            k.op("act", lambda e: e.copy(out=zt[i][:np_, c0:c0 + cw], in_=pz[j][:np_, :cw]), reads=[pz_b[j]], writes=[zt_b[i]])
                        else:
                            k.op("dve", lambda e: e.tensor_copy(out=zt[i][:np_, c0:c0 + cw], in_=pz[j][:np_, :cw]), reads=[pz_b[j]], writes=[zt_b[i]])
                    k.dma("sp", Z[r0:r0 + np_, :], zt[i][:np_, :], reads=[zt_b[i]])
                    if r0 < PB * S:
                        for oi in range(4):
                            k.dma("sp", p_kv[oi][r0:r0 + np_, :], zt[i][:np_, 512 + 128 * oi:640 + 128 * oi], reads=[zt_b[i]])
                        seq, pos = divmod(r0, S)
                        if pos >= S - WIN:
                            for oi in range(2):
                                k.dma("sp", p_win[oi][seq, pos - (S - WIN):pos - (S - WIN) + np_, :],
                                      zt[i][:np_, 1024 + 128 * oi:1152 + 128 * oi], reads=[zt_b[i]])
                        if pos + np_ == S:
                            k.dma("sp", p_shift[seq:seq + 1, :], zt[i][np_ - 1:np_, NSA_COLS:INC], reads=[zt_b[i]])
                    else:
                        for oi in range(4):
                            k.dma("sp", s_kv[oi][:, :], zt[i][:np_, 512 + 128 * oi:640 + 128 * oi], reads=[zt_b[i]])
            k.barrier()

        rwkv_phase(k, nc, W, Z, O, p_wkv, rwc_in, state_shift_in, state_wkv_in, s_wkv, RS)
        nsa_prompt_phase(k, nc, W, Z, O, cin, GD)
        if NSA_SAMPLE:
            nsa_sample_phase(k, nc, W, Z, O, cin, pools_in, page_table_in, (cache_win_k, cache_win_v), PAST, OBR)

        with contextlib.ExitStack() as es2:
            sb2 = lambda n, s, d: es2.enter_context(nc.sbuf_tensor(k.uniq(n), s, d))
            ps2 = lambda n, s, d: es2.enter_context(nc.psum_tensor(k.uniq(n), s, d))
            wo = sb2("wo", [128, 8, D], BF16)
            wob = k.buf("wo")
            for kc in range(8):
                load_cast(k, wo[:, kc, :], W["w_out"][kc * 128:(kc + 1) * 128, :], wob)
            ot_ = [sb2(f"ot{i}", [128, D], F32) for i in range(2)]
            ot_b = [k.buf() for _ in range(2)]
            ob = [sb2(f"ob{i}", [128, D], BF16) for i in range(2)]
            ob_b = [k.buf() for _ in range(2)]
            oT = [sb2(f"oT{i}", [128, 8, 128], BF16) for i in range(2)]
            oT_b = [k.buf() for _ in range(2)]
            x1t = [sb2(f"x1t{i}", [128, D], F32) for i in range(2)]
            x1t_b = [k.buf() for _ in range(2)]
            pTo = [ps2(f"pTo{i}", [128, 8, 128], BF16) for i in range(2)]
            pTo_b = [k.buf() for _ in range(2)]
            pw_ = [ps2(f"pwo{i}", [128, 512], F32) for i in range(4)]
            pw_b = [k.buf() for _ in range(4)]
            n_sub = 0
            n_pw = 0
            for (row0, nt, _) in tiles:
                for st in range((nt + 127) // 128):
                    np_ = min(128, nt - st * 128)
                    i = n_sub % 2
                    n_sub += 1
                    r0 = row0 + st * 128
                    k.dma("sp", ot_[i][:np_, :], O[r0:r0 + np_, :], writes=[ot_b[i]])
                    k.dma("sp", x1t[i][:np_, :], X1[r0:r0 + np_, :], writes=[x1t_b[i]])
                    k.op("act", lambda e: e.copy(out=ob[i][:np_, :], in_=ot_[i][:np_, :]), reads=[ot_b[i]], writes=[ob_b[i]])
                    for kc in range(8):
                        k.op("pe", lambda e: e.transpose(out=pTo[i][:, kc, :np_], in_=ob[i][:np_, kc * 128:(kc + 1) * 128], identity=ident[:np_, :np_]),
                             reads=[ob_b[i]], writes=[pTo_b[i]])
                    k.op("dve", lambda e: e.tensor_copy(out=oT[i][:, :, :np_], in_=pTo[i][:, :, :np_]), reads=[pTo_b[i]], writes=[oT_b[i]])
                    for dh in range(2):
                        j = n_pw % 4
                        n_pw += 1
                        for kc in range(8):
                            k.op("pe", lambda e: e.matmul(pw_[j][:np_, :], lhsT=oT[i][:, kc, :np_], rhs=wo[:, kc, dh * 512:(dh + 1) * 512],
                                                          start=(kc == 0), stop=(kc == 7)), reads=[oT_b[i], wob], writes=[pw_b[j]])
                        k.op("dve", lambda e: e.tensor_tensor(out=x1t[i][:np_, dh * 512:(dh + 1) * 512], in0=pw_[j][:np_, :],
                                                              in1=x1t[i][:np_, dh * 512:(dh + 1) * 512], op=ALU.add),
                             reads=[pw_b[j], x1t_b[i]], writes=[x1t_b[i]])
                    k.dma("sp", X1[r0:r0 + np_, :], x1t[i][:np_, :], reads=[x1t_b[i]])
            k.barrier()

        def y_rows(r0, n):
            if r0 < PB * S:
                return y_prompt[r0:r0 + n, :]
            return y_sample[r0 - PB * S:r0 - PB * S + n, :]

        ffn_phase("ffn2", [(r0, nt, X1[r0:r0 + nt, :]) for (r0, nt, _) in tiles], y_rows, final=True)

        Zs = Z[PB * S:PB * S + NS_TOK, :].rearrange("(t b) c -> b t c", t=DS)
        for oi, cw in enumerate((cache_win_k, cache_win_v)):
            k.dma("sp", s_win[oi][:, 0:WIN - DS, :], cw[:, DS:WIN, :])
            k.dma("sp", s_win[oi][:, WIN - DS:WIN, :], Zs[:, :, 1024 + 128 * oi:1152 + 128 * oi])
        k.dma("sp", s_shift[:, :], Zs[:, DS - 1, NSA_COLS:INC])
        k.finish()
    return nc, list(dout.keys())


_CACHE = {}


def kernel(**inputs):
    if "nc" not in _CACHE:
        _CACHE["nc"] = build_program()
    nc, out_names = _CACHE["nc"]
    f = lambda a: np.ascontiguousarray(np.asarray(a, dtype=np.float32))
    shared = {"ident": np.eye(128, dtype=np.float32)}
    for n in ("ffn1_norm", "ffn1_wg", "ffn1_wu", "ffn1_wd", "ffn2_norm", "ffn2_wg", "ffn2_wu", "ffn2_wd", "mix_norm", "w_in", "w_out"):
        shared[n] = f(inputs[n][0])
    shared["final_norm"] = f(inputs["final_norm"])
    for n in ("shift_mu", "decay_w0", "decay_w2", "aaa_a0", "aaa_a2", "gate_g2", "k_k", "k_a", "ln_x_w", "ln_x_b"):
        shared[n] = f(inputs[n][0])
    shared["r_k"] = f(inputs["r_k"][0]).reshape(512)
    shared["rw_consts"] = rw_host_consts()
    shared["rel_bias_table"] = f(inputs["rel_bias_table"])
    for n in ("cmp_pe_k", "cmp_w1_k", "cmp_w2_k", "cmp_pe_v", "cmp_w1_v", "cmp_w2_v"):
        shared[n] = f(inputs[n][0])
    shared.update(nsa_host_consts())
    if NSA_SAMPLE:
        shared.update(nsa_sample_host_consts())
        for n in ("cache_cmp_k", "cache_cmp_v", "cache_slc_k", "cache_slc_v"):
            shared[n] = np.asarray(inputs[n], dtype=np.float32).reshape(40960, 4096)
    in_maps = []
    for c in range(NCORES):
        m = dict(shared)
        m["x_prompt"] = f(inputs["x_prompt"][PB * c:PB * (c + 1)]).reshape(PB * S, D)
        m["x_sample"] = f(np.asarray(inputs["x_sample"][SBT * c:SBT * (c + 1)]).transpose(1, 0, 2)).reshape(NS_TOK, D)
        if NSA_SAMPLE:
            m["page_table"] = np.ascontiguousarray(np.asarray(inputs["page_table"][SBT * c:SBT * (c + 1)], dtype=np.int32))
        m["state_shift"] = f(inputs["state_shift"][0, SBT * c:SBT * (c + 1)])
        m["state_wkv"] = f(inputs["state_wkv"][0, SBT * c:SBT * (c + 1)])
        m["cache_win_k"] = f(inputs["cache_win_k"][0, SBT * c:SBT * (c + 1)]).reshape(SBT, WIN, 128)
        m["cache_win_v"] = f(inputs["cache_win_v"][0, SBT * c:SBT * (c + 1)]).reshape(SBT, WIN, 128)
        in_maps.append(m)
    if _CACHE.get("dev1"):
        _CACHE["in0"] = in_maps[0]
        res = run_bass_kernel_spmd(nc, in_maps[:1], core_ids=[0])
        _CACHE["last"] = res.results
        return None
    res = run_bass_kernel_spmd(nc, in_maps, core_ids=list(range(NCORES)))
    R = res.results
    _CACHE["last"] = R
    cat = lambda n: np.concatenate([np.asarray(r[n]) for r in R], axis=0)
    cats = lambda n, w: np.concatenate([np.asarray(r[n]).reshape(DS, SBT, w).transpose(1, 0, 2) for r in R], axis=0)
    B, BS = PB * NCORES, SBT * NCORES
    outs = (
        cat("y_prompt").reshape(B, S, D),
        cats("y_sample", D).reshape(BS, DS, D),
        cat("p_cmp_k").reshape(1, B, S, 2, 64), cat("p_cmp_v").reshape(1, B, S, 2, 64),
        cat("p_slc_k").reshape(1, B, S, 2, 64), cat("p_slc_v").reshape(1, B, S, 2, 64),
        cat("p_win_k").reshape(1, B, WIN, 2, 64), cat("p_win_v").reshape(1, B, WIN, 2, 64),
        cat("p_wkv").reshape(1, B, 8, 64, 64), cat("p_shift").reshape(1, B, RWC),
        cats("s_cmp_k", 128).reshape(1, BS, DS, 2, 64), cats("s_cmp_v", 128).reshape(1, BS, DS, 2, 64),
        cats("s_slc_k", 128).reshape(1, BS, DS, 2, 64), cats("s_slc_v", 128).reshape(1, BS, DS, 2, 64),
        cat("s_win_k").reshape(1, BS, WIN, 2, 64), cat("s_win_v").reshape(1, BS, WIN, 2, 64),
        cat("s_wkv").reshape(1, BS, 8, 64, 64), cat("s_shift").reshape(1, BS, RWC),
    )
    return tuple(np.ascontiguousarray(o.astype(np.float32)) for o in outs)
```
